# Optimizing a Trainium2 kernel written in Bass

```python
import math
import jax, jax.numpy as jnp
from jax import lax
import numpy as np

D_MODEL = 2048
BATCH = 4
SEQ = 2048
DEPTH = 1

CHUNK = 64
Q_BLOCK = 128
D_MIX = D_MODEL
SSM_WIDTH = D_MIX // 2
SSM_GROUP = 16
SSM_GROUPS = SSM_WIDTH // SSM_GROUP
SSM_STATE = 64
N_HEADS = 8
QK_NOPE = 128
QK_ROPE = 64
QK_HEAD = QK_NOPE + QK_ROPE
V_HEAD = 128
ATTN_WIDTH = N_HEADS * V_HEAD
Q_LORA = 512
KV_LORA = 256
D_IN = SSM_WIDTH + Q_LORA + KV_LORA + QK_ROPE
D_FF = 5632
ROPE_THETA = 10000.0
EPS = 1e-6
STEP_MIN = 1e-3
STEP_MAX = 1e-1

kernel_name = "hymba_s5_mla_macaron_block"


def rms_norm(x, g):
    xf = x.astype(jnp.float32)
    y = xf * lax.rsqrt(jnp.mean(xf * xf, axis=-1, keepdims=True) + EPS)
    return (y * g.astype(jnp.float32)).astype(x.dtype)


def swiglu(h, w_gate, w_up, w_down):
    return (jax.nn.silu(h @ w_gate) * (h @ w_up)) @ w_down


def rope(x, cos, sin):
    half = x.shape[-1] // 2
    x1, x2 = x[..., :half], x[..., half:]
    return jnp.concatenate([x1 * cos - x2 * sin, x2 * cos + x1 * sin], axis=-1)


def s5_mixer(u, log_step, a_re, a_im, b_re, b_im, c_re, c_im, d_skip, w_glu, b_glu):
    f32 = jnp.float32
    bsz, seq, _ = u.shape
    uf = u.astype(f32).reshape(bsz, seq, SSM_GROUPS, SSM_GROUP)
    dt = jnp.exp(log_step.astype(f32))[:, None]
    ar, ai = a_re.astype(f32), a_im.astype(f32)
    mag = jnp.exp(ar * dt)
    ang = ai * dt
    lr, li = mag * jnp.cos(ang), mag * jnp.sin(ang)
    den = ar * ar + ai * ai
    fr = ((lr - 1.0) * ar + li * ai) / den
    fi = (li * ar - (lr - 1.0) * ai) / den
    bu_r = jnp.einsum('blgc,gnc->blgn', uf, b_re.astype(f32))
    bu_i = jnp.einsum('blgc,gnc->blgn', uf, b_im.astype(f32))
    br = fr * bu_r - fi * bu_i
    bi = fr * bu_i + fi * bu_r
    lr_t = jnp.broadcast_to(lr, (1, seq, SSM_GROUPS, SSM_STATE))
    li_t = jnp.broadcast_to(li, (1, seq, SSM_GROUPS, SSM_STATE))

    def combine(e1, e2):
        a1r, a1i, b1r, b1i = e1
        a2r, a2i, b2r, b2i = e2
        return (a2r * a1r - a2i * a1i,
                a2r * a1i + a2i * a1r,
                a2r * b1r - a2i * b1i + b2r,
                a2r * b1i + a2i * b1r + b2i)

    _, _, sr, si = lax.associative_scan(combine, (lr_t, li_t, br, bi), axis=1)
    y = (jnp.einsum('blgn,gcn->blgc', sr, c_re.astype(f32))
         - jnp.einsum('blgn,gcn->blgc', si, c_im.astype(f32))
         + d_skip.astype(f32) * uf)
    y = jax.nn.gelu(y.reshape(bsz, seq, SSM_WIDTH))
    y = y * jax.nn.sigmoid(y @ w_glu.astype(f32) + b_glu.astype(f32))
    return y.astype(u.dtype)


def mla_mixer(q_lat, kv_lat, k_pe, cos, sin, q_a_norm, w_q_up, kv_a_norm, w_kv_up, q_norm, k_norm):
    bsz, seq, _ = q_lat.shape
    q = (rms_norm(q_lat, q_a_norm) @ w_q_up).reshape(bsz, seq, N_HEADS, QK_HEAD)
    kv = (rms_norm(kv_lat, kv_a_norm) @ w_kv_up).reshape(bsz, seq, N_HEADS, QK_NOPE + V_HEAD)
    k_nope, v = kv[..., :QK_NOPE], kv[..., QK_NOPE:]
    k = jnp.concatenate(
        [k_nope, jnp.broadcast_to(k_pe[:, :, None, :], (bsz, seq, N_HEADS, QK_ROPE))], axis=-1)
    q = rms_norm(q, q_norm)
    k = rms_norm(k, k_norm)
    cos_c, sin_c = cos.astype(q.dtype), sin.astype(q.dtype)
    q = jnp.concatenate([q[..., :QK_NOPE], rope(q[..., QK_NOPE:], cos_c, sin_c)], axis=-1)
    k = jnp.concatenate([k[..., :QK_NOPE], rope(k[..., QK_NOPE:], cos_c, sin_c)], axis=-1)
    q = q.transpose(0, 2, 1, 3)
    k = k.transpose(0, 2, 1, 3)
    v = v.transpose(0, 2, 1, 3)
    scale = QK_HEAD ** -0.5
    outs = []
    for i in range(seq // Q_BLOCK):
        q0 = i * Q_BLOCK
        k_end = q0 + Q_BLOCK
        s = jnp.einsum('bhqd,bhkd->bhqk', q[:, :, q0:k_end], k[:, :, :k_end]).astype(jnp.float32) * scale
        q_chunk = (q0 + jnp.arange(Q_BLOCK)) // CHUNK
        k_chunk = jnp.arange(k_end) // CHUNK
        s = jnp.where(k_chunk[None, :] <= q_chunk[:, None], s, -jnp.inf)
        p = jax.nn.softmax(s, axis=-1).astype(v.dtype)
        outs.append(jnp.einsum('bhqk,bhkd->bhqd', p, v[:, :, :k_end]))
    o = jnp.concatenate(outs, axis=2)
    return o.transpose(0, 2, 1, 3).reshape(bsz, seq, ATTN_WIDTH)


def setup_inputs(seed: int = 0) -> dict:
    key = jax.random.key(seed)
    ks = jax.random.split(key, 40)
    f32 = jnp.float32

    def dense(k, fan_in, fan_out):
        return jax.random.normal(k, (DEPTH, fan_in, fan_out), f32) * fan_in ** -0.5

    def gain(k, n):
        return 1.0 + 0.01 * jax.random.normal(k, (DEPTH, n), f32)

    G, N, C = SSM_GROUPS, SSM_STATE, SSM_GROUP
    x = jax.random.normal(ks[0], (BATCH, SEQ, D_MODEL), f32)
    offsets = jax.random.randint(ks[1], (BATCH, 1), 0, 4096, dtype=jnp.int32)
    positions = (offsets + jnp.arange(SEQ, dtype=jnp.int32)[None, :]).astype(jnp.int32)
    return {
        "x": x,
        "positions": positions,
        "ffn1_norm": gain(ks[2], D_MODEL),
        "ffn1_w_gate": dense(ks[3], D_MODEL, D_FF),
        "ffn1_w_up": dense(ks[4], D_MODEL, D_FF),
        "ffn1_w_down": dense(ks[5], D_FF, D_MODEL),
        "mix_norm": gain(ks[6], D_MODEL),
        "w_in": dense(ks[7], D_MODEL, D_IN),
        "ssm_log_step": jax.random.uniform(ks[8], (DEPTH, G), f32,
                                           minval=math.log(STEP_MIN), maxval=math.log(STEP_MAX)),
        "ssm_a_re": -0.5 + 0.01 * jax.random.normal(ks[9], (DEPTH, G, N), f32),
        "ssm_a_im": math.pi * jnp.arange(N, dtype=f32)[None, None, :]
                    + 0.01 * jax.random.normal(ks[10], (DEPTH, G, N), f32),
        "ssm_b_re": jax.random.normal(ks[11], (DEPTH, G, N, C), f32) * (2 * C) ** -0.5,
        "ssm_b_im": jax.random.normal(ks[12], (DEPTH, G, N, C), f32) * (2 * C) ** -0.5,
        "ssm_c_re": jax.random.normal(ks[13], (DEPTH, G, C, N), f32) * (2 * N) ** -0.5,
        "ssm_c_im": jax.random.normal(ks[14], (DEPTH, G, C, N), f32) * (2 * N) ** -0.5,
        "ssm_d": jax.random.normal(ks[15], (DEPTH, G, C), f32),
        "ssm_w_glu": dense(ks[16], SSM_WIDTH, SSM_WIDTH),
        "ssm_b_glu": 0.01 * jax.random.normal(ks[17], (DEPTH, SSM_WIDTH), f32),
        "mla_q_a_norm": gain(ks[18], Q_LORA),
        "mla_w_q_up": dense(ks[19], Q_LORA, N_HEADS * QK_HEAD),
        "mla_kv_a_norm": gain(ks[20], KV_LORA),
        "mla_w_kv_up": dense(ks[21], KV_LORA, N_HEADS * (QK_NOPE + V_HEAD)),
        "mla_q_norm": gain(ks[22], QK_HEAD),
        "mla_k_norm": gain(ks[23], QK_HEAD),
        "ssm_out_norm": gain(ks[24], SSM_WIDTH),
        "attn_out_norm": gain(ks[25], ATTN_WIDTH),
        "w_out": dense(ks[26], D_MIX, D_MODEL),
        "ffn2_norm": gain(ks[27], D_MODEL),
        "ffn2_w_gate": dense(ks[28], D_MODEL, D_FF),
        "ffn2_w_up": dense(ks[29], D_MODEL, D_FF),
        "ffn2_w_down": dense(ks[30], D_FF, D_MODEL),
        "final_norm": gain(ks[31], D_MODEL),
    }


def reference(x, positions, ffn1_norm, ffn1_w_gate, ffn1_w_up, ffn1_w_down, mix_norm, w_in,
              ssm_log_step, ssm_a_re, ssm_a_im, ssm_b_re, ssm_b_im, ssm_c_re, ssm_c_im, ssm_d,
              ssm_w_glu, ssm_b_glu, mla_q_a_norm, mla_w_q_up, mla_kv_a_norm, mla_w_kv_up,
              mla_q_norm, mla_k_norm, ssm_out_norm, attn_out_norm, w_out,
              ffn2_norm, ffn2_w_gate, ffn2_w_up, ffn2_w_down, final_norm):
    inv_freq = ROPE_THETA ** (-jnp.arange(0, QK_ROPE, 2, dtype=jnp.float32) / QK_ROPE)
    ang = positions.astype(jnp.float32)[..., None] * inv_freq
    cos = jnp.cos(ang)[:, :, None, :]
    sin = jnp.sin(ang)[:, :, None, :]
    o1 = SSM_WIDTH
    o2 = o1 + Q_LORA
    o3 = o2 + KV_LORA
    for l in range(DEPTH):
        x = x + 0.5 * swiglu(rms_norm(x, ffn1_norm[l]), ffn1_w_gate[l], ffn1_w_up[l], ffn1_w_down[l])
        z = rms_norm(x, mix_norm[l]) @ w_in[l]
        y_ssm = s5_mixer(z[..., :o1], ssm_log_step[l], ssm_a_re[l], ssm_a_im[l], ssm_b_re[l],
                         ssm_b_im[l], ssm_c_re[l], ssm_c_im[l], ssm_d[l], ssm_w_glu[l], ssm_b_glu[l])
        y_att = mla_mixer(z[..., o1:o2], z[..., o2:o3], z[..., o3:], cos, sin,
                          mla_q_a_norm[l], mla_w_q_up[l], mla_kv_a_norm[l], mla_w_kv_up[l],
                          mla_q_norm[l], mla_k_norm[l])
        y = jnp.concatenate([rms_norm(y_ssm, ssm_out_norm[l]), rms_norm(y_att, attn_out_norm[l])], axis=-1)
        x = x + y @ w_out[l]
        x = x + 0.5 * swiglu(rms_norm(x, ffn2_norm[l]), ffn2_w_gate[l], ffn2_w_up[l], ffn2_w_down[l])
        x = rms_norm(x, final_norm[l])
    return x
```

```python
import math
import numpy as np
import concourse.bass as bass
import concourse.mybir as mybir
from concourse.bass_utils import run_bass_kernel_spmd

F32 = mybir.dt.float32
BF16 = mybir.dt.bfloat16
I32 = mybir.dt.int32
AF = mybir.ActivationFunctionType
ALU = mybir.AluOpType

D = 2048
T = 1024
NDC = 16
DFF = 5632
NG = DFF // 256
EPS = 1e-6
TWO_PI = 2.0 * math.pi
PI_SAFE = 3.1415925
SCALE = 192 ** -0.5
_DBG_STAGE = None
_DBG_SUB = None


class Buf:
    __slots__ = ("w", "r", "dsem", "dcnt", "name", "excl")

    def __init__(self, name="", excl=False):
        self.excl = excl
        self.w = None
        self.r = {}
        self.dsem = None
        self.dcnt = 0
        self.name = name


class _Rec:
    def __getattr__(self, name):
        def f(*a, **k):
            self.__dict__["call"] = (name, a, k)
            return self
        return f


class Sched:
    ENG = ("sp", "act", "dve", "pool", "pe")

    def __init__(self, nc, sems, dma_sems):
        self.nc = nc
        self.sem = sems
        half = len(dma_sems) // 2
        self.free_dma_sems = {"pool": list(dma_sems[:half]), "sp": list(dma_sems[half:])}
        self.ops = {e: [] for e in self.ENG}
        self.cnt = {e: 0 for e in self.ENG}
        self.seen = {e: {} for e in self.ENG}
        self.all_dma = {}
        self.sem_bufs = []

    def _deps(self, e, reads, writes):
        deps = {}

        def add(k, v):
            if v > deps.get(k, 0):
                deps[k] = v
        for b in reads:
            if b.w is not None:
                add(*b.w)
            if b.excl:
                for k, v in b.r.items():
                    add(k, v)
        for b in writes:
            if b.w is not None:
                add(*b.w)
            for k, v in b.r.items():
                add(k, v)
        waits = []
        for k, v in deps.items():
            if k is self.sem["pe"] and e == "pe":
                continue
            if v > self.seen[e].get(k, 0):
                self.seen[e][k] = v
                waits.append((k, v))
        return waits

    def op(self, e, emit, reads=(), writes=(), signal=True):
        rec = _Rec()
        emit(rec)
        name, a, k = rec.call
        emit = lambda eng, name=name, a=a, k=k: getattr(eng, name)(*a, **k)
        waits = self._deps(e, reads, writes)
        if signal:
            self.cnt[e] += 1
            val = self.cnt[e]
            sig = (self.sem[e], 1)
        else:
            val = self.cnt[e] + 1
            sig = None
        self.ops[e].append((waits, emit, sig))
        key = self.sem[e]
        for b in reads:
            if b.r.get(key, 0) < val:
                b.r[key] = val
        for b in writes:
            b.w = (key, val)
            b.r = {}

    def dma(self, q, out, in_, reads=(), writes=(), sembuf=None):
        sb = sembuf if sembuf is not None else (writes[0] if writes else reads[0])
        if sb.dsem is None:
            sb.dsem = self.free_dma_sems[q].pop()
            self.sem_bufs.append((sb, q))
        waits = self._deps(q, reads, writes)
        sem = sb.dsem
        val = self.all_dma.get(sem, 0) + 16
        self.ops[q].append((waits, lambda eng: eng.dma_start(out=out, in_=in_), (sem, 16)))
        self.all_dma[sem] = val
        for b in reads:
            if b.r.get(sem, 0) < val:
                b.r[sem] = val
        for b in writes:
            b.w = (sem, val)
            b.r = {}

    def barrier(self):
        targets = [(self.sem[f], self.cnt[f]) for f in self.ENG if self.cnt[f] > 0]
        targets += list(self.all_dma.items())
        for e in self.ENG:
            waits = []
            for k, v in targets:
                if k is self.sem[e]:
                    continue
                if v > self.seen[e].get(k, 0):
                    self.seen[e][k] = v
                    waits.append((k, v))
            if waits:
                self.ops[e].append((waits, None, None))
        for b, q in self.sem_bufs:
            self.free_dma_sems[q].append(b.dsem)
            b.dsem = None
        self.sem_bufs = []

    def emit(self, block):
        decos = {"sp": block.sync, "act": block.scalar, "dve": block.vector,
                 "pool": block.gpsimd, "pe": block.tensor}
        for name in self.ENG:
            ops = self.ops[name]

            def body(eng, ops=ops):
                for waits, emit, sig in ops:
                    for (s, v) in waits:
                        eng.wait_ge(s, v)
                    if emit is None:
                        continue
                    ins = emit(eng)
                    if sig is not None:
                        ins.then_inc(sig[0], sig[1])
            decos[name](body)


def build_program():
    nc = bass.Bass("TRN2", target_bir_lowering=False)

    in_names = []

    def din(name, shape, dt=F32):
        in_names.append(name)
        return nc.dram_tensor(name, list(shape), dt, kind="ExternalInput").ap()

    x_pre = din("x_pre", [T, D])
    x_own = din("x_own", [T, D])
    pos_pre = din("pos_pre", [64, T], I32)
    pos_own = din("pos_own", [64, T], I32)
    maskb_d = din("maskbias", [128, 1])
    ident_d = din("ident", [128, 128])
    tidx_d = din("tidx", [128, T])
    cols_d = din("cols", [128, 128])
    ssmp_d = din("ssmp", [128, 3, 32])
    bblk_d = din("bblk", [128, 32, 2, 128])
    cblk_d = din("cblk", [128, 32, 2, 128])
    gfin_d = din("gfin", [128, D])
    st = _DBG_STAGE
    SUB = _DBG_SUB
    USE_FFN1 = st not in (0, 5)
    USE_FFN2 = st is None or st == 3
    if USE_FFN1:
        w1g = din("w1g", [D, DFF]); w1u = din("w1u", [D, DFF]); w1d = din("w1d", [DFF, D])
    if USE_FFN2:
        w2g = din("w2g", [D, DFF]); w2u = din("w2u", [D, DFF]); w2d = din("w2d", [DFF, D])
    win_d = din("win", [D, 1920])
    wglu_d = din("wglu", [1024, 1024])
    wqn_d = din("wqn", [512, 1024]); wqr_d = din("wqr", [512, 512]); wqs_d = din("wqs", [512, 512])
    wkk_d = din("wkk", [256, 1024]); wkv_d = din("wkv", [256, 1024])
    wout_d = din("wout", [D, D])
    out_d = nc.dram_tensor("out", [T, D], F32, kind="ExternalOutput").ap()

    C_G1, C_GM, C_G2, C_GF = 0, 16, 32, 48
    C_GQA, C_GKA = 64, 68
    C_GSO, C_GAO, C_BGLU, C_DSK = 70, 78, 86, 94
    C_GQN, C_GQR, C_GQS, C_GKN, C_GKR, C_GKS = 102, 103, 104, 105, 106, 107
    C_INVF, C_SGN = 108, 109

    import contextlib
    es = contextlib.ExitStack()
    uid = [0]

    def sbt(name, shape, dt=F32):
        uid[0] += 1
        return nc.sbuf_tensor("sb%d_%s" % (uid[0], name), list(shape), dt)
    with es:
        def sb(name, shape, dt=F32):
            return es.enter_context(sbt(name, list(shape), dt))

        sems = {e: es.enter_context(nc.semaphore("s_" + e)) for e in Sched.ENG}
        dma_sems = [es.enter_context(nc.semaphore("d%d" % i)) for i in range(24)]
        S = Sched(nc, sems, dma_sems)

        xT = sb("xT", [128, NDC, T]); B_xT = Buf("xT")
        ident = sb("ident", [128, 128])
        onesb = sb("onesb", [128, 128], BF16)
        cols = sb("cols", [128, 128])
        maskb = sb("maskb", [128, 1])
        tidx = sb("tidx", [128, T])
        ssmp = sb("ssmp", [128, 3, 32])
        sth = sb("sth", [128, 32]); srd = sb("srd", [128, 32])
        sfr = sb("sfr", [128, 32]); sfi = sb("sfi", [128, 32])
        scT = sb("scT", [128, 32]); ssT = sb("ssT", [128, 32])
        stmp = sb("stmp", [128, 8, 32])
        state = sb("state", [128, 32, 2])
        kvnT = sb("kvnT", [128, 2, 2 * T], BF16)
        krT = sb("krT", [64, 2 * T], BF16)
        sqkpe = sb("sqkpe", [64, 2 * T], BF16)
        negpi = sb("negpi", [128, 1]); magicc = sb("magicc", [128, 1]); nmagicc = sb("nmagicc", [128, 1]); halfpi = sb("halfpi", [128, 1]); sthn = sb("sthn", [128, 32]); epsc = sb("epsc", [128, 1]); zeroc = sb("zeroc", [128, 1])
        B_const = Buf("const"); B_state = Buf("state")
        B_kvn = Buf("kvn"); B_kr = Buf("kr"); B_sqk = Buf("sqk")
        ring = {"slots": [], "bufs": [], "i": 0}

        def open_ring(sc, n):
            ring["slots"] = [sc.enter_context(sbt("wslot%d_%d" % (i, ctr["ring"]), [128, 4096], BF16)) for i in range(n)]
            ring["bufs"] = [Buf("ws%d" % i) for i in range(n)]
            ring["i"] = 0
            ctr["ring"] += 1
        ps = [es.enter_context(nc.psum_tensor("ps%d" % i, [128, 512], F32)) for i in range(8)]
        B_ps = [Buf("ps%d" % i, excl=True) for i in range(8)]

        def load_w(src_ap, view):
            i = ring["i"] % len(ring["slots"])
            ring["i"] += 1
            wslot, B_ws = ring["slots"], ring["bufs"]
            if view[0] == "c":
                dst = wslot[i][:, 0:view[1] * view[2]].rearrange("p (c n) -> p c n", c=view[1])
            else:
                dst = wslot[i][:, 0:view[1]]
            S.dma("pool", dst, src_ap, writes=[B_ws[i]])
            return dst, B_ws[i]

        def wpiece_cols(w, k0, ncol, nchunk):
            return w[:, k0:k0 + ncol].rearrange("(c p) n -> p c n", p=128), ("c", nchunk, ncol)

        def wpiece_rows(w, r0, nrow_chunks, ncol):
            return (w[r0:r0 + 128 * nrow_chunks, :].rearrange("(c p) n -> p c n", p=128),
                    ("c", nrow_chunks, ncol))

        def mm(out, lhsT, rhs, start, stop, reads, bank, signal=None):
            sig = True if signal is None else signal
            S.op("pe", lambda e: e.matmul(out, lhsT=lhsT, rhs=rhs, start=start, stop=stop),
                 reads=reads, writes=[bank], signal=sig)

        def col(c, n=128):
            return cols[0:n, c:c + 1]

        ctr = {"ring": 0}
        for dst, src in ((ident[:], ident_d), (cols[:], cols_d), (maskb[:], maskb_d),
                         (tidx[:], tidx_d), (ssmp[:], ssmp_d)):
            S.dma("sp", dst, src, writes=[B_const], sembuf=B_const)
        S.op("dve", lambda e: e.memset(onesb[:], 1.0), writes=[B_const])
        S.op("dve", lambda e: e.memset(negpi[:], -math.pi), writes=[B_const])
        S.op("dve", lambda e: e.memset(magicc[:], 12582912.0), writes=[B_const])
        S.op("dve", lambda e: e.memset(nmagicc[:], -12582912.0), writes=[B_const])
        S.op("dve", lambda e: e.memset(halfpi[:], 0.5 * math.pi), writes=[B_const])
        S.op("dve", lambda e: e.memset(epsc[:], EPS), writes=[B_const])
        S.op("dve", lambda e: e.memset(zeroc[:], 0.0), writes=[B_const])
        S.op("dve", lambda e: e.memset(state[:], 0.0), writes=[B_state])

        sqt = [sb("sqt%d" % i, [128, 512], BF16) for i in range(2)]
        B_sqt = [Buf("sqt0"), Buf("sqt1")]
        rstd = [sb("rstd%d" % i, [128, 512]) for i in range(2)]
        B_rstd = [Buf("rstd0"), Buf("rstd1")]
        ctr["sq"] = 0; ctr["rs"] = 0; ctr["ps"] = 0

        def next_ps():
            i = ctr["ps"] % 6
            ctr["ps"] += 1
            return ps[i], B_ps[i]

        def long_ps(i):
            return ps[6 + i], B_ps[6 + i]

        def rmsnorm(srcs, nfeat, gains, outs, ntok=T):
            for blk in range(ntok // 512):
                pst, B_pst = next_ps()
                n = len(srcs)
                for i, (fn, P, bsrc) in enumerate(srcs):
                    j = ctr["sq"] % 2
                    ctr["sq"] += 1
                    src = fn(blk)
                    S.op("act", lambda e, j=j, src=src, P=P: e.activation(out=sqt[j][0:P, :], in_=src, func=AF.Square),
                         reads=[bsrc], writes=[B_sqt[j]])
                    mm(pst[:, :], onesb[0:P, :], sqt[j][0:P, :], i == 0, i == n - 1, [B_sqt[j], B_const], B_pst)
                k = ctr["rs"] % 2
                ctr["rs"] += 1
                S.op("act", lambda e, k=k, pst=pst: e.activation(out=rstd[k][:], in_=pst[:, :], func=AF.Ln,
                                                                scale=1.0 / nfeat, bias=epsc[:, 0:1]),
                     reads=[B_pst, B_const], writes=[B_rstd[k]])
                S.op("act", lambda e, k=k: e.activation(out=rstd[k][:], in_=rstd[k][:], func=AF.Exp, scale=-0.5),
                     reads=[B_rstd[k]], writes=[B_rstd[k]])
                for (fn, P, bsrc), g, (ofn, bout) in zip(srcs, gains, outs):
                    src = fn(blk)
                    o = ofn(blk)
                    S.op("dve", lambda e, src=src, o=o, g=g, k=k, P=P: e.scalar_tensor_tensor(
                        out=o, in0=src, scalar=g, in1=rstd[k][0:P, :], op0=ALU.mult, op1=ALU.mult),
                        reads=[bsrc, B_rstd[k], B_const], writes=[bout])

        def blk(ap3, c):
            return lambda b: ap3[:, c, b * 512:(b + 1) * 512]

        def load_xT(x_dram):
            with contextlib.ExitStack() as sc:
                xin = [sc.enter_context(sbt("xin%d" % i, [128, D], F32)) for i in range(2)]
                B_xin = [Buf("xin0"), Buf("xin1")]
                for tc in range(T // 128):
                    j = tc % 2
                    S.dma("sp", xin[j][:], x_dram[tc * 128:(tc + 1) * 128, :], writes=[B_xin[j]])
                    for q in range(4):
                        pst, B_pst = next_ps()
                        for r in range(4):
                            dc = q * 4 + r
                            S.op("pe", lambda e, pst=pst, r=r, dc=dc, j=j: e.transpose(
                                pst[:, r * 128:(r + 1) * 128], xin[j][:, dc * 128:(dc + 1) * 128], ident[:]),
                                reads=[B_xin[j], B_const], writes=[B_pst], signal=(r == 3))
                        S.op("act", lambda e, pst=pst, q=q, tc=tc: e.activation(
                            out=xT[:, q * 4:(q + 1) * 4, tc * 128:(tc + 1) * 128],
                            in_=pst[:, :].rearrange("p (r n) -> p r n", r=4), func=AF.Copy),
                            reads=[B_pst], writes=[B_xT])
                S.barrier()

        def ffn(gcol, wg, wu, wd):
            with contextlib.ExitStack() as sc:
                xnT = sc.enter_context(sbt("xnT", [128, NDC, T], BF16)); B_xn = Buf("xn")
                H = [sc.enter_context(sbt("H%d" % i, [128, 2, T], BF16)) for i in range(2)]
                B_H = [Buf("H0"), Buf("H1")]
                sg = [sc.enter_context(sbt("sg%d" % i, [128, 512], F32)) for i in range(2)]
                B_sg = [Buf("sg0"), Buf("sg1")]
                open_ring(sc, 6)
                rmsnorm([(blk(xT, c), 128, B_xT) for c in range(NDC)], D,
                        [col(gcol + c) for c in range(NDC)],
                        [(blk(xnT, c), B_xn) for c in range(NDC)])
                pend = []

                def issue_gu(k):
                    a, v = wpiece_cols(wg, k * 256, 256, 16)
                    g_ = load_w(a, v)
                    a, v = wpiece_cols(wu, k * 256, 256, 16)
                    u_ = load_w(a, v)
                    pend.append([g_, u_, None])

                def issue_d(k):
                    a, v = wpiece_rows(wd, k * 256, 2, D)
                    pend[k][2] = load_w(a, v)
                issue_gu(0)
                issue_d(0)
                sgi = 0

                def gate_up(k):
                    nonlocal sgi
                    (wgs, B_wg), (wus, B_wu), _ = pend[k]
                    h = H[k % 2]
                    B_h = B_H[k % 2]
                    for fc in range(2):
                        for hb in range(2):
                            pg, B_pg = next_ps()
                            pu, B_pu = next_ps()
                            for dc in range(NDC):
                                mm(pg[:, :], wgs[:, dc, fc * 128:(fc + 1) * 128], xnT[:, dc, hb * 512:(hb + 1) * 512],
                                   dc == 0, dc == NDC - 1, [B_wg, B_xn], B_pg, signal=(dc == NDC - 1))
                            for dc in range(NDC):
                                mm(pu[:, :], wus[:, dc, fc * 128:(fc + 1) * 128], xnT[:, dc, hb * 512:(hb + 1) * 512],
                                   dc == 0, dc == NDC - 1, [B_wu, B_xn], B_pu, signal=(dc == NDC - 1))
                            j = sgi % 2
                            sgi += 1
                            S.op("act", lambda e: e.activation(out=sg[j][:], in_=pg[:, :], func=AF.Silu),
                                 reads=[B_pg], writes=[B_sg[j]])
                            S.op("dve", lambda e: e.tensor_tensor(
                                out=h[:, fc, hb * 512:(hb + 1) * 512], in0=sg[j][:], in1=pu[:, :], op=ALU.mult),
                                reads=[B_sg[j], B_pu], writes=[B_h])

                def down(k):
                    _, _, (wds, B_wd) = pend[k]
                    h = H[k % 2]
                    B_h = B_H[k % 2]
                    for dc in range(NDC):
                        for hb in range(2):
                            py, B_py = next_ps()
                            for fc in range(2):
                                mm(py[:, :], wds[:, fc, dc * 128:(dc + 1) * 128], h[:, fc, hb * 512:(hb + 1) * 512],
                                   fc == 0, fc == 1, [B_wd, B_h], B_py, signal=(fc == 1))
                            S.op("dve", lambda e: e.scalar_tensor_tensor(
                                out=xT[:, dc, hb * 512:(hb + 1) * 512], in0=py[:, :], scalar=0.5,
                                in1=xT[:, dc, hb * 512:(hb + 1) * 512], op0=ALU.mult, op1=ALU.add),
                                reads=[B_py, B_xT], writes=[B_xT])

                for k in range(NG):
                    if k + 1 < NG:
                        issue_gu(k + 1)
                    gate_up(k)
                    if k > 0:
                        down(k - 1)
                    if k + 1 < NG:
                        issue_d(k + 1)
                down(NG - 1)
                S.barrier()

        MAGIC = 12582912.0

        def sincos(dst_sin, dst_cos, ang, tmp, R):
            S.op("dve", lambda e: e.tensor_scalar(out=tmp, in0=ang, scalar1=1.0 / TWO_PI, scalar2=MAGIC, op0=ALU.mult, op1=ALU.add), reads=R, writes=R)
            S.op("dve", lambda e: e.tensor_scalar(out=tmp, in0=tmp, scalar1=MAGIC, scalar2=-TWO_PI, op0=ALU.subtract, op1=ALU.mult), reads=R, writes=R)
            S.op("dve", lambda e: e.tensor_tensor(out=tmp, in0=tmp, in1=ang, op=ALU.add), reads=R, writes=R)
            S.op("dve", lambda e: e.tensor_scalar(out=tmp, in0=tmp, scalar1=PI_SAFE, scalar2=-PI_SAFE, op0=ALU.min, op1=ALU.max), reads=R, writes=R)
            S.op("act", lambda e: e.activation(out=dst_sin, in_=tmp, func=AF.Sin), reads=R, writes=R)
            S.op("dve", lambda e: e.scalar_tensor_tensor(out=tmp, in0=tmp, scalar=-1.0, in1=tmp, op0=ALU.mult, op1=ALU.max), reads=R, writes=R)
            S.op("act", lambda e: e.activation(out=dst_cos, in_=tmp, func=AF.Sin, scale=-1.0, bias=halfpi[:, 0:1]), reads=R, writes=R)

        def ssm_prep():
            ls, ar, ai = ssmp[:, 0, :], ssmp[:, 1, :], ssmp[:, 2, :]
            tm = lambda i: stmp[:, i, :]
            R = [B_const]

            def dv(fn):
                S.op("dve", fn, reads=R, writes=R)

            def ac(fn):
                S.op("act", fn, reads=R, writes=R)
            ac(lambda e: e.activation(out=tm(0), in_=ls, func=AF.Exp))
            dv(lambda e: e.tensor_tensor(out=tm(1), in0=ar, in1=tm(0), op=ALU.mult))
            dv(lambda e: e.tensor_tensor(out=sth[:], in0=ai, in1=tm(0), op=ALU.mult))
            ac(lambda e: e.activation(out=srd[:], in_=tm(1), func=AF.Exp))
            dv(lambda e: e.tensor_scalar(out=sthn[:], in0=sth[:], scalar1=1.0 / TWO_PI, scalar2=None, op0=ALU.mult))
            sincos(tm(2), tm(3), sth[:], tm(7), R)
            dv(lambda e: e.tensor_scalar(out=tm(6), in0=sth[:], scalar1=float(T), scalar2=None, op0=ALU.mult))
            sincos(ssT[:], scT[:], tm(6), tm(7), R)
            dv(lambda e: e.tensor_tensor(out=tm(4), in0=srd[:], in1=tm(3), op=ALU.mult))
            dv(lambda e: e.tensor_tensor(out=tm(5), in0=srd[:], in1=tm(2), op=ALU.mult))
            dv(lambda e: e.tensor_scalar(out=tm(4), in0=tm(4), scalar1=-1.0, scalar2=None, op0=ALU.add))
            dv(lambda e: e.tensor_tensor(out=tm(6), in0=ar, in1=ar, op=ALU.mult))
            dv(lambda e: e.tensor_tensor(out=tm(7), in0=ai, in1=ai, op=ALU.mult))
            dv(lambda e: e.tensor_tensor(out=tm(6), in0=tm(6), in1=tm(7), op=ALU.add))
            dv(lambda e: e.reciprocal(out=tm(6), in_=tm(6)))
            dv(lambda e: e.tensor_tensor(out=tm(0), in0=tm(4), in1=ar, op=ALU.mult))
            dv(lambda e: e.tensor_tensor(out=tm(1), in0=tm(5), in1=ai, op=ALU.mult))
            dv(lambda e: e.tensor_tensor(out=tm(0), in0=tm(0), in1=tm(1), op=ALU.add))
            dv(lambda e: e.tensor_tensor(out=sfr[:], in0=tm(0), in1=tm(6), op=ALU.mult))
            dv(lambda e: e.tensor_tensor(out=tm(0), in0=tm(5), in1=ar, op=ALU.mult))
            dv(lambda e: e.tensor_tensor(out=tm(1), in0=tm(4), in1=ai, op=ALU.mult))
            dv(lambda e: e.tensor_tensor(out=tm(0), in0=tm(0), in1=tm(1), op=ALU.subtract))
            dv(lambda e: e.tensor_tensor(out=sfi[:], in0=tm(0), in1=tm(6), op=ALU.mult))

        def rotate_state():
            R = [B_const, B_state]
            re, im = state[:, :, 0], state[:, :, 1]
            tm = lambda i: stmp[:, i, :]

            def dv(fn):
                S.op("dve", fn, reads=R, writes=R)
            dv(lambda e: e.tensor_tensor(out=tm(0), in0=scT[:], in1=re, op=ALU.mult))
            dv(lambda e: e.tensor_tensor(out=tm(1), in0=ssT[:], in1=im, op=ALU.mult))
            dv(lambda e: e.tensor_tensor(out=tm(2), in0=ssT[:], in1=re, op=ALU.mult))
            dv(lambda e: e.tensor_tensor(out=tm(3), in0=scT[:], in1=im, op=ALU.mult))
            dv(lambda e: e.tensor_tensor(out=re, in0=tm(0), in1=tm(1), op=ALU.subtract))
            dv(lambda e: e.tensor_tensor(out=im, in0=tm(2), in1=tm(3), op=ALU.add))

        def build_Bl(sc, Bl, B_Bl):
            bb = [sc.enter_context(sbt("bb%d_%d" % (i, ctr["ring"]), [128, 2, 128], F32)) for i in range(2)]
            B_bb = [Buf("bb0"), Buf("bb1")]
            bo = [sc.enter_context(sbt("bo%d_%d" % (i, ctr["ring"]), [128, 2, 128], F32)) for i in range(2)]
            B_bo = [Buf("bo0"), Buf("bo1")]
            for ti in range(32):
                j = ti % 2
                S.dma("sp", bb[j][:], bblk_d[:, ti, :, :], writes=[B_bb[j]])
                fr, fi = sfr[:, ti:ti + 1], sfi[:, ti:ti + 1]
                S.op("dve", lambda e, j=j, fi=fi: e.tensor_scalar(out=bo[j][:, 0, :], in0=bb[j][:, 1, :], scalar1=fi, scalar2=-1.0, op0=ALU.mult, op1=ALU.mult),
                     reads=[B_bb[j], B_const], writes=[B_bo[j]])
                S.op("dve", lambda e, j=j, fr=fr: e.scalar_tensor_tensor(out=bo[j][:, 0, :], in0=bb[j][:, 0, :], scalar=fr, in1=bo[j][:, 0, :], op0=ALU.mult, op1=ALU.add),
                     reads=[B_bb[j], B_const, B_bo[j]], writes=[B_bo[j]])
                S.op("dve", lambda e, j=j, fi=fi: e.tensor_scalar(out=bo[j][:, 1, :], in0=bb[j][:, 0, :], scalar1=fi, scalar2=None, op0=ALU.mult),
                     reads=[B_bb[j], B_const, B_bo[j]], writes=[B_bo[j]])
                S.op("dve", lambda e, j=j, fr=fr: e.scalar_tensor_tensor(out=bo[j][:, 1, :], in0=bb[j][:, 1, :], scalar=fr, in1=bo[j][:, 1, :], op0=ALU.mult, op1=ALU.add),
                     reads=[B_bb[j], B_const, B_bo[j]], writes=[B_bo[j]])
                pst, B_pst = next_ps()
                for ri in range(2):
                    S.op("pe", lambda e, pst=pst, ri=ri, j=j: e.transpose(pst[:, ri * 128:(ri + 1) * 128], bo[j][:, ri, :], ident[:]),
                         reads=[B_bo[j], B_const], writes=[B_pst], signal=(ri == 1))
                S.op("act", lambda e, pst=pst, ti=ti: e.activation(out=Bl[:, ti, :, :], in_=pst[:, 0:256].rearrange("p (r n) -> p r n", r=2), func=AF.Copy),
                     reads=[B_pst], writes=[B_Bl])

        def mixer(own, pos_d):
            t0 = T if own else 0
            sfx = "o" if own else "p"
            with contextlib.ExitStack() as sc:
                def sbl(name, shape, dt=F32):
                    return sc.enter_context(sbt(name + sfx, list(shape), dt))
                ycat = sbl("ycat", [128, 16, T], BF16); B_ycat = Buf("ycat")
                qnT = sbl("qnT", [128, 4, T], BF16) if own else None
                B_qn = Buf("qn")
                s_u = sc.enter_context(contextlib.ExitStack())
                uT = s_u.enter_context(sbt("uT" + sfx, [128, 8, T], BF16)); B_u = Buf("u")

                def rope_tables(s2):
                    ropc = s2.enter_context(sbt("ropc%d" % ctr["ring"], [64, T], F32))
                    rops = s2.enter_context(sbt("rops%d" % ctr["ring"], [64, T], F32))
                    B_rope = Buf("rope")
                    ctr["ring"] += 1
                    posi = s2.enter_context(sbt("posi%d" % ctr["ring"], [64, T], I32))
                    S.dma("sp", posi[:], pos_d, writes=[B_rope])
                    S.op("dve", lambda e: e.tensor_copy(out=rops[:], in_=posi[:]), reads=[B_rope], writes=[B_rope])
                    S.op("dve", lambda e: e.tensor_scalar(out=ropc[:], in0=rops[:], scalar1=col(C_INVF, 64), scalar2=None, op0=ALU.mult),
                         reads=[B_rope, B_const], writes=[B_rope])
                    R = [B_rope, B_const]
                    S.op("dve", lambda e: e.tensor_scalar(out=rops[:], in0=ropc[:], scalar1=1.0 / TWO_PI, scalar2=MAGIC, op0=ALU.mult, op1=ALU.add), reads=R, writes=R)
                    S.op("dve", lambda e: e.tensor_scalar(out=rops[:], in0=rops[:], scalar1=MAGIC, scalar2=-TWO_PI, op0=ALU.subtract, op1=ALU.mult), reads=R, writes=R)
                    S.op("dve", lambda e: e.tensor_tensor(out=rops[:], in0=rops[:], in1=ropc[:], op=ALU.add), reads=R, writes=R)
                    S.op("dve", lambda e: e.tensor_scalar(out=rops[:], in0=rops[:], scalar1=PI_SAFE, scalar2=-PI_SAFE, op0=ALU.min, op1=ALU.max), reads=R, writes=R)
                    S.op("dve", lambda e: e.scalar_tensor_tensor(out=ropc[:], in0=rops[:], scalar=-1.0, in1=rops[:], op0=ALU.mult, op1=ALU.max), reads=R, writes=R)
                    S.op("act", lambda e: e.activation(out=rops[:], in_=rops[:], func=AF.Sin), reads=R, writes=R)
                    S.op("act", lambda e: e.activation(out=ropc[:], in_=ropc[:], func=AF.Sin, scale=-1.0, bias=halfpi[0:64, 0:1]), reads=R, writes=R)
                    S.op("dve", lambda e: e.tensor_scalar(out=rops[:], in0=rops[:], scalar1=col(C_SGN, 64), scalar2=None, op0=ALU.mult), reads=R, writes=R)
                    return ropc, rops, B_rope

                with contextlib.ExitStack() as s2:
                    xnT, B_xn = ycat, B_ycat
                    ropc, rops, B_rope = rope_tables(s2)
                    if SUB == 21:
                        S.barrier()
                        return
                    zt = s2.enter_context(sbt("zt" + sfx, [128, 4, T], F32)); B_zt = Buf("zt")
                    ta = s2.enter_context(sbt("kta" + sfx, [64, 512], F32))
                    tb = s2.enter_context(sbt("ktb" + sfx, [64, 512], F32)); B_t = Buf("kt")
                    open_ring(s2, 2)
                    rmsnorm([(blk(xT, c), 128, B_xT) for c in range(NDC)], D,
                            [col(C_GM + c) for c in range(NDC)], [(blk(xnT, c), B_xn) for c in range(NDC)])
                    pieces = [0, 1, 2, 3, 6] + ([4, 5] if own else [])
                    for pc in pieces:
                        a, v = wpiece_cols(win_d, pc * 256, 256, 16)
                        wsl, B_w = load_w(a, v)
                        for sub in range(2):
                            for hb in range(2):
                                pst, B_pst = next_ps()
                                for dc in range(NDC):
                                    mm(pst[:, :], wsl[:, dc, sub * 128:(sub + 1) * 128], xnT[:, dc, hb * 512:(hb + 1) * 512],
                                       dc == 0, dc == NDC - 1, [B_w, B_xn], B_pst, signal=(dc == NDC - 1))
                                sl = slice(hb * 512, (hb + 1) * 512)
                                if pc < 4:
                                    dst, bd = uT[:, pc * 2 + sub, sl], B_u
                                elif pc == 6:
                                    dst, bd = zt[:, sub, sl], B_zt
                                else:
                                    dst, bd = zt[:, (pc - 4) * 2 + sub, sl], B_zt
                                S.op("act", lambda e, dst=dst, pst=pst: e.activation(out=dst, in_=pst[:, :], func=AF.Copy),
                                     reads=[B_pst], writes=[bd])
                        if pc == 6:
                            rmsnorm([(blk(zt, c), 128, B_zt) for c in range(2)], 256,
                                    [col(C_GKA + c) for c in range(2)],
                                    [((lambda b, c=c: kvnT[:, c, t0 + b * 512:t0 + (b + 1) * 512]), B_kvn) for c in range(2)])
                    if SUB == 22:
                        S.barrier()
                        return
                    if own:
                        rmsnorm([(blk(zt, c), 128, B_zt) for c in range(4)], 512,
                                [col(C_GQA + c) for c in range(4)], [(blk(qnT, c), B_qn) for c in range(4)])
                    if SUB == 23:
                        S.barrier()
                        return
                    a, v = wpiece_cols(win_d, 1792, 128, 16)
                    wsl, B_w = load_w(a, v)
                    for hb in range(2):
                        sl = slice(hb * 512, (hb + 1) * 512)
                        gsl = slice(t0 + hb * 512, t0 + (hb + 1) * 512)
                        pk, B_pk = next_ps()
                        pw, B_pw = next_ps()
                        for dc in range(NDC):
                            mm(pk[0:64, :], wsl[:, dc, 0:64], xnT[:, dc, sl], dc == 0, dc == NDC - 1, [B_w, B_xn], B_pk)
                        for dc in range(NDC):
                            mm(pw[0:64, :], wsl[:, dc, 64:128], xnT[:, dc, sl], dc == 0, dc == NDC - 1, [B_w, B_xn], B_pw)
                        if SUB == 24:
                            continue
                        S.op("act", lambda e, pk=pk, gsl=gsl: e.activation(out=sqkpe[:, gsl], in_=pk[0:64, :], func=AF.Square),
                             reads=[B_pk], writes=[B_sqk])
                        if SUB == 25:
                            continue
                        S.op("dve", lambda e, pk=pk, sl=sl: e.scalar_tensor_tensor(out=ta[:], in0=pk[0:64, :], scalar=col(C_GKR, 64), in1=ropc[:, sl], op0=ALU.mult, op1=ALU.mult),
                             reads=[B_pk, B_rope, B_const], writes=[B_t])
                        if SUB == 26:
                            continue
                        S.op("dve", lambda e, pw=pw, sl=sl: e.scalar_tensor_tensor(out=tb[:], in0=pw[0:64, :], scalar=col(C_GKS, 64), in1=rops[:, sl], op0=ALU.mult, op1=ALU.mult),
                             reads=[B_pw, B_rope, B_const, B_t], writes=[B_t])
                        if SUB == 27:
                            continue
                        S.op("dve", lambda e, gsl=gsl: e.tensor_tensor(out=krT[:, gsl], in0=ta[:], in1=tb[:], op=ALU.add),
                             reads=[B_t], writes=[B_kr])
                    S.barrier()
                if SUB == 2:
                    return
                with contextlib.ExitStack() as s2:
                    def s2b(name, shape, dt=F32):
                        return s2.enter_context(sbt(name + sfx, list(shape), dt))
                    tq = s2b("tq", [128, T]); B_tq = Buf("tq")
                    t1, t2 = tq[:, 0:512], tq[:, 512:1024]
                    s_loop = s2.enter_context(contextlib.ExitStack())

                    def s2b(name, shape, dt=F32):
                        return s_loop.enter_context(sbt(name + sfx, list(shape), dt))
                    Bl = s2b("Bl", [128, 32, 2, 128], BF16); B_Bl = Buf("Bl")
                    ncs2 = [s2b("ncs%d" % i, [128, T]) for i in range(2)]
                    nsn2 = [s2b("nsn%d" % i, [128, T]) for i in range(2)]
                    B_tab2 = [Buf("tab0"), Buf("tab1")]
                    rdec = s2b("rdec", [128, T]); B_rdec = Buf("rdec")
                    cre = s2b("cre", [128, T]); cim = s2b("cim", [128, T]); B_c = Buf("c")
                    sre = s2b("sre", [128, 512], BF16); sim = s2b("sim", [128, 512], BF16); B_s = Buf("s")
                    cl = [s2b("cl%d" % i, [128, 2, 128], BF16) for i in range(2)]
                    B_cl = [Buf("cl0"), Buf("cl1")]
                    Ddiag = s2b("Ddiag", [128, 8, 128], BF16)
                    with contextlib.ExitStack() as s3:
                        build_Bl(s3, Bl, B_Bl)
                        S.barrier()
                    if own:
                        for c in range(8):
                            S.op("dve", lambda e, c=c: e.tensor_scalar(out=Ddiag[:, c, :], in0=ident[:], scalar1=col(C_DSK + c), scalar2=None, op0=ALU.mult),
                                 reads=[B_const], writes=[B_const])
                    def gen_tables(ti):
                        ncs, nsn, B_tab = ncs2[ti % 2], nsn2[ti % 2], B_tab2[ti % 2]
                        th = sth[:, ti:ti + 1]
                        thn = sthn[:, ti:ti + 1]
                        S.op("act", lambda e: e.activation(out=nsn[:], in_=tidx[:], func=AF.Identity, scale=thn, bias=magicc[:, 0:1]),
                             reads=[B_const], writes=[B_tab])
                        S.op("act", lambda e: e.activation(out=nsn[:], in_=nsn[:], func=AF.Identity, scale=1.0, bias=nmagicc[:, 0:1]),
                             reads=[B_tab, B_const], writes=[B_tab])
                        S.op("act", lambda e: e.activation(out=ncs[:], in_=tidx[:], func=AF.Identity, scale=th),
                             reads=[B_const], writes=[B_tab])
                        S.op("dve", lambda e: e.scalar_tensor_tensor(out=nsn[:], in0=nsn[:], scalar=-TWO_PI, in1=ncs[:], op0=ALU.mult, op1=ALU.add),
                             reads=[B_tab], writes=[B_tab])
                        S.op("dve", lambda e: e.tensor_scalar(out=nsn[:], in0=nsn[:], scalar1=PI_SAFE, scalar2=-PI_SAFE, op0=ALU.min, op1=ALU.max),
                             reads=[B_tab], writes=[B_tab])
                        S.op("dve", lambda e: e.scalar_tensor_tensor(out=ncs[:], in0=nsn[:], scalar=-1.0, in1=nsn[:], op0=ALU.mult, op1=ALU.max), reads=[B_tab], writes=[B_tab])
                        S.op("act", lambda e: e.activation(out=nsn[:], in_=nsn[:], func=AF.Sin), reads=[B_tab], writes=[B_tab])
                        S.op("act", lambda e: e.activation(out=ncs[:], in_=ncs[:], func=AF.Sin, scale=-1.0, bias=halfpi[:, 0:1]), reads=[B_tab, B_const], writes=[B_tab])

                    for uc in range(8):
                        if own:
                            py = [long_ps(0), long_ps(1)]
                        for tl in range(4):
                            ti = uc * 4 + tl
                            if ti == 0:
                                gen_tables(0)
                            if ti + 1 < 32:
                                gen_tables(ti + 1)
                            ncs, nsn, B_tab = ncs2[ti % 2], nsn2[ti % 2], B_tab2[ti % 2]
                            S.op("act", lambda e, ti=ti: e.activation(out=rdec[:], in_=tidx[:], func=AF.Identity, scale=0.0, bias=srd[:, ti:ti + 1]),
                                 reads=[B_const], writes=[B_rdec])
                            if own:
                                j = ti % 2
                                S.dma("pool", cl[j][:], cblk_d[:, ti, :, :], writes=[B_cl[j]])
                            for hb in range(2):
                                sl = slice(hb * 512, (hb + 1) * 512)
                                pr, B_pr = next_ps()
                                pi_, B_pi = next_ps()
                                mm(pr[:, :], Bl[:, ti, 0, :], uT[:, uc, sl], True, True, [B_Bl, B_u], B_pr)
                                mm(pi_[:, :], Bl[:, ti, 1, :], uT[:, uc, sl], True, True, [B_Bl, B_u], B_pi)
                                S.op("dve", lambda e, pr=pr, sl=sl: e.tensor_tensor(out=t1, in0=ncs[:, sl], in1=pr[:, :], op=ALU.mult), reads=[B_tab, B_pr], writes=[B_tq])
                                S.op("dve", lambda e, pi_=pi_, sl=sl: e.tensor_tensor(out=t2, in0=nsn[:, sl], in1=pi_[:, :], op=ALU.mult), reads=[B_tab, B_pi], writes=[B_tq])
                                S.op("dve", lambda e, sl=sl: e.tensor_tensor(out=cre[:, sl], in0=t1, in1=t2, op=ALU.add), reads=[B_tq], writes=[B_c])
                                S.op("dve", lambda e, pi_=pi_, sl=sl: e.tensor_tensor(out=t1, in0=ncs[:, sl], in1=pi_[:, :], op=ALU.mult), reads=[B_tab, B_pi, B_c], writes=[B_tq])
                                S.op("dve", lambda e, pr=pr, sl=sl: e.tensor_tensor(out=t2, in0=nsn[:, sl], in1=pr[:, :], op=ALU.mult), reads=[B_tab, B_pr, B_c], writes=[B_tq])
                                S.op("dve", lambda e, sl=sl: e.tensor_tensor(out=cim[:, sl], in0=t1, in1=t2, op=ALU.subtract), reads=[B_tq], writes=[B_c])
                            S.op("dve", lambda e, ti=ti: e.tensor_tensor_scan(out=cre[:], data0=rdec[:], data1=cre[:], initial=state[:, ti, 0:1], op0=ALU.mult, op1=ALU.add),
                                 reads=[B_c, B_rdec, B_state], writes=[B_c])
                            S.op("dve", lambda e, ti=ti: e.tensor_tensor_scan(out=cim[:], data0=rdec[:], data1=cim[:], initial=state[:, ti, 1:2], op0=ALU.mult, op1=ALU.add),
                                 reads=[B_c, B_rdec, B_state], writes=[B_c])
                            if not own:
                                S.op("dve", lambda e, ti=ti: e.tensor_copy(out=state[:, ti, 0:1], in_=cre[:, T - 1:T]), reads=[B_c], writes=[B_state])
                                S.op("dve", lambda e, ti=ti: e.tensor_copy(out=state[:, ti, 1:2], in_=cim[:, T - 1:T]), reads=[B_c], writes=[B_state])
                                continue
                            for hb in range(2):
                                sl = slice(hb * 512, (hb + 1) * 512)
                                S.op("dve", lambda e, sl=sl: e.tensor_tensor(out=t1, in0=ncs[:, sl], in1=cre[:, sl], op=ALU.mult), reads=[B_tab, B_c, B_s], writes=[B_tq])
                                S.op("dve", lambda e, sl=sl: e.tensor_tensor(out=t2, in0=nsn[:, sl], in1=cim[:, sl], op=ALU.mult), reads=[B_tab, B_c], writes=[B_tq])
                                S.op("dve", lambda e: e.tensor_tensor(out=sre[:], in0=t1, in1=t2, op=ALU.subtract), reads=[B_tq], writes=[B_s])
                                S.op("dve", lambda e, sl=sl: e.tensor_tensor(out=t1, in0=nsn[:, sl], in1=cre[:, sl], op=ALU.mult), reads=[B_tab, B_c, B_s], writes=[B_tq])
                                S.op("dve", lambda e, sl=sl: e.tensor_tensor(out=t2, in0=ncs[:, sl], in1=cim[:, sl], op=ALU.mult), reads=[B_tab, B_c, B_s], writes=[B_tq])
                                S.op("dve", lambda e: e.scalar_tensor_tensor(out=sim[:], in0=t1, scalar=-1.0, in1=t2, op0=ALU.mult, op1=ALU.subtract), reads=[B_tq], writes=[B_s])
                                p_, B_p = py[hb]
                                mm(p_[:, :], cl[j][:, 0, :], sre[:], tl == 0, False, [B_cl[j], B_s], B_p, signal=False)
                                mm(p_[:, :], cl[j][:, 1, :], sim[:], False, False, [B_cl[j], B_s], B_p, signal=True)
                        if own:
                            for hb in range(2):
                                sl = slice(hb * 512, (hb + 1) * 512)
                                p_, B_p = py[hb]
                                mm(p_[:, :], Ddiag[:, uc, :], uT[:, uc, sl], False, True, [B_const, B_u], B_p)
                                S.op("act", lambda e, p_=p_: e.activation(out=t1, in_=p_[:, :], func=AF.Copy), reads=[B_p], writes=[B_tq])
                                S.op("dve", lambda e: e.tensor_tensor(out=t2, in0=t1, in1=t1, op=ALU.mult), reads=[B_tq], writes=[B_tq])
                                S.op("dve", lambda e: e.tensor_scalar(out=t2, in0=t2, scalar1=0.044715, scalar2=1.0, op0=ALU.mult, op1=ALU.add), reads=[B_tq], writes=[B_tq])
                                S.op("dve", lambda e: e.tensor_tensor(out=t2, in0=t2, in1=t1, op=ALU.mult), reads=[B_tq], writes=[B_tq])
                                S.op("act", lambda e: e.activation(out=t2, in_=t2, func=AF.Sigmoid, scale=2.0 * math.sqrt(2.0 / math.pi)), reads=[B_tq], writes=[B_tq])
                                S.op("dve", lambda e, uc=uc, sl=sl: e.tensor_tensor(out=ycat[:, 8 + uc, sl], in0=t2, in1=t1, op=ALU.mult), reads=[B_tq], writes=[B_ycat])
                    S.barrier()
                    s_loop.close()
                    if own:
                        open_ring(s2, 2)
                        pz = [long_ps(0), long_ps(1)]
                        for pc in range(4):
                            a, v = wpiece_cols(wglu_d, pc * 256, 256, 8)
                            wsl, B_w = load_w(a, v)
                            for sub in range(2):
                                oc = pc * 2 + sub
                                for hb in range(2):
                                    sl = slice(hb * 512, (hb + 1) * 512)
                                    pst, B_pst = next_ps()
                                    for c in range(8):
                                        mm(pst[:, :], wsl[:, c, sub * 128:(sub + 1) * 128], ycat[:, 8 + c, sl], c == 0, c == 7, [B_w, B_ycat], B_pst)
                                    S.op("act", lambda e, pst=pst, oc=oc: e.activation(out=t1, in_=pst[:, :], func=AF.Sigmoid, bias=col(C_BGLU + oc)),
                                         reads=[B_pst, B_const], writes=[B_tq])
                                    S.op("dve", lambda e, sl=sl, oc=oc: e.tensor_tensor(out=t1, in0=t1, in1=ycat[:, 8 + oc, sl], op=ALU.mult),
                                         reads=[B_tq, B_ycat], writes=[B_tq])
                                    S.op("act", lambda e, sl=sl, oc=oc: e.activation(out=ycat[:, oc, sl], in_=t1, func=AF.Copy), reads=[B_tq], writes=[B_ycat])
                                    jq = ctr["sq"] % 2; ctr["sq"] += 1
                                    S.op("act", lambda e, jq=jq: e.activation(out=sqt[jq][:], in_=t1, func=AF.Square), reads=[B_tq], writes=[B_sqt[jq]])
                                    p_, B_p = pz[hb]
                                    mm(p_[:, :], onesb[:, :], sqt[jq][:], oc == 0, oc == 7, [B_sqt[jq], B_const], B_p, signal=True)
                        for hb in range(2):
                            sl = slice(hb * 512, (hb + 1) * 512)
                            p_, B_p = pz[hb]
                            S.op("act", lambda e, p_=p_: e.activation(out=t1, in_=p_[:, :], func=AF.Ln, scale=1.0 / 1024, bias=epsc[:, 0:1]), reads=[B_p, B_const], writes=[B_tq])
                            S.op("act", lambda e: e.activation(out=t1, in_=t1, func=AF.Exp, scale=-0.5), reads=[B_tq], writes=[B_tq])
                            for c in range(8):
                                S.op("dve", lambda e, c=c, sl=sl: e.scalar_tensor_tensor(out=ycat[:, c, sl], in0=ycat[:, c, sl], scalar=col(C_GSO + c), in1=t1, op0=ALU.mult, op1=ALU.mult),
                                     reads=[B_tq, B_ycat, B_const], writes=[B_ycat])
                    S.barrier()
                s_u.close()
                if not own:
                    rotate_state()
                    S.barrier()
                    return
                if SUB == 3:
                    return
                with contextlib.ExitStack() as s2:
                    def s2b(name, shape, dt=F32):
                        return s2.enter_context(sbt(name, list(shape), dt))
                    ropc, rops, B_rope = rope_tables(s2)
                    ssqA = s2b("ssqA", [128, T]); B_ssqA = Buf("ssqA")
                    knT2 = [s2b("knT%d" % i, [128, 2 * T], BF16) for i in range(2)]
                    kpT2 = [s2b("kpT%d" % i, [64, 2 * T], BF16) for i in range(2)]
                    Vt2 = [s2b("Vt%d" % i, [128, 16, 128], BF16) for i in range(2)]
                    qnh2 = [s2b("qnh%d" % i, [128, T], BF16) for i in range(2)]
                    qrh2 = [s2b("qrh%d" % i, [64, T], BF16) for i in range(2)]
                    B_k2 = [Buf("k0"), Buf("k1")]; B_V2 = [Buf("V0"), Buf("V1")]; B_q2 = [Buf("q0"), Buf("q1")]
                    PT = [s2b("PT%d" % i, [128, 512], BF16) for i in range(2)]
                    B_PT = [Buf("PT0"), Buf("PT1")]
                    ra = s2b("ra", [64, 512]); rb = s2b("rb", [64, 512]); B_r = Buf("r")
                    rsum = s2b("rsum", [128, 512]); otmp = s2b("otmp", [128, 512]); B_rsum = Buf("rsum")
                    S.op("dve", lambda e: e.memset(ssqA[:], 0.0), writes=[B_ssqA])
                    wset = []
                    for i in range(2):
                        wset.append(dict(
                            qn=s2b("hwqn%d" % i, [128, 4, 128], BF16), qr=s2b("hwqr%d" % i, [128, 4, 64], BF16),
                            qs=s2b("hwqs%d" % i, [128, 4, 64], BF16), kk=s2b("hwkk%d" % i, [128, 2, 128], BF16),
                            kv=s2b("hwkv%d" % i, [128, 2, 128], BF16), B=Buf("hw%d" % i)))
                    pti = [0]

                    def prep_gen(h):
                        sl_ = h % 2
                        W = wset[sl_]
                        B_W = W["B"]
                        knT, kpT, Vt, qnh, qrh = knT2[sl_], kpT2[sl_], Vt2[sl_], qnh2[sl_], qrh2[sl_]
                        B_k, B_V, B_q = B_k2[sl_], B_V2[sl_], B_q2[sl_]
                        for nm, wd_, wcols in (("qn", wqn_d, 128), ("qr", wqr_d, 64), ("qs", wqs_d, 64), ("kk", wkk_d, 128), ("kv", wkv_d, 128)):
                            S.dma("pool", W[nm][:], wd_[:, h * wcols:(h + 1) * wcols].rearrange("(c p) n -> p c n", p=128), writes=[B_W])
                        yield
                        for kb in range(4):
                            sl = slice(kb * 512, (kb + 1) * 512)
                            pk, B_pk = next_ps()
                            for c in range(2):
                                mm(pk[:, :], W["kk"][:, c, :], kvnT[:, c, sl], c == 0, c == 1, [B_W, B_kvn], B_pk)
                            pm, B_pm = next_ps()
                            j = ctr["sq"] % 2; ctr["sq"] += 1
                            S.op("act", lambda e: e.activation(out=sqt[j][:], in_=pk[:, :], func=AF.Square), reads=[B_pk], writes=[B_sqt[j]])
                            mm(pm[:, :], onesb[:, :], sqt[j][:], True, False, [B_sqt[j], B_const], B_pm, signal=True)
                            mm(pm[:, :], onesb[0:64, :], sqkpe[:, sl], False, True, [B_sqk, B_const], B_pm)
                            k_ = ctr["rs"] % 2; ctr["rs"] += 1
                            S.op("act", lambda e: e.activation(out=rstd[k_][:], in_=pm[:, :], func=AF.Ln, scale=1.0 / 192, bias=epsc[:, 0:1]), reads=[B_pm, B_const], writes=[B_rstd[k_]])
                            S.op("act", lambda e: e.activation(out=rstd[k_][:], in_=rstd[k_][:], func=AF.Exp, scale=-0.5), reads=[B_rstd[k_]], writes=[B_rstd[k_]])
                            S.op("dve", lambda e: e.scalar_tensor_tensor(out=knT[:, sl], in0=pk[:, :], scalar=col(C_GKN), in1=rstd[k_][:], op0=ALU.mult, op1=ALU.mult),
                                 reads=[B_pk, B_rstd[k_], B_const], writes=[B_k])
                            S.op("dve", lambda e: e.tensor_tensor(out=kpT[:, sl], in0=krT[:, sl], in1=rstd[k_][0:64, :], op=ALU.mult),
                                 reads=[B_kr, B_rstd[k_]], writes=[B_k])
                            yield
                        for kc in range(16):
                            if kc % 4 == 0:
                                pv, B_pv = next_ps()
                            q4 = kc % 4
                            for c in range(2):
                                mm(pv[:, q4 * 128:(q4 + 1) * 128], kvnT[:, c, kc * 128:(kc + 1) * 128], W["kv"][:, c, :], c == 0, c == 1,
                                   [B_W, B_kvn], B_pv, signal=(c == 1 and q4 == 3))
                            if q4 == 3:
                                S.op("act", lambda e: e.activation(out=Vt[:, kc - 3:kc + 1, :], in_=pv[:, :].rearrange("p (a n) -> p a n", a=4), func=AF.Copy),
                                     reads=[B_pv], writes=[B_V])
                                yield
                        for hb in range(2):
                            sl = slice(hb * 512, (hb + 1) * 512)
                            pq, B_pq = next_ps(); pr_, B_pr = next_ps(); pw, B_pw = next_ps(); pm, B_pm = next_ps()
                            for c in range(4):
                                mm(pq[:, :], W["qn"][:, c, :], qnT[:, c, sl], c == 0, c == 3, [B_W, B_qn], B_pq)
                            for c in range(4):
                                mm(pr_[0:64, :], W["qr"][:, c, :], qnT[:, c, sl], c == 0, c == 3, [B_W, B_qn], B_pr)
                            for c in range(4):
                                mm(pw[0:64, :], W["qs"][:, c, :], qnT[:, c, sl], c == 0, c == 3, [B_W, B_qn], B_pw)
                            j = ctr["sq"] % 2; ctr["sq"] += 1
                            S.op("act", lambda e: e.activation(out=sqt[j][:], in_=pq[:, :], func=AF.Square), reads=[B_pq], writes=[B_sqt[j]])
                            mm(pm[:, :], onesb[:, :], sqt[j][:], True, False, [B_sqt[j], B_const], B_pm, signal=True)
                            j2 = ctr["sq"] % 2; ctr["sq"] += 1
                            S.op("act", lambda e: e.activation(out=sqt[j2][0:64, :], in_=pr_[0:64, :], func=AF.Square), reads=[B_pr], writes=[B_sqt[j2]])
                            mm(pm[:, :], onesb[0:64, :], sqt[j2][0:64, :], False, True, [B_sqt[j2], B_const], B_pm)
                            k_ = ctr["rs"] % 2; ctr["rs"] += 1
                            S.op("act", lambda e: e.activation(out=rstd[k_][:], in_=pm[:, :], func=AF.Ln, scale=1.0 / 192, bias=epsc[:, 0:1]), reads=[B_pm, B_const], writes=[B_rstd[k_]])
                            S.op("act", lambda e: e.activation(out=rstd[k_][:], in_=rstd[k_][:], func=AF.Exp, scale=-0.5), reads=[B_rstd[k_]], writes=[B_rstd[k_]])
                            S.op("dve", lambda e: e.scalar_tensor_tensor(out=qnh[:, sl], in0=pq[:, :], scalar=col(C_GQN), in1=rstd[k_][:], op0=ALU.mult, op1=ALU.mult),
                                 reads=[B_pq, B_rstd[k_], B_const], writes=[B_q])
                            S.op("dve", lambda e: e.scalar_tensor_tensor(out=ra[:], in0=pr_[0:64, :], scalar=col(C_GQR, 64), in1=ropc[:, sl], op0=ALU.mult, op1=ALU.mult),
                                 reads=[B_pr, B_rope, B_const], writes=[B_r])
                            S.op("dve", lambda e: e.scalar_tensor_tensor(out=rb[:], in0=pw[0:64, :], scalar=col(C_GQS, 64), in1=rops[:, sl], op0=ALU.mult, op1=ALU.mult),
                                 reads=[B_pw, B_rope, B_const, B_r], writes=[B_r])
                            S.op("dve", lambda e: e.tensor_tensor(out=ra[:], in0=ra[:], in1=rb[:], op=ALU.add), reads=[B_r], writes=[B_r])
                            S.op("dve", lambda e: e.tensor_tensor(out=qrh[:, sl], in0=ra[:], in1=rstd[k_][0:64, :], op=ALU.mult),
                                 reads=[B_r, B_rstd[k_]], writes=[B_q])
                            yield

                    def score_gen(h):
                        sl_ = h % 2
                        knT, kpT, Vt, qnh, qrh = knT2[sl_], kpT2[sl_], Vt2[sl_], qnh2[sl_], qrh2[sl_]
                        B_k, B_V, B_q = B_k2[sl_], B_V2[sl_], B_q2[sl_]
                        for qb in range(2):
                            po, B_po = long_ps(0)
                            pz_, B_pz = long_ps(1)
                            nkc = 8 + 4 * qb + 4

                            def s_mm(kc):
                                jl = kc - 8 - 4 * qb
                                c0 = max(jl, 0) * 128
                                qs = slice(qb * 512 + c0, (qb + 1) * 512)
                                cs = slice(c0, 512)
                                pS, B_pS = next_ps()
                                mm(pS[:, cs], knT[:, kc * 128:(kc + 1) * 128], qnh[:, qs], True, False, [B_k, B_q], B_pS, signal=False)
                                mm(pS[:, cs], kpT[:, kc * 128:(kc + 1) * 128], qrh[:, qs], False, True, [B_k, B_q], B_pS)
                                return pS, B_pS, jl, c0, cs
                            nxt = s_mm(0)
                            for kc in range(nkc):
                                pS, B_pS, jl, c0, cs = nxt
                                if kc + 1 < nkc:
                                    nxt = s_mm(kc + 1)
                                pj = pti[0] % 2; pti[0] += 1
                                bias = maskb[:, 0:1] if kc < 8 else zeroc[:, 0:1]
                                S.op("act", lambda e: e.activation(out=PT[pj][:, cs], in_=pS[:, cs], func=AF.Exp, scale=SCALE, bias=bias),
                                     reads=[B_pS, B_const], writes=[B_PT[pj]])
                                if jl >= 0:
                                    S.op("dve", lambda e: e.memset(PT[pj][64:128, c0:c0 + 64], 0.0), reads=[B_PT[pj]], writes=[B_PT[pj]])
                                last = kc == nkc - 1
                                mm(po[:, cs], Vt[:, kc, :], PT[pj][:, cs], kc == 0, last, [B_V, B_PT[pj]], B_po, signal=last)
                                mm(pz_[:, cs], onesb[:, :], PT[pj][:, cs], kc == 0, last, [B_const, B_PT[pj]], B_pz, signal=True)
                                yield
                            sl = slice(qb * 512, (qb + 1) * 512)
                            S.op("act", lambda e: e.activation(out=rsum[:], in_=pz_[:, :], func=AF.Ln), reads=[B_pz], writes=[B_rsum])
                            S.op("act", lambda e: e.activation(out=rsum[:], in_=rsum[:], func=AF.Exp, scale=-1.0), reads=[B_rsum], writes=[B_rsum])
                            S.op("dve", lambda e: e.tensor_tensor(out=otmp[:], in0=po[:, :], in1=rsum[:], op=ALU.mult), reads=[B_po, B_rsum], writes=[B_rsum])
                            S.op("act", lambda e: e.activation(out=ycat[:, 8 + h, sl], in_=otmp[:], func=AF.Copy), reads=[B_rsum], writes=[B_ycat])
                            jq = ctr["sq"] % 2; ctr["sq"] += 1
                            S.op("act", lambda e: e.activation(out=sqt[jq][:], in_=otmp[:], func=AF.Square), reads=[B_rsum], writes=[B_sqt[jq]])
                            pa, B_pa = next_ps()
                            mm(pa[:, :], onesb[:, :], sqt[jq][:], True, True, [B_sqt[jq], B_const], B_pa)
                            S.op("dve", lambda e: e.tensor_tensor(out=ssqA[:, sl], in0=pa[:, :], in1=ssqA[:, sl], op=ALU.add), reads=[B_pa, B_ssqA], writes=[B_ssqA])
                            yield

                    for _ in prep_gen(0):
                        pass
                    for h in range(8):
                        g1 = score_gen(h)
                        g2 = prep_gen(h + 1) if h < 7 else iter(())
                        done1 = done2 = False
                        step = 0
                        while not (done1 and done2):
                            if not done1:
                                try:
                                    next(g1)
                                except StopIteration:
                                    done1 = True
                            if not done2 and (step % 2 == 1 or done1):
                                try:
                                    next(g2)
                                except StopIteration:
                                    done2 = True
                            step += 1
                    for hb in range(2):
                        sl = slice(hb * 512, (hb + 1) * 512)
                        S.op("act", lambda e: e.activation(out=rsum[:], in_=ssqA[:, sl], func=AF.Ln, scale=1.0 / 1024, bias=epsc[:, 0:1]), reads=[B_ssqA, B_const], writes=[B_rsum])
                        S.op("act", lambda e: e.activation(out=rsum[:], in_=rsum[:], func=AF.Exp, scale=-0.5), reads=[B_rsum], writes=[B_rsum])
                        for c in range(8):
                            S.op("dve", lambda e: e.scalar_tensor_tensor(out=ycat[:, 8 + c, sl], in0=ycat[:, 8 + c, sl], scalar=col(C_GAO + c), in1=rsum[:], op0=ALU.mult, op1=ALU.mult),
                                 reads=[B_rsum, B_ycat, B_const], writes=[B_ycat])
                    S.barrier()
                if SUB == 4:
                    return
                with contextlib.ExitStack() as s2:
                    open_ring(s2, 2)
                    for pc in range(8):
                        a, v = wpiece_cols(wout_d, pc * 256, 256, 16)
                        wsl, B_w = load_w(a, v)
                        for sub in range(2):
                            dc = pc * 2 + sub
                            for hb in range(2):
                                sl = slice(hb * 512, (hb + 1) * 512)
                                pst, B_pst = next_ps()
                                for c in range(16):
                                    mm(pst[:, :], wsl[:, c, sub * 128:(sub + 1) * 128], ycat[:, c, sl], c == 0, c == 15, [B_w, B_ycat], B_pst, signal=(c == 15))
                                S.op("dve", lambda e, pst=pst, dc=dc, sl=sl: e.tensor_tensor(out=xT[:, dc, sl], in0=pst[:, :], in1=xT[:, dc, sl], op=ALU.add),
                                     reads=[B_pst, B_xT], writes=[B_xT])
                    S.barrier()

        def store_out(do_norm):
            with contextlib.ExitStack() as sc:
                xo = [sc.enter_context(sbt("xo%d" % i, [128, D], F32)) for i in range(2)]
                B_xo = [Buf("xo0"), Buf("xo1")]
                gf = sc.enter_context(sbt("gf", [128, D], F32)); B_gf = Buf("gf")
                ssq = sc.enter_context(sbt("ssq", [128, 8]))
                junk = sc.enter_context(sbt("junk", [128, D], F32)); B_j = Buf("junk")
                B_ssq = Buf("ssq")
                S.dma("sp", gf[:], gfin_d, writes=[B_gf])
                S.op("dve", lambda e: e.memset(ssq[:], 0.0), writes=[B_ssq])
                for tc in range(T // 128):
                    j = tc % 2
                    for q in range(4):
                        pst, B_pst = next_ps()
                        for r in range(4):
                            dc = q * 4 + r
                            S.op("pe", lambda e, pst=pst, r=r, dc=dc, tc=tc: e.transpose(
                                pst[:, r * 128:(r + 1) * 128], xT[:, dc, tc * 128:(tc + 1) * 128], ident[:]),
                                reads=[B_xT, B_const], writes=[B_pst], signal=(r == 3))
                        S.op("act", lambda e, pst=pst, q=q, j=j: e.activation(out=xo[j][:, q * 512:(q + 1) * 512], in_=pst[:, :], func=AF.Copy),
                             reads=[B_pst], writes=[B_xo[j]])
                    if do_norm:
                        S.op("act", lambda e, j=j, tc=tc: e.activation(out=junk[:], in_=xo[j][:], func=AF.Square, accum_out=ssq[:, tc:tc + 1]),
                             reads=[B_xo[j], B_ssq], writes=[B_j, B_ssq])
                        S.op("act", lambda e, tc=tc: e.activation(out=ssq[:, tc:tc + 1], in_=ssq[:, tc:tc + 1], func=AF.Sqrt, scale=1.0 / D, bias=epsc[:, 0:1]),
                             reads=[B_ssq, B_const], writes=[B_ssq])
                        S.op("dve", lambda e, tc=tc: e.reciprocal(out=ssq[:, tc:tc + 1], in_=ssq[:, tc:tc + 1]), reads=[B_ssq], writes=[B_ssq])
                        S.op("dve", lambda e, j=j, tc=tc: e.scalar_tensor_tensor(out=xo[j][:], in0=xo[j][:], scalar=ssq[:, tc:tc + 1], in1=gf[:], op0=ALU.mult, op1=ALU.mult),
                             reads=[B_xo[j], B_ssq, B_gf], writes=[B_xo[j]])
                    S.dma("sp", out_d[tc * 128:(tc + 1) * 128, :], xo[j][:], reads=[B_xo[j]])
                S.barrier()

        if st is None or st in (2, 3, 5):
            ssm_prep()
            load_xT(x_pre)
            if USE_FFN1:
                ffn(C_G1, w1g, w1u, w1d)
            if SUB != 20:
                mixer(False, pos_pre)
        load_xT(x_own)
        if USE_FFN1:
            ffn(C_G1, w1g, w1u, w1d)
        if (st is None or st in (2, 3, 5)) and SUB != 20:
            mixer(True, pos_own)
        if USE_FFN2:
            ffn(C_G2, w2g, w2u, w2d)
        store_out(st is None)
        S.barrier()
        with nc.Block() as block:
            S.emit(block)
    nc._in_names = in_names
    return nc


def _cols16(g):
    return np.ascontiguousarray(np.asarray(g, np.float32).reshape(-1, 128).T)


def _prep_shared(inp):
    f = lambda k: np.asarray(inp[k], np.float32)[0]
    sh = {}
    cols = np.zeros((128, 128), np.float32)
    cols[:, 0:16] = _cols16(f("ffn1_norm")); cols[:, 16:32] = _cols16(f("mix_norm"))
    cols[:, 32:48] = _cols16(f("ffn2_norm")); cols[:, 48:64] = _cols16(f("final_norm"))
    cols[:, 64:68] = _cols16(f("mla_q_a_norm")); cols[:, 68:70] = _cols16(f("mla_kv_a_norm"))
    cols[:, 70:78] = _cols16(f("ssm_out_norm")); cols[:, 78:86] = _cols16(f("attn_out_norm"))
    cols[:, 86:94] = _cols16(f("ssm_b_glu")); cols[:, 94:102] = _cols16(f("ssm_d").reshape(-1))
    perm = np.concatenate([np.arange(32, 64), np.arange(0, 32)])
    qn, kn = f("mla_q_norm"), f("mla_k_norm")
    cols[:, 102] = qn[:128]; cols[:64, 103] = qn[128:]; cols[:64, 104] = qn[128:][perm]
    cols[:, 105] = kn[:128]; cols[:64, 106] = kn[128:]; cols[:64, 107] = kn[128:][perm]
    invf = (10000.0 ** (-np.arange(0, 64, 2, dtype=np.float32) / 64)).astype(np.float32)
    cols[:64, 108] = np.concatenate([invf, invf])
    cols[:32, 109] = -1.0; cols[32:64, 109] = 1.0
    sh["cols"] = cols
    sh["ident"] = np.eye(128, dtype=np.float32)
    sh["tidx"] = np.ascontiguousarray(np.broadcast_to(np.arange(T, dtype=np.float32), (128, T)))
    ls, ar, ai = f("ssm_log_step"), f("ssm_a_re"), f("ssm_a_im")
    ssmp = np.zeros((128, 3, 32), np.float32)
    bblk = np.zeros((128, 32, 2, 128), np.float32)
    cblk = np.zeros((128, 32, 2, 128), np.float32)
    bre, bim, cre, cim = f("ssm_b_re"), f("ssm_b_im"), f("ssm_c_re"), f("ssm_c_im")
    for ti in range(32):
        for gl in range(2):
            g = 2 * ti + gl
            rows = slice(gl * 64, gl * 64 + 64)
            ssmp[rows, 0, ti] = ls[g]; ssmp[rows, 1, ti] = ar[g]; ssmp[rows, 2, ti] = ai[g]
            c0 = (ti % 4) * 32 + gl * 16
            bblk[rows, ti, 0, c0:c0 + 16] = bre[g]; bblk[rows, ti, 1, c0:c0 + 16] = bim[g]
            cblk[rows, ti, 0, c0:c0 + 16] = cre[g].T; cblk[rows, ti, 1, c0:c0 + 16] = cim[g].T
    sh["ssmp"], sh["bblk"], sh["cblk"] = ssmp, bblk, cblk
    sh["gfin"] = np.ascontiguousarray(np.broadcast_to(f("final_norm"), (128, D)))
    sh["w1g"], sh["w1u"], sh["w1d"] = f("ffn1_w_gate"), f("ffn1_w_up"), f("ffn1_w_down")
    sh["w2g"], sh["w2u"], sh["w2d"] = f("ffn2_w_gate"), f("ffn2_w_up"), f("ffn2_w_down")
    win = f("w_in")
    sh["win"] = np.ascontiguousarray(np.concatenate([win, win[:, 1792:][:, perm]], axis=1))
    sh["wglu"] = f("ssm_w_glu")
    wq = f("mla_w_q_up").reshape(512, 8, 192)
    sh["wqn"] = np.ascontiguousarray(wq[:, :, :128].reshape(512, 1024))
    sh["wqr"] = np.ascontiguousarray(wq[:, :, 128:].reshape(512, 512))
    sh["wqs"] = np.ascontiguousarray(wq[:, :, 128:][:, :, perm].reshape(512, 512))
    wkv = f("mla_w_kv_up").reshape(256, 8, 256)
    sh["wkk"] = np.ascontiguousarray(wkv[:, :, :128].reshape(256, 1024))
    sh["wkv"] = np.ascontiguousarray(wkv[:, :, 128:].reshape(256, 1024))
    sh["wout"] = f("w_out")
    return sh


def kernel(**inputs):
    x = np.asarray(inputs["x"], np.float32)
    pos = np.asarray(inputs["positions"], np.int32)
    sh = _prep_shared(inputs)
    in_maps = []
    for c in range(8):
        b, p = c // 2, c % 2
        m = dict(sh)
        m["x_own"] = np.ascontiguousarray(x[b, p * T:(p + 1) * T])
        m["pos_own"] = np.ascontiguousarray(np.broadcast_to(pos[b, p * T:(p + 1) * T], (64, T)))
        if p == 1:
            m["x_pre"] = np.ascontiguousarray(x[b, 0:T])
            m["pos_pre"] = np.ascontiguousarray(np.broadcast_to(pos[b, 0:T], (64, T)))
            m["maskbias"] = np.zeros((128, 1), np.float32)
        else:
            m["x_pre"] = np.zeros((T, D), np.float32)
            m["pos_pre"] = np.zeros((64, T), np.int32)
            m["maskbias"] = np.full((128, 1), -30000.0, np.float32)
        in_maps.append(m)
    nc = build_program()
    in_maps = [{k: m[k] for k in nc._in_names} for m in in_maps]
    res = run_bass_kernel_spmd(nc, in_maps, core_ids=list(range(8)))
    out = np.zeros((4, 2 * T, D), np.float32)
    for c in range(8):
        out[c // 2, (c % 2) * T:(c % 2 + 1) * T] = res.results[c]["out"]
    return out
```

```python
import math
import numpy as np
import concourse.bass as bass
import concourse.mybir as mybir
from concourse.bass_utils import run_bass_kernel_spmd

F32 = mybir.dt.float32
BF16 = mybir.dt.bfloat16
I32 = mybir.dt.int32
AF = mybir.ActivationFunctionType
ALU = mybir.AluOpType

D = 2048
T = 1024
NDC = 16
DFF = 5632
NG = DFF // 256
EPS = 1e-6
TWO_PI = 2.0 * math.pi
PI_SAFE = 3.1415925
SCALE = 192 ** -0.5
_DBG_STAGE = None
_DBG_SUB = None


class Buf:
    __slots__ = ("w", "r", "dsem", "dcnt", "name", "excl")

    def __init__(self, name="", excl=False):
        self.excl = excl
        self.w = None
        self.r = {}
        self.dsem = None
        self.dcnt = 0
        self.name = name


class _Rec:
    def __getattr__(self, name):
        def f(*a, **k):
            self.__dict__["call"] = (name, a, k)
            return self
        return f


class Sched:
    ENG = ("sp", "act", "dve", "pool", "pe")

    def __init__(self, nc, sems, dma_sems):
        self.nc = nc
        self.sem = sems
        half = len(dma_sems) // 2
        self.free_dma_sems = {"pool": list(dma_sems[:half]), "sp": list(dma_sems[half:])}
        self.ops = {e: [] for e in self.ENG}
        self.cnt = {e: 0 for e in self.ENG}
        self.seen = {e: {} for e in self.ENG}
        self.all_dma = {}
        self.sem_bufs = []

    def _deps(self, e, reads, writes):
        deps = {}

        def add(k, v):
            if v > deps.get(k, 0):
                deps[k] = v
        for b in reads:
            if b.w is not None:
                add(*b.w)
            if b.excl:
                for k, v in b.r.items():
                    add(k, v)
        for b in writes:
            if b.w is not None:
                add(*b.w)
            for k, v in b.r.items():
                add(k, v)
        waits = []
        for k, v in deps.items():
            if k is self.sem["pe"] and e == "pe":
                continue
            if v > self.seen[e].get(k, 0):
                self.seen[e][k] = v
                waits.append((k, v))
        return waits

    def op(self, e, emit, reads=(), writes=(), signal=True):
        rec = _Rec()
        emit(rec)
        name, a, k = rec.call
        emit = lambda eng, name=name, a=a, k=k: getattr(eng, name)(*a, **k)
        waits = self._deps(e, reads, writes)
        if signal:
            self.cnt[e] += 1
            val = self.cnt[e]
            sig = (self.sem[e], 1)
        else:
            val = self.cnt[e] + 1
            sig = None
        self.ops[e].append((waits, emit, sig))
        key = self.sem[e]
        for b in reads:
            if b.r.get(key, 0) < val:
                b.r[key] = val
        for b in writes:
            b.w = (key, val)
            b.r = {}

    def dma(self, q, out, in_, reads=(), writes=(), sembuf=None):
        sb = sembuf if sembuf is not None else (writes[0] if writes else reads[0])
        if sb.dsem is None:
            sb.dsem = self.free_dma_sems[q].pop()
            self.sem_bufs.append((sb, q))
        waits = self._deps(q, reads, writes)
        sem = sb.dsem
        val = self.all_dma.get(sem, 0) + 16
        self.ops[q].append((waits, lambda eng: eng.dma_start(out=out, in_=in_), (sem, 16)))
        self.all_dma[sem] = val
        for b in reads:
            if b.r.get(sem, 0) < val:
                b.r[sem] = val
        for b in writes:
            b.w = (sem, val)
            b.r = {}

    def barrier(self):
        targets = [(self.sem[f], self.cnt[f]) for f in self.ENG if self.cnt[f] > 0]
        targets += list(self.all_dma.items())
        for e in self.ENG:
            waits = []
            for k, v in targets:
                if k is self.sem[e]:
                    continue
                if v > self.seen[e].get(k, 0):
                    self.seen[e][k] = v
                    waits.append((k, v))
            if waits:
                self.ops[e].append((waits, None, None))
        for b, q in self.sem_bufs:
            self.free_dma_sems[q].append(b.dsem)
            b.dsem = None
        self.sem_bufs = []

    def emit(self, block):
        decos = {"sp": block.sync, "act": block.scalar, "dve": block.vector,
                 "pool": block.gpsimd, "pe": block.tensor}
        for name in self.ENG:
            ops = self.ops[name]

            def body(eng, ops=ops):
                for waits, emit, sig in ops:
                    for (s, v) in waits:
                        eng.wait_ge(s, v)
                    if emit is None:
                        continue
                    ins = emit(eng)
                    if sig is not None:
                        ins.then_inc(sig[0], sig[1])
            decos[name](body)


def build_program():
    nc = bass.Bass("TRN2", target_bir_lowering=False)

    in_names = []

    def din(name, shape, dt=F32):
        in_names.append(name)
        return nc.dram_tensor(name, list(shape), dt, kind="ExternalInput").ap()

    x_pre = din("x_pre", [T, D])
    x_own = din("x_own", [T, D])
    pos_pre = din("pos_pre", [64, T], I32)
    pos_own = din("pos_own", [64, T], I32)
    maskb_d = din("maskbias", [128, 1])
    ident_d = din("ident", [128, 128])
    tidx_d = din("tidx", [128, T])
    cols_d = din("cols", [128, 128])
    ssmp_d = din("ssmp", [128, 3, 32])
    bblk_d = din("bblk", [128, 32, 2, 128])
    cblk_d = din("cblk", [128, 32, 2, 128])
    gfin_d = din("gfin", [128, D])
    st = _DBG_STAGE
    SUB = _DBG_SUB
    USE_FFN1 = st not in (0, 5)
    USE_FFN2 = st is None or st == 3
    if USE_FFN1:
        w1g = din("w1g", [D, DFF]); w1u = din("w1u", [D, DFF]); w1d = din("w1d", [DFF, D])
    if USE_FFN2:
        w2g = din("w2g", [D, DFF]); w2u = din("w2u", [D, DFF]); w2d = din("w2d", [DFF, D])
    win_d = din("win", [D, 1920])
    wglu_d = din("wglu", [1024, 1024])
    wqn_d = din("wqn", [512, 1024]); wqr_d = din("wqr", [512, 512]); wqs_d = din("wqs", [512, 512])
    wkk_d = din("wkk", [256, 1024]); wkv_d = din("wkv", [256, 1024])
    wout_d = din("wout", [D, D])
    out_d = nc.dram_tensor("out", [T, D], F32, kind="ExternalOutput").ap()

    C_G1, C_GM, C_G2, C_GF = 0, 16, 32, 48
    C_GQA, C_GKA = 64, 68
    C_GSO, C_GAO, C_BGLU, C_DSK = 70, 78, 86, 94
    C_GQN, C_GQR, C_GQS, C_GKN, C_GKR, C_GKS = 102, 103, 104, 105, 106, 107
    C_INVF, C_SGN = 108, 109

    import contextlib
    es = contextlib.ExitStack()
    uid = [0]

    def sbt(name, shape, dt=F32):
        uid[0] += 1
        return nc.sbuf_tensor("sb%d_%s" % (uid[0], name), list(shape), dt)
    with es:
        def sb(name, shape, dt=F32):
            return es.enter_context(sbt(name, list(shape), dt))

        sems = {e: es.enter_context(nc.semaphore("s_" + e)) for e in Sched.ENG}
        dma_sems = [es.enter_context(nc.semaphore("d%d" % i)) for i in range(24)]
        S = Sched(nc, sems, dma_sems)

        xT = sb("xT", [128, NDC, T]); B_xT = Buf("xT")
        ident = sb("ident", [128, 128])
        onesb = sb("onesb", [128, 128], BF16)
        cols = sb("cols", [128, 128])
        maskb = sb("maskb", [128, 1])
        tidx = sb("tidx", [128, T])
        ssmp = sb("ssmp", [128, 3, 32])
        sth = sb("sth", [128, 32]); srd = sb("srd", [128, 32])
        sfr = sb("sfr", [128, 32]); sfi = sb("sfi", [128, 32])
        scT = sb("scT", [128, 32]); ssT = sb("ssT", [128, 32])
        stmp = sb("stmp", [128, 8, 32])
        state = sb("state", [128, 32, 2])
        kvnT = sb("kvnT", [128, 2, 2 * T], BF16)
        krT = sb("krT", [64, 2 * T], BF16)
        sqkpe = sb("sqkpe", [64, 2 * T], BF16)
        negpi = sb("negpi", [128, 1]); halfpi = sb("halfpi", [128, 1]); sthn = sb("sthn", [128, 32]); epsc = sb("epsc", [128, 1]); zeroc = sb("zeroc", [128, 1])
        B_const = Buf("const"); B_state = Buf("state")
        B_kvn = Buf("kvn"); B_kr = Buf("kr"); B_sqk = Buf("sqk")
        ring = {"slots": [], "bufs": [], "i": 0}

        def open_ring(sc, n):
            ring["slots"] = [sc.enter_context(sbt("wslot%d_%d" % (i, ctr["ring"]), [128, 4096], BF16)) for i in range(n)]
            ring["bufs"] = [Buf("ws%d" % i) for i in range(n)]
            ring["i"] = 0
            ctr["ring"] += 1
        ps = [es.enter_context(nc.psum_tensor("ps%d" % i, [128, 512], F32)) for i in range(8)]
        B_ps = [Buf("ps%d" % i, excl=True) for i in range(8)]

        def load_w(src_ap, view):
            i = ring["i"] % len(ring["slots"])
            ring["i"] += 1
            wslot, B_ws = ring["slots"], ring["bufs"]
            if view[0] == "c":
                dst = wslot[i][:, 0:view[1] * view[2]].rearrange("p (c n) -> p c n", c=view[1])
            else:
                dst = wslot[i][:, 0:view[1]]
            S.dma("pool", dst, src_ap, writes=[B_ws[i]])
            return dst, B_ws[i]

        def wpiece_cols(w, k0, ncol, nchunk):
            return w[:, k0:k0 + ncol].rearrange("(c p) n -> p c n", p=128), ("c", nchunk, ncol)

        def wpiece_rows(w, r0, nrow_chunks, ncol):
            return (w[r0:r0 + 128 * nrow_chunks, :].rearrange("(c p) n -> p c n", p=128),
                    ("c", nrow_chunks, ncol))

        def mm(out, lhsT, rhs, start, stop, reads, bank, signal=None):
            sig = True if signal is None else signal
            S.op("pe", lambda e: e.matmul(out, lhsT=lhsT, rhs=rhs, start=start, stop=stop),
                 reads=reads, writes=[bank], signal=sig)

        def col(c, n=128):
            return cols[0:n, c:c + 1]

        ctr = {"ring": 0}
        for dst, src in ((ident[:], ident_d), (cols[:], cols_d), (maskb[:], maskb_d),
                         (tidx[:], tidx_d), (ssmp[:], ssmp_d)):
            S.dma("sp", dst, src, writes=[B_const], sembuf=B_const)
        S.op("dve", lambda e: e.memset(onesb[:], 1.0), writes=[B_const])
        S.op("dve", lambda e: e.memset(negpi[:], -math.pi), writes=[B_const])
        S.op("dve", lambda e: e.memset(halfpi[:], 0.5 * math.pi), writes=[B_const])
        S.op("dve", lambda e: e.memset(epsc[:], EPS), writes=[B_const])
        S.op("dve", lambda e: e.memset(zeroc[:], 0.0), writes=[B_const])
        S.op("dve", lambda e: e.memset(state[:], 0.0), writes=[B_state])

        sqt = [sb("sqt%d" % i, [128, 512], BF16) for i in range(2)]
        B_sqt = [Buf("sqt0"), Buf("sqt1")]
        rstd = [sb("rstd%d" % i, [128, 512]) for i in range(2)]
        B_rstd = [Buf("rstd0"), Buf("rstd1")]
        ctr["sq"] = 0; ctr["rs"] = 0; ctr["ps"] = 0; ctr["psn"] = 6

        def next_ps():
            i = ctr["ps"] % ctr["psn"]
            ctr["ps"] += 1
            return ps[i], B_ps[i]

        def long_ps(i):
            return ps[6 + i], B_ps[6 + i]

        def rmsnorm(srcs, nfeat, gains, outs, ntok=T):
            for blk in range(ntok // 512):
                pst, B_pst = next_ps()
                n = len(srcs)
                for i, (fn, P, bsrc) in enumerate(srcs):
                    j = ctr["sq"] % 2
                    ctr["sq"] += 1
                    src = fn(blk)
                    S.op("act", lambda e, j=j, src=src, P=P: e.activation(out=sqt[j][0:P, :], in_=src, func=AF.Square),
                         reads=[bsrc], writes=[B_sqt[j]])
                    mm(pst[:, :], onesb[0:P, :], sqt[j][0:P, :], i == 0, i == n - 1, [B_sqt[j], B_const], B_pst)
                k = ctr["rs"] % 2
                ctr["rs"] += 1
                S.op("act", lambda e, k=k, pst=pst: e.activation(out=rstd[k][:], in_=pst[:, :], func=AF.Ln,
                                                                scale=1.0 / nfeat, bias=epsc[:, 0:1]),
                     reads=[B_pst, B_const], writes=[B_rstd[k]])
                S.op("act", lambda e, k=k: e.activation(out=rstd[k][:], in_=rstd[k][:], func=AF.Exp, scale=-0.5),
                     reads=[B_rstd[k]], writes=[B_rstd[k]])
                for (fn, P, bsrc), g, (ofn, bout) in zip(srcs, gains, outs):
                    src = fn(blk)
                    o = ofn(blk)
                    S.op("dve", lambda e, src=src, o=o, g=g, k=k, P=P: e.scalar_tensor_tensor(
                        out=o, in0=src, scalar=g, in1=rstd[k][0:P, :], op0=ALU.mult, op1=ALU.mult),
                        reads=[bsrc, B_rstd[k], B_const], writes=[bout])

        def blk(ap3, c):
            return lambda b: ap3[:, c, b * 512:(b + 1) * 512]

        def load_xT(x_dram):
            with contextlib.ExitStack() as sc:
                xin = [sc.enter_context(sbt("xin%d" % i, [128, D], F32)) for i in range(2)]
                B_xin = [Buf("xin0"), Buf("xin1")]
                for tc in range(T // 128):
                    j = tc % 2
                    S.dma("sp", xin[j][:], x_dram[tc * 128:(tc + 1) * 128, :], writes=[B_xin[j]])
                    for q in range(4):
                        pst, B_pst = next_ps()
                        for r in range(4):
                            dc = q * 4 + r
                            S.op("pe", lambda e, pst=pst, r=r, dc=dc, j=j: e.transpose(
                                pst[:, r * 128:(r + 1) * 128], xin[j][:, dc * 128:(dc + 1) * 128], ident[:]),
                                reads=[B_xin[j], B_const], writes=[B_pst], signal=(r == 3))
                        S.op("act", lambda e, pst=pst, q=q, tc=tc: e.activation(
                            out=xT[:, q * 4:(q + 1) * 4, tc * 128:(tc + 1) * 128],
                            in_=pst[:, :].rearrange("p (r n) -> p r n", r=4), func=AF.Copy),
                            reads=[B_pst], writes=[B_xT])
                S.barrier()

        def ffn(gcol, wg, wu, wd):
            with contextlib.ExitStack() as sc:
                xnT = sc.enter_context(sbt("xnT", [128, NDC, T], BF16)); B_xn = Buf("xn")
                H = [sc.enter_context(sbt("H%d" % i, [128, 2, T], BF16)) for i in range(2)]
                B_H = [Buf("H0"), Buf("H1")]
                sg = [sc.enter_context(sbt("sg%d" % i, [128, 512], F32)) for i in range(2)]
                B_sg = [Buf("sg0"), Buf("sg1")]
                open_ring(sc, 6)
                rmsnorm([(blk(xT, c), 128, B_xT) for c in range(NDC)], D,
                        [col(gcol + c) for c in range(NDC)],
                        [(blk(xnT, c), B_xn) for c in range(NDC)])
                pend = []

                def issue_gu(k):
                    a, v = wpiece_cols(wg, k * 256, 256, 16)
                    g_ = load_w(a, v)
                    a, v = wpiece_cols(wu, k * 256, 256, 16)
                    u_ = load_w(a, v)
                    pend.append([g_, u_, None])

                def issue_d(k):
                    a, v = wpiece_rows(wd, k * 256, 2, D)
                    pend[k][2] = load_w(a, v)
                issue_gu(0)
                issue_d(0)
                sgi = 0

                def gate_up(k):
                    nonlocal sgi
                    yield_steps = True
                    (wgs, B_wg), (wus, B_wu), _ = pend[k]
                    h = H[k % 2]
                    B_h = B_H[k % 2]
                    for fc in range(2):
                        for hb in range(2):
                            pg, B_pg = next_ps()
                            pu, B_pu = next_ps()
                            for dc in range(NDC):
                                mm(pg[:, :], wgs[:, dc, fc * 128:(fc + 1) * 128], xnT[:, dc, hb * 512:(hb + 1) * 512],
                                   dc == 0, dc == NDC - 1, [B_wg, B_xn], B_pg, signal=(dc == NDC - 1))
                            for dc in range(NDC):
                                mm(pu[:, :], wus[:, dc, fc * 128:(fc + 1) * 128], xnT[:, dc, hb * 512:(hb + 1) * 512],
                                   dc == 0, dc == NDC - 1, [B_wu, B_xn], B_pu, signal=(dc == NDC - 1))
                            j = sgi % 2
                            sgi += 1
                            S.op("act", lambda e: e.activation(out=sg[j][:], in_=pg[:, :], func=AF.Silu),
                                 reads=[B_pg], writes=[B_sg[j]])
                            S.op("dve", lambda e: e.tensor_tensor(
                                out=h[:, fc, hb * 512:(hb + 1) * 512], in0=sg[j][:], in1=pu[:, :], op=ALU.mult),
                                reads=[B_sg[j], B_pu], writes=[B_h])
                            yield

                def down(k):
                    _, _, (wds, B_wd) = pend[k]
                    h = H[k % 2]
                    B_h = B_H[k % 2]
                    for dc in range(NDC):
                        for hb in range(2):
                            py, B_py = next_ps()
                            for fc in range(2):
                                mm(py[:, :], wds[:, fc, dc * 128:(dc + 1) * 128], h[:, fc, hb * 512:(hb + 1) * 512],
                                   fc == 0, fc == 1, [B_wd, B_h], B_py, signal=(fc == 1))
                            S.op("dve", lambda e: e.scalar_tensor_tensor(
                                out=xT[:, dc, hb * 512:(hb + 1) * 512], in0=py[:, :], scalar=0.5,
                                in1=xT[:, dc, hb * 512:(hb + 1) * 512], op0=ALU.mult, op1=ALU.add),
                                reads=[B_py, B_xT], writes=[B_xT])
                            yield

                ctr["psn"] = 8
                for k in range(NG):
                    if k + 1 < NG:
                        issue_gu(k + 1)
                    gd = down(k - 1) if k > 0 else iter(())
                    for _ in gate_up(k):
                        for _i in range(8):
                            next(gd, None)
                    for _ in gd:
                        pass
                    if k + 1 < NG:
                        issue_d(k + 1)
                for _ in down(NG - 1):
                    pass
                ctr["psn"] = 6
                S.barrier()

        MAGIC = 12582912.0

        def sincos(dst_sin, dst_cos, ang, tmp, R):
            S.op("dve", lambda e: e.tensor_scalar(out=tmp, in0=ang, scalar1=1.0 / TWO_PI, scalar2=MAGIC, op0=ALU.mult, op1=ALU.add), reads=R, writes=R)
            S.op("dve", lambda e: e.tensor_scalar(out=tmp, in0=tmp, scalar1=MAGIC, scalar2=-TWO_PI, op0=ALU.subtract, op1=ALU.mult), reads=R, writes=R)
            S.op("dve", lambda e: e.tensor_tensor(out=tmp, in0=tmp, in1=ang, op=ALU.add), reads=R, writes=R)
            S.op("dve", lambda e: e.tensor_scalar(out=tmp, in0=tmp, scalar1=PI_SAFE, scalar2=-PI_SAFE, op0=ALU.min, op1=ALU.max), reads=R, writes=R)
            S.op("act", lambda e: e.activation(out=dst_sin, in_=tmp, func=AF.Sin), reads=R, writes=R)
            S.op("dve", lambda e: e.scalar_tensor_tensor(out=tmp, in0=tmp, scalar=-1.0, in1=tmp, op0=ALU.mult, op1=ALU.max), reads=R, writes=R)
            S.op("act", lambda e: e.activation(out=dst_cos, in_=tmp, func=AF.Sin, scale=-1.0, bias=halfpi[:, 0:1]), reads=R, writes=R)

        def ssm_prep():
            ls, ar, ai = ssmp[:, 0, :], ssmp[:, 1, :], ssmp[:, 2, :]
            tm = lambda i: stmp[:, i, :]
            R = [B_const]

            def dv(fn):
                S.op("dve", fn, reads=R, writes=R)

            def ac(fn):
                S.op("act", fn, reads=R, writes=R)
            ac(lambda e: e.activation(out=tm(0), in_=ls, func=AF.Exp))
            dv(lambda e: e.tensor_tensor(out=tm(1), in0=ar, in1=tm(0), op=ALU.mult))
            dv(lambda e: e.tensor_tensor(out=sth[:], in0=ai, in1=tm(0), op=ALU.mult))
            ac(lambda e: e.activation(out=srd[:], in_=tm(1), func=AF.Exp))
            dv(lambda e: e.tensor_scalar(out=sthn[:], in0=sth[:], scalar1=1.0 / TWO_PI, scalar2=None, op0=ALU.mult))
            sincos(tm(2), tm(3), sth[:], tm(7), R)
            dv(lambda e: e.tensor_scalar(out=tm(6), in0=sth[:], scalar1=float(T), scalar2=None, op0=ALU.mult))
            sincos(ssT[:], scT[:], tm(6), tm(7), R)
            dv(lambda e: e.tensor_tensor(out=tm(4), in0=srd[:], in1=tm(3), op=ALU.mult))
            dv(lambda e: e.tensor_tensor(out=tm(5), in0=srd[:], in1=tm(2), op=ALU.mult))
            dv(lambda e: e.tensor_scalar(out=tm(4), in0=tm(4), scalar1=-1.0, scalar2=None, op0=ALU.add))
            dv(lambda e: e.tensor_tensor(out=tm(6), in0=ar, in1=ar, op=ALU.mult))
            dv(lambda e: e.tensor_tensor(out=tm(7), in0=ai, in1=ai, op=ALU.mult))
            dv(lambda e: e.tensor_tensor(out=tm(6), in0=tm(6), in1=tm(7), op=ALU.add))
            dv(lambda e: e.reciprocal(out=tm(6), in_=tm(6)))
            dv(lambda e: e.tensor_tensor(out=tm(0), in0=tm(4), in1=ar, op=ALU.mult))
            dv(lambda e: e.tensor_tensor(out=tm(1), in0=tm(5), in1=ai, op=ALU.mult))
            dv(lambda e: e.tensor_tensor(out=tm(0), in0=tm(0), in1=tm(1), op=ALU.add))
            dv(lambda e: e.tensor_tensor(out=sfr[:], in0=tm(0), in1=tm(6), op=ALU.mult))
            dv(lambda e: e.tensor_tensor(out=tm(0), in0=tm(5), in1=ar, op=ALU.mult))
            dv(lambda e: e.tensor_tensor(out=tm(1), in0=tm(4), in1=ai, op=ALU.mult))
            dv(lambda e: e.tensor_tensor(out=tm(0), in0=tm(0), in1=tm(1), op=ALU.subtract))
            dv(lambda e: e.tensor_tensor(out=sfi[:], in0=tm(0), in1=tm(6), op=ALU.mult))

        def rotate_state():
            R = [B_const, B_state]
            re, im = state[:, :, 0], state[:, :, 1]
            tm = lambda i: stmp[:, i, :]

            def dv(fn):
                S.op("dve", fn, reads=R, writes=R)
            dv(lambda e: e.tensor_tensor(out=tm(0), in0=scT[:], in1=re, op=ALU.mult))
            dv(lambda e: e.tensor_tensor(out=tm(1), in0=ssT[:], in1=im, op=ALU.mult))
            dv(lambda e: e.tensor_tensor(out=tm(2), in0=ssT[:], in1=re, op=ALU.mult))
            dv(lambda e: e.tensor_tensor(out=tm(3), in0=scT[:], in1=im, op=ALU.mult))
            dv(lambda e: e.tensor_tensor(out=re, in0=tm(0), in1=tm(1), op=ALU.subtract))
            dv(lambda e: e.tensor_tensor(out=im, in0=tm(2), in1=tm(3), op=ALU.add))

        def build_Bl(sc, Bl, B_Bl):
            bb = [sc.enter_context(sbt("bb%d_%d" % (i, ctr["ring"]), [128, 2, 128], F32)) for i in range(2)]
            B_bb = [Buf("bb0"), Buf("bb1")]
            bo = [sc.enter_context(sbt("bo%d_%d" % (i, ctr["ring"]), [128, 2, 128], F32)) for i in range(2)]
            B_bo = [Buf("bo0"), Buf("bo1")]
            for ti in range(32):
                j = ti % 2
                S.dma("sp", bb[j][:], bblk_d[:, ti, :, :], writes=[B_bb[j]])
                fr, fi = sfr[:, ti:ti + 1], sfi[:, ti:ti + 1]
                S.op("dve", lambda e, j=j, fi=fi: e.tensor_scalar(out=bo[j][:, 0, :], in0=bb[j][:, 1, :], scalar1=fi, scalar2=-1.0, op0=ALU.mult, op1=ALU.mult),
                     reads=[B_bb[j], B_const], writes=[B_bo[j]])
                S.op("dve", lambda e, j=j, fr=fr: e.scalar_tensor_tensor(out=bo[j][:, 0, :], in0=bb[j][:, 0, :], scalar=fr, in1=bo[j][:, 0, :], op0=ALU.mult, op1=ALU.add),
                     reads=[B_bb[j], B_const, B_bo[j]], writes=[B_bo[j]])
                S.op("dve", lambda e, j=j, fi=fi: e.tensor_scalar(out=bo[j][:, 1, :], in0=bb[j][:, 0, :], scalar1=fi, scalar2=None, op0=ALU.mult),
                     reads=[B_bb[j], B_const, B_bo[j]], writes=[B_bo[j]])
                S.op("dve", lambda e, j=j, fr=fr: e.scalar_tensor_tensor(out=bo[j][:, 1, :], in0=bb[j][:, 1, :], scalar=fr, in1=bo[j][:, 1, :], op0=ALU.mult, op1=ALU.add),
                     reads=[B_bb[j], B_const, B_bo[j]], writes=[B_bo[j]])
                pst, B_pst = next_ps()
                for ri in range(2):
                    S.op("pe", lambda e, pst=pst, ri=ri, j=j: e.transpose(pst[:, ri * 128:(ri + 1) * 128], bo[j][:, ri, :], ident[:]),
                         reads=[B_bo[j], B_const], writes=[B_pst], signal=(ri == 1))
                S.op("act", lambda e, pst=pst, ti=ti: e.activation(out=Bl[:, ti, :, :], in_=pst[:, 0:256].rearrange("p (r n) -> p r n", r=2), func=AF.Copy),
                     reads=[B_pst], writes=[B_Bl])

        def mixer(own, pos_d):
            t0 = T if own else 0
            sfx = "o" if own else "p"
            with contextlib.ExitStack() as sc:
                def sbl(name, shape, dt=F32):
                    return sc.enter_context(sbt(name + sfx, list(shape), dt))
                ycat = sbl("ycat", [128, 16, T], BF16); B_ycat = Buf("ycat")
                qnT = sbl("qnT", [128, 4, T], BF16) if own else None
                B_qn = Buf("qn")
                s_u = sc.enter_context(contextlib.ExitStack())
                uT = s_u.enter_context(sbt("uT" + sfx, [128, 8, T], BF16)); B_u = Buf("u")

                def rope_tables(s2):
                    ropc = s2.enter_context(sbt("ropc%d" % ctr["ring"], [64, T], F32))
                    rops = s2.enter_context(sbt("rops%d" % ctr["ring"], [64, T], F32))
                    B_rope = Buf("rope")
                    ctr["ring"] += 1
                    posi = s2.enter_context(sbt("posi%d" % ctr["ring"], [64, T], I32))
                    S.dma("sp", posi[:], pos_d, writes=[B_rope])
                    S.op("dve", lambda e: e.tensor_copy(out=rops[:], in_=posi[:]), reads=[B_rope], writes=[B_rope])
                    S.op("dve", lambda e: e.tensor_scalar(out=ropc[:], in0=rops[:], scalar1=col(C_INVF, 64), scalar2=None, op0=ALU.mult),
                         reads=[B_rope, B_const], writes=[B_rope])
                    R = [B_rope, B_const]
                    S.op("dve", lambda e: e.tensor_scalar(out=rops[:], in0=ropc[:], scalar1=1.0 / TWO_PI, scalar2=MAGIC, op0=ALU.mult, op1=ALU.add), reads=R, writes=R)
                    S.op("dve", lambda e: e.tensor_scalar(out=rops[:], in0=rops[:], scalar1=MAGIC, scalar2=-TWO_PI, op0=ALU.subtract, op1=ALU.mult), reads=R, writes=R)
                    S.op("dve", lambda e: e.tensor_tensor(out=rops[:], in0=rops[:], in1=ropc[:], op=ALU.add), reads=R, writes=R)
                    S.op("dve", lambda e: e.tensor_scalar(out=rops[:], in0=rops[:], scalar1=PI_SAFE, scalar2=-PI_SAFE, op0=ALU.min, op1=ALU.max), reads=R, writes=R)
                    S.op("dve", lambda e: e.scalar_tensor_tensor(out=ropc[:], in0=rops[:], scalar=-1.0, in1=rops[:], op0=ALU.mult, op1=ALU.max), reads=R, writes=R)
                    S.op("act", lambda e: e.activation(out=rops[:], in_=rops[:], func=AF.Sin), reads=R, writes=R)
                    S.op("act", lambda e: e.activation(out=ropc[:], in_=ropc[:], func=AF.Sin, scale=-1.0, bias=halfpi[0:64, 0:1]), reads=R, writes=R)
                    S.op("dve", lambda e: e.tensor_scalar(out=rops[:], in0=rops[:], scalar1=col(C_SGN, 64), scalar2=None, op0=ALU.mult), reads=R, writes=R)
                    return ropc, rops, B_rope

                with contextlib.ExitStack() as s2:
                    xnT, B_xn = ycat, B_ycat
                    ropc, rops, B_rope = rope_tables(s2)
                    if SUB == 21:
                        S.barrier()
                        return
                    zt = s2.enter_context(sbt("zt" + sfx, [128, 4, T], F32)); B_zt = Buf("zt")
                    ta = s2.enter_context(sbt("kta" + sfx, [64, 512], F32))
                    tb = s2.enter_context(sbt("ktb" + sfx, [64, 512], F32)); B_t = Buf("kt")
                    open_ring(s2, 2)
                    rmsnorm([(blk(xT, c), 128, B_xT) for c in range(NDC)], D,
                            [col(C_GM + c) for c in range(NDC)], [(blk(xnT, c), B_xn) for c in range(NDC)])
                    pieces = [0, 1, 2, 3, 6] + ([4, 5] if own else [])
                    for pc in pieces:
                        a, v = wpiece_cols(win_d, pc * 256, 256, 16)
                        wsl, B_w = load_w(a, v)
                        for sub in range(2):
                            for hb in range(2):
                                pst, B_pst = next_ps()
                                for dc in range(NDC):
                                    mm(pst[:, :], wsl[:, dc, sub * 128:(sub + 1) * 128], xnT[:, dc, hb * 512:(hb + 1) * 512],
                                       dc == 0, dc == NDC - 1, [B_w, B_xn], B_pst, signal=(dc == NDC - 1))
                                sl = slice(hb * 512, (hb + 1) * 512)
                                if pc < 4:
                                    dst, bd = uT[:, pc * 2 + sub, sl], B_u
                                elif pc == 6:
                                    dst, bd = zt[:, sub, sl], B_zt
                                else:
                                    dst, bd = zt[:, (pc - 4) * 2 + sub, sl], B_zt
                                S.op("act", lambda e, dst=dst, pst=pst: e.activation(out=dst, in_=pst[:, :], func=AF.Copy),
                                     reads=[B_pst], writes=[bd])
                        if pc == 6:
                            rmsnorm([(blk(zt, c), 128, B_zt) for c in range(2)], 256,
                                    [col(C_GKA + c) for c in range(2)],
                                    [((lambda b, c=c: kvnT[:, c, t0 + b * 512:t0 + (b + 1) * 512]), B_kvn) for c in range(2)])
                    if SUB == 22:
                        S.barrier()
                        return
                    if own:
                        rmsnorm([(blk(zt, c), 128, B_zt) for c in range(4)], 512,
                                [col(C_GQA + c) for c in range(4)], [(blk(qnT, c), B_qn) for c in range(4)])
                    if SUB == 23:
                        S.barrier()
                        return
                    a, v = wpiece_cols(win_d, 1792, 128, 16)
                    wsl, B_w = load_w(a, v)
                    for hb in range(2):
                        sl = slice(hb * 512, (hb + 1) * 512)
                        gsl = slice(t0 + hb * 512, t0 + (hb + 1) * 512)
                        pk, B_pk = next_ps()
                        pw, B_pw = next_ps()
                        for dc in range(NDC):
                            mm(pk[0:64, :], wsl[:, dc, 0:64], xnT[:, dc, sl], dc == 0, dc == NDC - 1, [B_w, B_xn], B_pk)
                        for dc in range(NDC):
                            mm(pw[0:64, :], wsl[:, dc, 64:128], xnT[:, dc, sl], dc == 0, dc == NDC - 1, [B_w, B_xn], B_pw)
                        if SUB == 24:
                            continue
                        S.op("act", lambda e, pk=pk, gsl=gsl: e.activation(out=sqkpe[:, gsl], in_=pk[0:64, :], func=AF.Square),
                             reads=[B_pk], writes=[B_sqk])
                        if SUB == 25:
                            continue
                        S.op("dve", lambda e, pk=pk, sl=sl: e.scalar_tensor_tensor(out=ta[:], in0=pk[0:64, :], scalar=col(C_GKR, 64), in1=ropc[:, sl], op0=ALU.mult, op1=ALU.mult),
                             reads=[B_pk, B_rope, B_const], writes=[B_t])
                        if SUB == 26:
                            continue
                        S.op("dve", lambda e, pw=pw, sl=sl: e.scalar_tensor_tensor(out=tb[:], in0=pw[0:64, :], scalar=col(C_GKS, 64), in1=rops[:, sl], op0=ALU.mult, op1=ALU.mult),
                             reads=[B_pw, B_rope, B_const, B_t], writes=[B_t])
                        if SUB == 27:
                            continue
                        S.op("dve", lambda e, gsl=gsl: e.tensor_tensor(out=krT[:, gsl], in0=ta[:], in1=tb[:], op=ALU.add),
                             reads=[B_t], writes=[B_kr])
                    S.barrier()
                if SUB == 2:
                    return
                with contextlib.ExitStack() as s2:
                    def s2b(name, shape, dt=F32):
                        return s2.enter_context(sbt(name + sfx, list(shape), dt))
                    tq = s2b("tq", [128, T]); B_tq = Buf("tq")
                    t1, t2 = tq[:, 0:512], tq[:, 512:1024]
                    s_loop = s2.enter_context(contextlib.ExitStack())

                    def s2b(name, shape, dt=F32):
                        return s_loop.enter_context(sbt(name + sfx, list(shape), dt))
                    Bl = s2b("Bl", [128, 32, 2, 128], BF16); B_Bl = Buf("Bl")
                    ncs2 = [s2b("ncs%d" % i, [128, T]) for i in range(2)]
                    nsn2 = [s2b("nsn%d" % i, [128, T]) for i in range(2)]
                    B_tab2 = [Buf("tab0"), Buf("tab1")]
                    rdec = s2b("rdec", [128, T]); B_rdec = Buf("rdec")
                    cre = s2b("cre", [128, T]); cim = s2b("cim", [128, T]); B_c = Buf("c")
                    sre = s2b("sre", [128, 512], BF16); sim = s2b("sim", [128, 512], BF16); B_s = Buf("s")
                    cl = [s2b("cl%d" % i, [128, 2, 128], BF16) for i in range(2)]
                    B_cl = [Buf("cl0"), Buf("cl1")]
                    Ddiag = s2b("Ddiag", [128, 8, 128], BF16)
                    with contextlib.ExitStack() as s3:
                        build_Bl(s3, Bl, B_Bl)
                        S.barrier()
                    if own:
                        for c in range(8):
                            S.op("dve", lambda e, c=c: e.tensor_scalar(out=Ddiag[:, c, :], in0=ident[:], scalar1=col(C_DSK + c), scalar2=None, op0=ALU.mult),
                                 reads=[B_const], writes=[B_const])
                    def gen_tables(ti):
                        ncs, nsn, B_tab = ncs2[ti % 2], nsn2[ti % 2], B_tab2[ti % 2]
                        th = sth[:, ti:ti + 1]
                        thn = sthn[:, ti:ti + 1]
                        S.op("dve", lambda e: e.tensor_scalar(out=nsn[:], in0=tidx[:], scalar1=thn, scalar2=MAGIC, op0=ALU.mult, op1=ALU.add),
                             reads=[B_const], writes=[B_tab])
                        S.op("dve", lambda e: e.tensor_scalar(out=nsn[:], in0=nsn[:], scalar1=MAGIC, scalar2=-TWO_PI, op0=ALU.subtract, op1=ALU.mult),
                             reads=[B_tab], writes=[B_tab])
                        S.op("dve", lambda e: e.scalar_tensor_tensor(out=nsn[:], in0=tidx[:], scalar=th, in1=nsn[:], op0=ALU.mult, op1=ALU.add),
                             reads=[B_tab, B_const], writes=[B_tab])
                        S.op("dve", lambda e: e.tensor_scalar(out=nsn[:], in0=nsn[:], scalar1=PI_SAFE, scalar2=-PI_SAFE, op0=ALU.min, op1=ALU.max),
                             reads=[B_tab], writes=[B_tab])
                        S.op("dve", lambda e: e.scalar_tensor_tensor(out=ncs[:], in0=nsn[:], scalar=-1.0, in1=nsn[:], op0=ALU.mult, op1=ALU.max), reads=[B_tab], writes=[B_tab])
                        S.op("act", lambda e: e.activation(out=nsn[:], in_=nsn[:], func=AF.Sin), reads=[B_tab], writes=[B_tab])
                        S.op("act", lambda e: e.activation(out=ncs[:], in_=ncs[:], func=AF.Sin, scale=-1.0, bias=halfpi[:, 0:1]), reads=[B_tab, B_const], writes=[B_tab])

                    for uc in range(8):
                        if own:
                            py = [long_ps(0), long_ps(1)]
                        for tl in range(4):
                            ti = uc * 4 + tl
                            if ti == 0:
                                gen_tables(0)
                            if ti + 1 < 32:
                                gen_tables(ti + 1)
                            ncs, nsn, B_tab = ncs2[ti % 2], nsn2[ti % 2], B_tab2[ti % 2]
                            S.op("act", lambda e, ti=ti: e.activation(out=rdec[:], in_=tidx[:], func=AF.Identity, scale=0.0, bias=srd[:, ti:ti + 1]),
                                 reads=[B_const], writes=[B_rdec])
                            if own:
                                j = ti % 2
                                S.dma("pool", cl[j][:], cblk_d[:, ti, :, :], writes=[B_cl[j]])
                            for hb in range(2):
                                sl = slice(hb * 512, (hb + 1) * 512)
                                pr, B_pr = next_ps()
                                pi_, B_pi = next_ps()
                                mm(pr[:, :], Bl[:, ti, 0, :], uT[:, uc, sl], True, True, [B_Bl, B_u], B_pr)
                                mm(pi_[:, :], Bl[:, ti, 1, :], uT[:, uc, sl], True, True, [B_Bl, B_u], B_pi)
                                S.op("dve", lambda e, pr=pr, sl=sl: e.tensor_tensor(out=t1, in0=ncs[:, sl], in1=pr[:, :], op=ALU.mult), reads=[B_tab, B_pr], writes=[B_tq])
                                S.op("dve", lambda e, pi_=pi_, sl=sl: e.tensor_tensor(out=t2, in0=nsn[:, sl], in1=pi_[:, :], op=ALU.mult), reads=[B_tab, B_pi], writes=[B_tq])
                                S.op("dve", lambda e, sl=sl: e.tensor_tensor(out=cre[:, sl], in0=t1, in1=t2, op=ALU.add), reads=[B_tq], writes=[B_c])
                                S.op("dve", lambda e, pi_=pi_, sl=sl: e.tensor_tensor(out=t1, in0=ncs[:, sl], in1=pi_[:, :], op=ALU.mult), reads=[B_tab, B_pi, B_c], writes=[B_tq])
                                S.op("dve", lambda e, pr=pr, sl=sl: e.tensor_tensor(out=t2, in0=nsn[:, sl], in1=pr[:, :], op=ALU.mult), reads=[B_tab, B_pr, B_c], writes=[B_tq])
                                S.op("dve", lambda e, sl=sl: e.tensor_tensor(out=cim[:, sl], in0=t1, in1=t2, op=ALU.subtract), reads=[B_tq], writes=[B_c])
                            S.op("dve", lambda e, ti=ti: e.tensor_tensor_scan(out=cre[:], data0=rdec[:], data1=cre[:], initial=state[:, ti, 0:1], op0=ALU.mult, op1=ALU.add),
                                 reads=[B_c, B_rdec, B_state], writes=[B_c])
                            S.op("dve", lambda e, ti=ti: e.tensor_tensor_scan(out=cim[:], data0=rdec[:], data1=cim[:], initial=state[:, ti, 1:2], op0=ALU.mult, op1=ALU.add),
                                 reads=[B_c, B_rdec, B_state], writes=[B_c])
                            if not own:
                                S.op("dve", lambda e, ti=ti: e.tensor_copy(out=state[:, ti, 0:1], in_=cre[:, T - 1:T]), reads=[B_c], writes=[B_state])
                                S.op("dve", lambda e, ti=ti: e.tensor_copy(out=state[:, ti, 1:2], in_=cim[:, T - 1:T]), reads=[B_c], writes=[B_state])
                                continue
                            for hb in range(2):
                                sl = slice(hb * 512, (hb + 1) * 512)
                                S.op("dve", lambda e, sl=sl: e.tensor_tensor(out=t1, in0=ncs[:, sl], in1=cre[:, sl], op=ALU.mult), reads=[B_tab, B_c, B_s], writes=[B_tq])
                                S.op("dve", lambda e, sl=sl: e.tensor_tensor(out=t2, in0=nsn[:, sl], in1=cim[:, sl], op=ALU.mult), reads=[B_tab, B_c], writes=[B_tq])
                                S.op("dve", lambda e: e.tensor_tensor(out=sre[:], in0=t1, in1=t2, op=ALU.subtract), reads=[B_tq], writes=[B_s])
                                S.op("dve", lambda e, sl=sl: e.tensor_tensor(out=t1, in0=nsn[:, sl], in1=cre[:, sl], op=ALU.mult), reads=[B_tab, B_c, B_s], writes=[B_tq])
                                S.op("dve", lambda e, sl=sl: e.tensor_tensor(out=t2, in0=ncs[:, sl], in1=cim[:, sl], op=ALU.mult), reads=[B_tab, B_c, B_s], writes=[B_tq])
                                S.op("dve", lambda e: e.scalar_tensor_tensor(out=sim[:], in0=t1, scalar=-1.0, in1=t2, op0=ALU.mult, op1=ALU.subtract), reads=[B_tq], writes=[B_s])
                                p_, B_p = py[hb]
                                mm(p_[:, :], cl[j][:, 0, :], sre[:], tl == 0, False, [B_cl[j], B_s], B_p, signal=False)
                                mm(p_[:, :], cl[j][:, 1, :], sim[:], False, False, [B_cl[j], B_s], B_p, signal=True)
                        if own:
                            for hb in range(2):
                                sl = slice(hb * 512, (hb + 1) * 512)
                                p_, B_p = py[hb]
                                mm(p_[:, :], Ddiag[:, uc, :], uT[:, uc, sl], False, True, [B_const, B_u], B_p)
                                S.op("act", lambda e, p_=p_: e.activation(out=t1, in_=p_[:, :], func=AF.Copy), reads=[B_p], writes=[B_tq])
                                S.op("dve", lambda e: e.tensor_tensor(out=t2, in0=t1, in1=t1, op=ALU.mult), reads=[B_tq], writes=[B_tq])
                                S.op("dve", lambda e: e.tensor_scalar(out=t2, in0=t2, scalar1=0.044715, scalar2=1.0, op0=ALU.mult, op1=ALU.add), reads=[B_tq], writes=[B_tq])
                                S.op("dve", lambda e: e.tensor_tensor(out=t2, in0=t2, in1=t1, op=ALU.mult), reads=[B_tq], writes=[B_tq])
                                S.op("act", lambda e: e.activation(out=t2, in_=t2, func=AF.Sigmoid, scale=2.0 * math.sqrt(2.0 / math.pi)), reads=[B_tq], writes=[B_tq])
                                S.op("dve", lambda e, uc=uc, sl=sl: e.tensor_tensor(out=ycat[:, 8 + uc, sl], in0=t2, in1=t1, op=ALU.mult), reads=[B_tq], writes=[B_ycat])
                    S.barrier()
                    s_loop.close()
                    if own:
                        open_ring(s2, 2)
                        pz = [long_ps(0), long_ps(1)]
                        for pc in range(4):
                            a, v = wpiece_cols(wglu_d, pc * 256, 256, 8)
                            wsl, B_w = load_w(a, v)
                            for sub in range(2):
                                oc = pc * 2 + sub
                                for hb in range(2):
                                    sl = slice(hb * 512, (hb + 1) * 512)
                                    pst, B_pst = next_ps()
                                    for c in range(8):
                                        mm(pst[:, :], wsl[:, c, sub * 128:(sub + 1) * 128], ycat[:, 8 + c, sl], c == 0, c == 7, [B_w, B_ycat], B_pst)
                                    S.op("act", lambda e, pst=pst, oc=oc: e.activation(out=t1, in_=pst[:, :], func=AF.Sigmoid, bias=col(C_BGLU + oc)),
                                         reads=[B_pst, B_const], writes=[B_tq])
                                    S.op("dve", lambda e, sl=sl, oc=oc: e.tensor_tensor(out=t1, in0=t1, in1=ycat[:, 8 + oc, sl], op=ALU.mult),
                                         reads=[B_tq, B_ycat], writes=[B_tq])
                                    S.op("act", lambda e, sl=sl, oc=oc: e.activation(out=ycat[:, oc, sl], in_=t1, func=AF.Copy), reads=[B_tq], writes=[B_ycat])
                                    jq = ctr["sq"] % 2; ctr["sq"] += 1
                                    S.op("act", lambda e, jq=jq: e.activation(out=sqt[jq][:], in_=t1, func=AF.Square), reads=[B_tq], writes=[B_sqt[jq]])
                                    p_, B_p = pz[hb]
                                    mm(p_[:, :], onesb[:, :], sqt[jq][:], oc == 0, oc == 7, [B_sqt[jq], B_const], B_p, signal=True)
                        for hb in range(2):
                            sl = slice(hb * 512, (hb + 1) * 512)
                            p_, B_p = pz[hb]
                            S.op("act", lambda e, p_=p_: e.activation(out=t1, in_=p_[:, :], func=AF.Ln, scale=1.0 / 1024, bias=epsc[:, 0:1]), reads=[B_p, B_const], writes=[B_tq])
                            S.op("act", lambda e: e.activation(out=t1, in_=t1, func=AF.Exp, scale=-0.5), reads=[B_tq], writes=[B_tq])
                            for c in range(8):
                                S.op("dve", lambda e, c=c, sl=sl: e.scalar_tensor_tensor(out=ycat[:, c, sl], in0=ycat[:, c, sl], scalar=col(C_GSO + c), in1=t1, op0=ALU.mult, op1=ALU.mult),
                                     reads=[B_tq, B_ycat, B_const], writes=[B_ycat])
                    S.barrier()
                s_u.close()
                if not own:
                    rotate_state()
                    S.barrier()
                    return
                if SUB == 3:
                    return
                with contextlib.ExitStack() as s2:
                    def s2b(name, shape, dt=F32):
                        return s2.enter_context(sbt(name, list(shape), dt))
                    ropc, rops, B_rope = rope_tables(s2)
                    ssqA = s2b("ssqA", [128, T]); B_ssqA = Buf("ssqA")
                    knT2 = [s2b("knT%d" % i, [128, 2 * T], BF16) for i in range(2)]
                    kpT2 = [s2b("kpT%d" % i, [64, 2 * T], BF16) for i in range(2)]
                    Vt2 = [s2b("Vt%d" % i, [128, 16, 128], BF16) for i in range(2)]
                    qnh2 = [s2b("qnh%d" % i, [128, T], BF16) for i in range(2)]
                    qrh2 = [s2b("qrh%d" % i, [64, T], BF16) for i in range(2)]
                    B_k2 = [Buf("k0"), Buf("k1")]; B_V2 = [Buf("V0"), Buf("V1")]; B_q2 = [Buf("q0"), Buf("q1")]
                    PT = [s2b("PT%d" % i, [128, 512], BF16) for i in range(2)]
                    B_PT = [Buf("PT0"), Buf("PT1")]
                    ra = s2b("ra", [64, 512]); rb = s2b("rb", [64, 512]); B_r = Buf("r")
                    rsum = s2b("rsum", [128, 512]); otmp = s2b("otmp", [128, 512]); B_rsum = Buf("rsum")
                    S.op("dve", lambda e: e.memset(ssqA[:], 0.0), writes=[B_ssqA])
                    wset = []
                    for i in range(2):
                        wset.append(dict(
                            qn=s2b("hwqn%d" % i, [128, 4, 128], BF16), qr=s2b("hwqr%d" % i, [128, 4, 64], BF16),
                            qs=s2b("hwqs%d" % i, [128, 4, 64], BF16), kk=s2b("hwkk%d" % i, [128, 2, 128], BF16),
                            kv=s2b("hwkv%d" % i, [128, 2, 128], BF16), B=Buf("hw%d" % i)))
                    pti = [0]

                    def prep_gen(h):
                        sl_ = h % 2
                        W = wset[sl_]
                        B_W = W["B"]
                        knT, kpT, Vt, qnh, qrh = knT2[sl_], kpT2[sl_], Vt2[sl_], qnh2[sl_], qrh2[sl_]
                        B_k, B_V, B_q = B_k2[sl_], B_V2[sl_], B_q2[sl_]
                        for nm, wd_, wcols in (("qn", wqn_d, 128), ("qr", wqr_d, 64), ("qs", wqs_d, 64), ("kk", wkk_d, 128), ("kv", wkv_d, 128)):
                            S.dma("pool", W[nm][:], wd_[:, h * wcols:(h + 1) * wcols].rearrange("(c p) n -> p c n", p=128), writes=[B_W])
                        yield
                        for kb in range(4):
                            sl = slice(kb * 512, (kb + 1) * 512)
                            pk, B_pk = next_ps()
                            for c in range(2):
                                mm(pk[:, :], W["kk"][:, c, :], kvnT[:, c, sl], c == 0, c == 1, [B_W, B_kvn], B_pk)
                            pm, B_pm = next_ps()
                            j = ctr["sq"] % 2; ctr["sq"] += 1
                            S.op("act", lambda e: e.activation(out=sqt[j][:], in_=pk[:, :], func=AF.Square), reads=[B_pk], writes=[B_sqt[j]])
                            mm(pm[:, :], onesb[:, :], sqt[j][:], True, False, [B_sqt[j], B_const], B_pm, signal=True)
                            mm(pm[:, :], onesb[0:64, :], sqkpe[:, sl], False, True, [B_sqk, B_const], B_pm)
                            k_ = ctr["rs"] % 2; ctr["rs"] += 1
                            S.op("act", lambda e: e.activation(out=rstd[k_][:], in_=pm[:, :], func=AF.Ln, scale=1.0 / 192, bias=epsc[:, 0:1]), reads=[B_pm, B_const], writes=[B_rstd[k_]])
                            S.op("act", lambda e: e.activation(out=rstd[k_][:], in_=rstd[k_][:], func=AF.Exp, scale=-0.5), reads=[B_rstd[k_]], writes=[B_rstd[k_]])
                            S.op("dve", lambda e: e.scalar_tensor_tensor(out=knT[:, sl], in0=pk[:, :], scalar=col(C_GKN), in1=rstd[k_][:], op0=ALU.mult, op1=ALU.mult),
                                 reads=[B_pk, B_rstd[k_], B_const], writes=[B_k])
                            S.op("dve", lambda e: e.tensor_tensor(out=kpT[:, sl], in0=krT[:, sl], in1=rstd[k_][0:64, :], op=ALU.mult),
                                 reads=[B_kr, B_rstd[k_]], writes=[B_k])
                            yield
                        for kc in range(16):
                            if kc % 4 == 0:
                                pv, B_pv = next_ps()
                            q4 = kc % 4
                            for c in range(2):
                                mm(pv[:, q4 * 128:(q4 + 1) * 128], kvnT[:, c, kc * 128:(kc + 1) * 128], W["kv"][:, c, :], c == 0, c == 1,
                                   [B_W, B_kvn], B_pv, signal=(c == 1 and q4 == 3))
                            if q4 == 3:
                                S.op("act", lambda e: e.activation(out=Vt[:, kc - 3:kc + 1, :], in_=pv[:, :].rearrange("p (a n) -> p a n", a=4), func=AF.Copy),
                                     reads=[B_pv], writes=[B_V])
                                yield
                        for hb in range(2):
                            sl = slice(hb * 512, (hb + 1) * 512)
                            pq, B_pq = next_ps(); pr_, B_pr = next_ps(); pw, B_pw = next_ps(); pm, B_pm = next_ps()
                            for c in range(4):
                                mm(pq[:, :], W["qn"][:, c, :], qnT[:, c, sl], c == 0, c == 3, [B_W, B_qn], B_pq)
                            for c in range(4):
                                mm(pr_[0:64, :], W["qr"][:, c, :], qnT[:, c, sl], c == 0, c == 3, [B_W, B_qn], B_pr)
                            for c in range(4):
                                mm(pw[0:64, :], W["qs"][:, c, :], qnT[:, c, sl], c == 0, c == 3, [B_W, B_qn], B_pw)
                            j = ctr["sq"] % 2; ctr["sq"] += 1
                            S.op("act", lambda e: e.activation(out=sqt[j][:], in_=pq[:, :], func=AF.Square), reads=[B_pq], writes=[B_sqt[j]])
                            mm(pm[:, :], onesb[:, :], sqt[j][:], True, False, [B_sqt[j], B_const], B_pm, signal=True)
                            j2 = ctr["sq"] % 2; ctr["sq"] += 1
                            S.op("act", lambda e: e.activation(out=sqt[j2][0:64, :], in_=pr_[0:64, :], func=AF.Square), reads=[B_pr], writes=[B_sqt[j2]])
                            mm(pm[:, :], onesb[0:64, :], sqt[j2][0:64, :], False, True, [B_sqt[j2], B_const], B_pm)
                            k_ = ctr["rs"] % 2; ctr["rs"] += 1
                            S.op("act", lambda e: e.activation(out=rstd[k_][:], in_=pm[:, :], func=AF.Ln, scale=1.0 / 192, bias=epsc[:, 0:1]), reads=[B_pm, B_const], writes=[B_rstd[k_]])
                            S.op("act", lambda e: e.activation(out=rstd[k_][:], in_=rstd[k_][:], func=AF.Exp, scale=-0.5), reads=[B_rstd[k_]], writes=[B_rstd[k_]])
                            S.op("dve", lambda e: e.scalar_tensor_tensor(out=qnh[:, sl], in0=pq[:, :], scalar=col(C_GQN), in1=rstd[k_][:], op0=ALU.mult, op1=ALU.mult),
                                 reads=[B_pq, B_rstd[k_], B_const], writes=[B_q])
                            S.op("dve", lambda e: e.scalar_tensor_tensor(out=ra[:], in0=pr_[0:64, :], scalar=col(C_GQR, 64), in1=ropc[:, sl], op0=ALU.mult, op1=ALU.mult),
                                 reads=[B_pr, B_rope, B_const], writes=[B_r])
                            S.op("dve", lambda e: e.scalar_tensor_tensor(out=rb[:], in0=pw[0:64, :], scalar=col(C_GQS, 64), in1=rops[:, sl], op0=ALU.mult, op1=ALU.mult),
                                 reads=[B_pw, B_rope, B_const, B_r], writes=[B_r])
                            S.op("dve", lambda e: e.tensor_tensor(out=ra[:], in0=ra[:], in1=rb[:], op=ALU.add), reads=[B_r], writes=[B_r])
                            S.op("dve", lambda e: e.tensor_tensor(out=qrh[:, sl], in0=ra[:], in1=rstd[k_][0:64, :], op=ALU.mult),
                                 reads=[B_r, B_rstd[k_]], writes=[B_q])
                            yield

                    def score_gen(h):
                        sl_ = h % 2
                        knT, kpT, Vt, qnh, qrh = knT2[sl_], kpT2[sl_], Vt2[sl_], qnh2[sl_], qrh2[sl_]
                        B_k, B_V, B_q = B_k2[sl_], B_V2[sl_], B_q2[sl_]
                        for qb in range(2):
                            po, B_po = long_ps(0)
                            pz_, B_pz = long_ps(1)
                            nkc = 8 + 4 * qb + 4

                            def s_mm(kc):
                                jl = kc - 8 - 4 * qb
                                c0 = max(jl, 0) * 128
                                qs = slice(qb * 512 + c0, (qb + 1) * 512)
                                cs = slice(c0, 512)
                                pS, B_pS = next_ps()
                                mm(pS[:, cs], knT[:, kc * 128:(kc + 1) * 128], qnh[:, qs], True, False, [B_k, B_q], B_pS, signal=False)
                                mm(pS[:, cs], kpT[:, kc * 128:(kc + 1) * 128], qrh[:, qs], False, True, [B_k, B_q], B_pS)
                                return pS, B_pS, jl, c0, cs
                            nxt = s_mm(0)
                            for kc in range(nkc):
                                pS, B_pS, jl, c0, cs = nxt
                                if kc + 1 < nkc:
                                    nxt = s_mm(kc + 1)
                                pj = pti[0] % 2; pti[0] += 1
                                bias = maskb[:, 0:1] if kc < 8 else zeroc[:, 0:1]
                                S.op("act", lambda e: e.activation(out=PT[pj][:, cs], in_=pS[:, cs], func=AF.Exp, scale=SCALE, bias=bias),
                                     reads=[B_pS, B_const], writes=[B_PT[pj]])
                                if jl >= 0:
                                    S.op("dve", lambda e: e.memset(PT[pj][64:128, c0:c0 + 64], 0.0), reads=[B_PT[pj]], writes=[B_PT[pj]])
                                last = kc == nkc - 1
                                mm(po[:, cs], Vt[:, kc, :], PT[pj][:, cs], kc == 0, last, [B_V, B_PT[pj]], B_po, signal=last)
                                mm(pz_[:, cs], onesb[:, :], PT[pj][:, cs], kc == 0, last, [B_const, B_PT[pj]], B_pz, signal=True)
                                yield
                            sl = slice(qb * 512, (qb + 1) * 512)
                            S.op("act", lambda e: e.activation(out=rsum[:], in_=pz_[:, :], func=AF.Ln), reads=[B_pz], writes=[B_rsum])
                            S.op("act", lambda e: e.activation(out=rsum[:], in_=rsum[:], func=AF.Exp, scale=-1.0), reads=[B_rsum], writes=[B_rsum])
                            S.op("dve", lambda e: e.tensor_tensor(out=otmp[:], in0=po[:, :], in1=rsum[:], op=ALU.mult), reads=[B_po, B_rsum], writes=[B_rsum])
                            S.op("act", lambda e: e.activation(out=ycat[:, 8 + h, sl], in_=otmp[:], func=AF.Copy), reads=[B_rsum], writes=[B_ycat])
                            jq = ctr["sq"] % 2; ctr["sq"] += 1
                            S.op("act", lambda e: e.activation(out=sqt[jq][:], in_=otmp[:], func=AF.Square), reads=[B_rsum], writes=[B_sqt[jq]])
                            pa, B_pa = next_ps()
                            mm(pa[:, :], onesb[:, :], sqt[jq][:], True, True, [B_sqt[jq], B_const], B_pa)
                            S.op("dve", lambda e: e.tensor_tensor(out=ssqA[:, sl], in0=pa[:, :], in1=ssqA[:, sl], op=ALU.add), reads=[B_pa, B_ssqA], writes=[B_ssqA])
                            yield

                    for _ in prep_gen(0):
                        pass
                    for h in range(8):
                        g1 = score_gen(h)
                        g2 = prep_gen(h + 1) if h < 7 else iter(())
                        done1 = done2 = False
                        step = 0
                        while not (done1 and done2):
                            if not done1:
                                try:
                                    next(g1)
                                except StopIteration:
                                    done1 = True
                            if not done2 and (step % 2 == 1 or done1):
                                try:
                                    next(g2)
                                except StopIteration:
                                    done2 = True
                            step += 1
                    for hb in range(2):
                        sl = slice(hb * 512, (hb + 1) * 512)
                        S.op("act", lambda e: e.activation(out=rsum[:], in_=ssqA[:, sl], func=AF.Ln, scale=1.0 / 1024, bias=epsc[:, 0:1]), reads=[B_ssqA, B_const], writes=[B_rsum])
                        S.op("act", lambda e: e.activation(out=rsum[:], in_=rsum[:], func=AF.Exp, scale=-0.5), reads=[B_rsum], writes=[B_rsum])
                        for c in range(8):
                            S.op("dve", lambda e: e.scalar_tensor_tensor(out=ycat[:, 8 + c, sl], in0=ycat[:, 8 + c, sl], scalar=col(C_GAO + c), in1=rsum[:], op0=ALU.mult, op1=ALU.mult),
                                 reads=[B_rsum, B_ycat, B_const], writes=[B_ycat])
                    S.barrier()
                if SUB == 4:
                    return
                with contextlib.ExitStack() as s2:
                    open_ring(s2, 2)
                    for pc in range(8):
                        a, v = wpiece_cols(wout_d, pc * 256, 256, 16)
                        wsl, B_w = load_w(a, v)
                        for sub in range(2):
                            dc = pc * 2 + sub
                            for hb in range(2):
                                sl = slice(hb * 512, (hb + 1) * 512)
                                pst, B_pst = next_ps()
                                for c in range(16):
                                    mm(pst[:, :], wsl[:, c, sub * 128:(sub + 1) * 128], ycat[:, c, sl], c == 0, c == 15, [B_w, B_ycat], B_pst, signal=(c == 15))
                                S.op("dve", lambda e, pst=pst, dc=dc, sl=sl: e.tensor_tensor(out=xT[:, dc, sl], in0=pst[:, :], in1=xT[:, dc, sl], op=ALU.add),
                                     reads=[B_pst, B_xT], writes=[B_xT])
                    S.barrier()

        def store_out(do_norm):
            with contextlib.ExitStack() as sc:
                xo = [sc.enter_context(sbt("xo%d" % i, [128, D], F32)) for i in range(2)]
                B_xo = [Buf("xo0"), Buf("xo1")]
                gf = sc.enter_context(sbt("gf", [128, D], F32)); B_gf = Buf("gf")
                ssq = sc.enter_context(sbt("ssq", [128, 8]))
                junk = sc.enter_context(sbt("junk", [128, D], F32)); B_j = Buf("junk")
                B_ssq = Buf("ssq")
                S.dma("sp", gf[:], gfin_d, writes=[B_gf])
                S.op("dve", lambda e: e.memset(ssq[:], 0.0), writes=[B_ssq])
                for tc in range(T // 128):
                    j = tc % 2
                    for q in range(4):
                        pst, B_pst = next_ps()
                        for r in range(4):
                            dc = q * 4 + r
                            S.op("pe", lambda e, pst=pst, r=r, dc=dc, tc=tc: e.transpose(
                                pst[:, r * 128:(r + 1) * 128], xT[:, dc, tc * 128:(tc + 1) * 128], ident[:]),
                                reads=[B_xT, B_const], writes=[B_pst], signal=(r == 3))
                        S.op("act", lambda e, pst=pst, q=q, j=j: e.activation(out=xo[j][:, q * 512:(q + 1) * 512], in_=pst[:, :], func=AF.Copy),
                             reads=[B_pst], writes=[B_xo[j]])
                    if do_norm:
                        S.op("act", lambda e, j=j, tc=tc: e.activation(out=junk[:], in_=xo[j][:], func=AF.Square, accum_out=ssq[:, tc:tc + 1]),
                             reads=[B_xo[j], B_ssq], writes=[B_j, B_ssq])
                        S.op("act", lambda e, tc=tc: e.activation(out=ssq[:, tc:tc + 1], in_=ssq[:, tc:tc + 1], func=AF.Sqrt, scale=1.0 / D, bias=epsc[:, 0:1]),
                             reads=[B_ssq, B_const], writes=[B_ssq])
                        S.op("dve", lambda e, tc=tc: e.reciprocal(out=ssq[:, tc:tc + 1], in_=ssq[:, tc:tc + 1]), reads=[B_ssq], writes=[B_ssq])
                        S.op("dve", lambda e, j=j, tc=tc: e.scalar_tensor_tensor(out=xo[j][:], in0=xo[j][:], scalar=ssq[:, tc:tc + 1], in1=gf[:], op0=ALU.mult, op1=ALU.mult),
                             reads=[B_xo[j], B_ssq, B_gf], writes=[B_xo[j]])
                    S.dma("sp", out_d[tc * 128:(tc + 1) * 128, :], xo[j][:], reads=[B_xo[j]])
                S.barrier()

        if st is None or st in (2, 3, 5):
            ssm_prep()
            load_xT(x_pre)
            if USE_FFN1:
                ffn(C_G1, w1g, w1u, w1d)
            if SUB != 20:
                mixer(False, pos_pre)
        load_xT(x_own)
        if USE_FFN1:
            ffn(C_G1, w1g, w1u, w1d)
        if (st is None or st in (2, 3, 5)) and SUB != 20:
            mixer(True, pos_own)
        if USE_FFN2:
            ffn(C_G2, w2g, w2u, w2d)
        store_out(st is None)
        S.barrier()
        with nc.Block() as block:
            S.emit(block)
    nc._in_names = in_names
    return nc


def _cols16(g):
    return np.ascontiguousarray(np.asarray(g, np.float32).reshape(-1, 128).T)


def _prep_shared(inp):
    f = lambda k: np.asarray(inp[k], np.float32)[0]
    sh = {}
    cols = np.zeros((128, 128), np.float32)
    cols[:, 0:16] = _cols16(f("ffn1_norm")); cols[:, 16:32] = _cols16(f("mix_norm"))
    cols[:, 32:48] = _cols16(f("ffn2_norm")); cols[:, 48:64] = _cols16(f("final_norm"))
    cols[:, 64:68] = _cols16(f("mla_q_a_norm")); cols[:, 68:70] = _cols16(f("mla_kv_a_norm"))
    cols[:, 70:78] = _cols16(f("ssm_out_norm")); cols[:, 78:86] = _cols16(f("attn_out_norm"))
    cols[:, 86:94] = _cols16(f("ssm_b_glu")); cols[:, 94:102] = _cols16(f("ssm_d").reshape(-1))
    perm = np.concatenate([np.arange(32, 64), np.arange(0, 32)])
    qn, kn = f("mla_q_norm"), f("mla_k_norm")
    cols[:, 102] = qn[:128]; cols[:64, 103] = qn[128:]; cols[:64, 104] = qn[128:][perm]
    cols[:, 105] = kn[:128]; cols[:64, 106] = kn[128:]; cols[:64, 107] = kn[128:][perm]
    invf = (10000.0 ** (-np.arange(0, 64, 2, dtype=np.float32) / 64)).astype(np.float32)
    cols[:64, 108] = np.concatenate([invf, invf])
    cols[:32, 109] = -1.0; cols[32:64, 109] = 1.0
    sh["cols"] = cols
    sh["ident"] = np.eye(128, dtype=np.float32)
    sh["tidx"] = np.ascontiguousarray(np.broadcast_to(np.arange(T, dtype=np.float32), (128, T)))
    ls, ar, ai = f("ssm_log_step"), f("ssm_a_re"), f("ssm_a_im")
    ssmp = np.zeros((128, 3, 32), np.float32)
    bblk = np.zeros((128, 32, 2, 128), np.float32)
    cblk = np.zeros((128, 32, 2, 128), np.float32)
    bre, bim, cre, cim = f("ssm_b_re"), f("ssm_b_im"), f("ssm_c_re"), f("ssm_c_im")
    for ti in range(32):
        for gl in range(2):
            g = 2 * ti + gl
            rows = slice(gl * 64, gl * 64 + 64)
            ssmp[rows, 0, ti] = ls[g]; ssmp[rows, 1, ti] = ar[g]; ssmp[rows, 2, ti] = ai[g]
            c0 = (ti % 4) * 32 + gl * 16
            bblk[rows, ti, 0, c0:c0 + 16] = bre[g]; bblk[rows, ti, 1, c0:c0 + 16] = bim[g]
            cblk[rows, ti, 0, c0:c0 + 16] = cre[g].T; cblk[rows, ti, 1, c0:c0 + 16] = cim[g].T
    sh["ssmp"], sh["bblk"], sh["cblk"] = ssmp, bblk, cblk
    sh["gfin"] = np.ascontiguousarray(np.broadcast_to(f("final_norm"), (128, D)))
    sh["w1g"], sh["w1u"], sh["w1d"] = f("ffn1_w_gate"), f("ffn1_w_up"), f("ffn1_w_down")
    sh["w2g"], sh["w2u"], sh["w2d"] = f("ffn2_w_gate"), f("ffn2_w_up"), f("ffn2_w_down")
    win = f("w_in")
    sh["win"] = np.ascontiguousarray(np.concatenate([win, win[:, 1792:][:, perm]], axis=1))
    sh["wglu"] = f("ssm_w_glu")
    wq = f("mla_w_q_up").reshape(512, 8, 192)
    sh["wqn"] = np.ascontiguousarray(wq[:, :, :128].reshape(512, 1024))
    sh["wqr"] = np.ascontiguousarray(wq[:, :, 128:].reshape(512, 512))
    sh["wqs"] = np.ascontiguousarray(wq[:, :, 128:][:, :, perm].reshape(512, 512))
    wkv = f("mla_w_kv_up").reshape(256, 8, 256)
    sh["wkk"] = np.ascontiguousarray(wkv[:, :, :128].reshape(256, 1024))
    sh["wkv"] = np.ascontiguousarray(wkv[:, :, 128:].reshape(256, 1024))
    sh["wout"] = f("w_out")
    return sh


def kernel(**inputs):
    x = np.asarray(inputs["x"], np.float32)
    pos = np.asarray(inputs["positions"], np.int32)
    sh = _prep_shared(inputs)
    in_maps = []
    for c in range(8):
        b, p = c // 2, c % 2
        m = dict(sh)
        m["x_own"] = np.ascontiguousarray(x[b, p * T:(p + 1) * T])
        m["pos_own"] = np.ascontiguousarray(np.broadcast_to(pos[b, p * T:(p + 1) * T], (64, T)))
        if p == 1:
            m["x_pre"] = np.ascontiguousarray(x[b, 0:T])
            m["pos_pre"] = np.ascontiguousarray(np.broadcast_to(pos[b, 0:T], (64, T)))
            m["maskbias"] = np.zeros((128, 1), np.float32)
        else:
            m["x_pre"] = np.zeros((T, D), np.float32)
            m["pos_pre"] = np.zeros((64, T), np.int32)
            m["maskbias"] = np.full((128, 1), -30000.0, np.float32)
        in_maps.append(m)
    nc = build_program()
    in_maps = [{k: m[k] for k in nc._in_names} for m in in_maps]
    res = run_bass_kernel_spmd(nc, in_maps, core_ids=list(range(8)))
    out = np.zeros((4, 2 * T, D), np.float32)
    for c in range(8):
        out[c // 2, (c % 2) * T:(c % 2 + 1) * T] = res.results[c]["out"]
    return out
```

```python
import math
import numpy as np
import concourse.bass as bass
import concourse.mybir as mybir
from concourse.bass_utils import run_bass_kernel_spmd

F32 = mybir.dt.float32
BF16 = mybir.dt.bfloat16
I32 = mybir.dt.int32
AF = mybir.ActivationFunctionType
ALU = mybir.AluOpType

D = 2048
T = 1024
NDC = 16
DFF = 5632
NG = DFF // 256
EPS = 1e-6
TWO_PI = 2.0 * math.pi
PI_SAFE = 3.1415925
SCALE = 192 ** -0.5
_DBG_STAGE = None
_DBG_SUB = None


class Buf:
    __slots__ = ("w", "r", "dsem", "dcnt", "name", "excl")

    def __init__(self, name="", excl=False):
        self.excl = excl
        self.w = None
        self.r = {}
        self.dsem = None
        self.dcnt = 0
        self.name = name


class _Rec:
    def __getattr__(self, name):
        def f(*a, **k):
            self.__dict__["call"] = (name, a, k)
            return self
        return f


class Sched:
    ENG = ("sp", "act", "dve", "pool", "pe")

    def __init__(self, nc, sems, dma_sems):
        self.nc = nc
        self.sem = sems
        half = len(dma_sems) // 2
        self.free_dma_sems = {"pool": list(dma_sems[:half]), "sp": list(dma_sems[half:])}
        self.ops = {e: [] for e in self.ENG}
        self.cnt = {e: 0 for e in self.ENG}
        self.seen = {e: {} for e in self.ENG}
        self.all_dma = {}
        self.sem_bufs = []

    def _deps(self, e, reads, writes):
        deps = {}

        def add(k, v):
            if v > deps.get(k, 0):
                deps[k] = v
        for b in reads:
            if b.w is not None:
                add(*b.w)
            if b.excl:
                for k, v in b.r.items():
                    add(k, v)
        for b in writes:
            if b.w is not None:
                add(*b.w)
            for k, v in b.r.items():
                add(k, v)
        waits = []
        for k, v in deps.items():
            if k is self.sem["pe"] and e == "pe":
                continue
            if v > self.seen[e].get(k, 0):
                self.seen[e][k] = v
                waits.append((k, v))
        return waits

    def op(self, e, emit, reads=(), writes=(), signal=True):
        rec = _Rec()
        emit(rec)
        name, a, k = rec.call
        emit = lambda eng, name=name, a=a, k=k: getattr(eng, name)(*a, **k)
        waits = self._deps(e, reads, writes)
        if signal:
            self.cnt[e] += 1
            val = self.cnt[e]
            sig = (self.sem[e], 1)
        else:
            val = self.cnt[e] + 1
            sig = None
        self.ops[e].append((waits, emit, sig))
        key = self.sem[e]
        for b in reads:
            if b.r.get(key, 0) < val:
                b.r[key] = val
        for b in writes:
            b.w = (key, val)
            b.r = {}

    def dma(self, q, out, in_, reads=(), writes=(), sembuf=None):
        sb = sembuf if sembuf is not None else (writes[0] if writes else reads[0])
        if sb.dsem is None:
            sb.dsem = self.free_dma_sems[q].pop()
            self.sem_bufs.append((sb, q))
        waits = self._deps(q, reads, writes)
        sem = sb.dsem
        val = self.all_dma.get(sem, 0) + 16
        self.ops[q].append((waits, lambda eng: eng.dma_start(out=out, in_=in_), (sem, 16)))
        self.all_dma[sem] = val
        for b in reads:
            if b.r.get(sem, 0) < val:
                b.r[sem] = val
        for b in writes:
            b.w = (sem, val)
            b.r = {}

    def barrier(self):
        targets = [(self.sem[f], self.cnt[f]) for f in self.ENG if self.cnt[f] > 0]
        targets += list(self.all_dma.items())
        for e in self.ENG:
            waits = []
            for k, v in targets:
                if k is self.sem[e]:
                    continue
                if v > self.seen[e].get(k, 0):
                    self.seen[e][k] = v
                    waits.append((k, v))
            if waits:
                self.ops[e].append((waits, None, None))
        for b, q in self.sem_bufs:
            self.free_dma_sems[q].append(b.dsem)
            b.dsem = None
        self.sem_bufs = []

    def emit(self, block):
        decos = {"sp": block.sync, "act": block.scalar, "dve": block.vector,
                 "pool": block.gpsimd, "pe": block.tensor}
        for name in self.ENG:
            ops = self.ops[name]

            def body(eng, ops=ops):
                for waits, emit, sig in ops:
                    for (s, v) in waits:
                        eng.wait_ge(s, v)
                    if emit is None:
                        continue
                    ins = emit(eng)
                    if sig is not None:
                        ins.then_inc(sig[0], sig[1])
            decos[name](body)


def build_program():
    nc = bass.Bass("TRN2", target_bir_lowering=False)

    in_names = []

    def din(name, shape, dt=F32):
        in_names.append(name)
        return nc.dram_tensor(name, list(shape), dt, kind="ExternalInput").ap()

    x_pre = din("x_pre", [T, D])
    x_own = din("x_own", [T, D])
    pos_pre = din("pos_pre", [64, T], I32)
    pos_own = din("pos_own", [64, T], I32)
    maskb_d = din("maskbias", [128, 1])
    ident_d = din("ident", [128, 128])
    tidx_d = din("tidx", [128, T])
    cols_d = din("cols", [128, 128])
    ssmp_d = din("ssmp", [128, 3, 32])
    bblk_d = din("bblk", [128, 32, 2, 128])
    cblk_d = din("cblk", [128, 32, 2, 128])
    gfin_d = din("gfin", [128, D])
    st = _DBG_STAGE
    SUB = _DBG_SUB
    USE_FFN1 = st not in (0, 5)
    USE_FFN2 = st is None or st == 3
    if USE_FFN1:
        w1g = din("w1g", [D, DFF]); w1u = din("w1u", [D, DFF]); w1d = din("w1d", [DFF, D])
    if USE_FFN2:
        w2g = din("w2g", [D, DFF]); w2u = din("w2u", [D, DFF]); w2d = din("w2d", [DFF, D])
    win_d = din("win", [D, 1920])
    wglu_d = din("wglu", [1024, 1024])
    wqn_d = din("wqn", [512, 1024]); wqr_d = din("wqr", [512, 512]); wqs_d = din("wqs", [512, 512])
    wkk_d = din("wkk", [256, 1024]); wkv_d = din("wkv", [256, 1024])
    wout_d = din("wout", [D, D])
    out_d = nc.dram_tensor("out", [T, D], F32, kind="ExternalOutput").ap()

    C_G1, C_GM, C_G2, C_GF = 0, 16, 32, 48
    C_GQA, C_GKA = 64, 68
    C_GSO, C_GAO, C_BGLU, C_DSK = 70, 78, 86, 94
    C_GQN, C_GQR, C_GQS, C_GKN, C_GKR, C_GKS = 102, 103, 104, 105, 106, 107
    C_INVF, C_SGN = 108, 109

    import contextlib
    es = contextlib.ExitStack()
    uid = [0]

    def sbt(name, shape, dt=F32):
        uid[0] += 1
        return nc.sbuf_tensor("sb%d_%s" % (uid[0], name), list(shape), dt)
    with es:
        def sb(name, shape, dt=F32):
            return es.enter_context(sbt(name, list(shape), dt))

        sems = {e: es.enter_context(nc.semaphore("s_" + e)) for e in Sched.ENG}
        dma_sems = [es.enter_context(nc.semaphore("d%d" % i)) for i in range(24)]
        S = Sched(nc, sems, dma_sems)

        xT = sb("xT", [128, NDC, T]); B_xT = Buf("xT")
        ident = sb("ident", [128, 128])
        onesb = sb("onesb", [128, 128], BF16)
        cols = sb("cols", [128, 128])
        maskb = sb("maskb", [128, 1])
        tidx = sb("tidx", [128, T])
        ssmp = sb("ssmp", [128, 3, 32])
        sth = sb("sth", [128, 32]); srd = sb("srd", [128, 32])
        sfr = sb("sfr", [128, 32]); sfi = sb("sfi", [128, 32])
        scT = sb("scT", [128, 32]); ssT = sb("ssT", [128, 32])
        stmp = sb("stmp", [128, 8, 32])
        state = sb("state", [128, 32, 2])
        kvnT = sb("kvnT", [128, 2, 2 * T], BF16)
        krT = sb("krT", [64, 2 * T], BF16)
        sqkpe = sb("sqkpe", [64, 2 * T], BF16)
        negpi = sb("negpi", [128, 1]); halfpi = sb("halfpi", [128, 1]); sthn = sb("sthn", [128, 32]); epsc = sb("epsc", [128, 1]); zeroc = sb("zeroc", [128, 1])
        B_const = Buf("const"); B_state = Buf("state")
        B_kvn = Buf("kvn"); B_kr = Buf("kr"); B_sqk = Buf("sqk")
        ring = {"slots": [], "bufs": [], "i": 0}

        def open_ring(sc, n):
            ring["slots"] = [sc.enter_context(sbt("wslot%d_%d" % (i, ctr["ring"]), [128, 4096], BF16)) for i in range(n)]
            ring["bufs"] = [Buf("ws%d" % i) for i in range(n)]
            ring["i"] = 0
            ctr["ring"] += 1
        pp = [es.enter_context(nc.psum_tensor("pp%d" % i, [128, 1024], F32)) for i in range(4)]
        ps = [pp[i // 2][:, (i % 2) * 512:(i % 2 + 1) * 512] for i in range(8)]
        B_ps = [Buf("ps%d" % i, excl=True) for i in range(8)]

        def load_w(src_ap, view):
            i = ring["i"] % len(ring["slots"])
            ring["i"] += 1
            wslot, B_ws = ring["slots"], ring["bufs"]
            if view[0] == "c":
                dst = wslot[i][:, 0:view[1] * view[2]].rearrange("p (c n) -> p c n", c=view[1])
            else:
                dst = wslot[i][:, 0:view[1]]
            S.dma("pool", dst, src_ap, writes=[B_ws[i]])
            return dst, B_ws[i]

        def wpiece_cols(w, k0, ncol, nchunk):
            return w[:, k0:k0 + ncol].rearrange("(c p) n -> p c n", p=128), ("c", nchunk, ncol)

        def wpiece_rows(w, r0, nrow_chunks, ncol):
            return (w[r0:r0 + 128 * nrow_chunks, :].rearrange("(c p) n -> p c n", p=128),
                    ("c", nrow_chunks, ncol))

        def mm(out, lhsT, rhs, start, stop, reads, bank, signal=None):
            sig = True if signal is None else signal
            S.op("pe", lambda e: e.matmul(out, lhsT=lhsT, rhs=rhs, start=start, stop=stop),
                 reads=reads, writes=[bank], signal=sig)

        def col(c, n=128):
            return cols[0:n, c:c + 1]

        ctr = {"ring": 0}
        for dst, src in ((ident[:], ident_d), (cols[:], cols_d), (maskb[:], maskb_d),
                         (tidx[:], tidx_d), (ssmp[:], ssmp_d)):
            S.dma("sp", dst, src, writes=[B_const], sembuf=B_const)
        S.op("dve", lambda e: e.memset(onesb[:], 1.0), writes=[B_const])
        S.op("dve", lambda e: e.memset(negpi[:], -math.pi), writes=[B_const])
        S.op("dve", lambda e: e.memset(halfpi[:], 0.5 * math.pi), writes=[B_const])
        S.op("dve", lambda e: e.memset(epsc[:], EPS), writes=[B_const])
        S.op("dve", lambda e: e.memset(zeroc[:], 0.0), writes=[B_const])
        S.op("dve", lambda e: e.memset(state[:], 0.0), writes=[B_state])

        sqt = [sb("sqt%d" % i, [128, 512], BF16) for i in range(2)]
        B_sqt = [Buf("sqt0"), Buf("sqt1")]
        rstd = [sb("rstd%d" % i, [128, 512]) for i in range(2)]
        B_rstd = [Buf("rstd0"), Buf("rstd1")]
        ctr["sq"] = 0; ctr["rs"] = 0; ctr["ps"] = 0; ctr["psn"] = 6; ctr["pair"] = 0

        def next_ps():
            i = ctr["ps"] % ctr["psn"]
            ctr["ps"] += 1
            return ps[i], B_ps[i]

        def long_ps(i):
            return ps[6 + i], B_ps[6 + i]

        def rmsnorm(srcs, nfeat, gains, outs, ntok=T):
            for blk in range(ntok // 512):
                pst, B_pst = next_ps()
                n = len(srcs)
                for i, (fn, P, bsrc) in enumerate(srcs):
                    j = ctr["sq"] % 2
                    ctr["sq"] += 1
                    src = fn(blk)
                    S.op("act", lambda e, j=j, src=src, P=P: e.activation(out=sqt[j][0:P, :], in_=src, func=AF.Square),
                         reads=[bsrc], writes=[B_sqt[j]])
                    mm(pst[:, :], onesb[0:P, :], sqt[j][0:P, :], i == 0, i == n - 1, [B_sqt[j], B_const], B_pst)
                k = ctr["rs"] % 2
                ctr["rs"] += 1
                S.op("act", lambda e, k=k, pst=pst: e.activation(out=rstd[k][:], in_=pst[:, :], func=AF.Ln,
                                                                scale=1.0 / nfeat, bias=epsc[:, 0:1]),
                     reads=[B_pst, B_const], writes=[B_rstd[k]])
                S.op("act", lambda e, k=k: e.activation(out=rstd[k][:], in_=rstd[k][:], func=AF.Exp, scale=-0.5),
                     reads=[B_rstd[k]], writes=[B_rstd[k]])
                for (fn, P, bsrc), g, (ofn, bout) in zip(srcs, gains, outs):
                    src = fn(blk)
                    o = ofn(blk)
                    S.op("dve", lambda e, src=src, o=o, g=g, k=k, P=P: e.scalar_tensor_tensor(
                        out=o, in0=src, scalar=g, in1=rstd[k][0:P, :], op0=ALU.mult, op1=ALU.mult),
                        reads=[bsrc, B_rstd[k], B_const], writes=[bout])

        def blk(ap3, c):
            return lambda b: ap3[:, c, b * 512:(b + 1) * 512]

        def load_xT(x_dram):
            with contextlib.ExitStack() as sc:
                xin = [sc.enter_context(sbt("xin%d" % i, [128, D], F32)) for i in range(2)]
                B_xin = [Buf("xin0"), Buf("xin1")]
                for tc in range(T // 128):
                    j = tc % 2
                    S.dma("sp", xin[j][:], x_dram[tc * 128:(tc + 1) * 128, :], writes=[B_xin[j]])
                    for q in range(4):
                        pst, B_pst = next_ps()
                        for r in range(4):
                            dc = q * 4 + r
                            S.op("pe", lambda e, pst=pst, r=r, dc=dc, j=j: e.transpose(
                                pst[:, r * 128:(r + 1) * 128], xin[j][:, dc * 128:(dc + 1) * 128], ident[:]),
                                reads=[B_xin[j], B_const], writes=[B_pst], signal=(r == 3))
                        S.op("act", lambda e, pst=pst, q=q, tc=tc: e.activation(
                            out=xT[:, q * 4:(q + 1) * 4, tc * 128:(tc + 1) * 128],
                            in_=pst[:, :].rearrange("p (r n) -> p r n", r=4), func=AF.Copy),
                            reads=[B_pst], writes=[B_xT])
                S.barrier()

        def ffn(gcol, wg, wu, wd):
            with contextlib.ExitStack() as sc:
                xnT = sc.enter_context(sbt("xnT", [128, NDC, T], BF16)); B_xn = Buf("xn")
                H = [sc.enter_context(sbt("H%d" % i, [128, 2, T], BF16)) for i in range(2)]
                B_H = [Buf("H0"), Buf("H1")]
                sg = [sc.enter_context(sbt("sg%d" % i, [128, 512], F32)) for i in range(2)]
                B_sg = [Buf("sg0"), Buf("sg1")]
                open_ring(sc, 6)
                rmsnorm([(blk(xT, c), 128, B_xT) for c in range(NDC)], D,
                        [col(gcol + c) for c in range(NDC)],
                        [(blk(xnT, c), B_xn) for c in range(NDC)])
                pend = []

                def issue_gu(k):
                    a, v = wpiece_cols(wg, k * 256, 256, 16)
                    g_ = load_w(a, v)
                    a, v = wpiece_cols(wu, k * 256, 256, 16)
                    u_ = load_w(a, v)
                    pend.append([g_, u_, None])

                def issue_d(k):
                    a, v = wpiece_rows(wd, k * 256, 2, D)
                    pend[k][2] = load_w(a, v)
                issue_gu(0)
                issue_d(0)
                sgi = 0

                def gate_up(k):
                    nonlocal sgi
                    yield_steps = True
                    (wgs, B_wg), (wus, B_wu), _ = pend[k]
                    h = H[k % 2]
                    B_h = B_H[k % 2]
                    for fc in range(2):
                        for hb in range(2):
                            pg, B_pg = next_ps()
                            pu, B_pu = next_ps()
                            for dc in range(NDC):
                                mm(pg[:, :], wgs[:, dc, fc * 128:(fc + 1) * 128], xnT[:, dc, hb * 512:(hb + 1) * 512],
                                   dc == 0, dc == NDC - 1, [B_wg, B_xn], B_pg, signal=(dc == NDC - 1))
                            for dc in range(NDC):
                                mm(pu[:, :], wus[:, dc, fc * 128:(fc + 1) * 128], xnT[:, dc, hb * 512:(hb + 1) * 512],
                                   dc == 0, dc == NDC - 1, [B_wu, B_xn], B_pu, signal=(dc == NDC - 1))
                            j = sgi % 2
                            sgi += 1
                            S.op("act", lambda e: e.activation(out=sg[j][:], in_=pg[:, :], func=AF.Silu),
                                 reads=[B_pg], writes=[B_sg[j]])
                            S.op("dve", lambda e: e.tensor_tensor(
                                out=h[:, fc, hb * 512:(hb + 1) * 512], in0=sg[j][:], in1=pu[:, :], op=ALU.mult),
                                reads=[B_sg[j], B_pu], writes=[B_h])
                            yield

                def down(k):
                    _, _, (wds, B_wd) = pend[k]
                    h = H[k % 2]
                    B_h = B_H[k % 2]
                    for dc in range(NDC):
                        for hb in range(2):
                            py, B_py = next_ps()
                            for fc in range(2):
                                mm(py[:, :], wds[:, fc, dc * 128:(dc + 1) * 128], h[:, fc, hb * 512:(hb + 1) * 512],
                                   fc == 0, fc == 1, [B_wd, B_h], B_py, signal=(fc == 1))
                            S.op("dve", lambda e: e.scalar_tensor_tensor(
                                out=xT[:, dc, hb * 512:(hb + 1) * 512], in0=py[:, :], scalar=0.5,
                                in1=xT[:, dc, hb * 512:(hb + 1) * 512], op0=ALU.mult, op1=ALU.add),
                                reads=[B_py, B_xT], writes=[B_xT])
                            yield

                ctr["psn"] = 8
                for k in range(NG):
                    if k + 1 < NG:
                        issue_gu(k + 1)
                    gd = down(k - 1) if k > 0 else iter(())
                    for _ in gate_up(k):
                        for _i in range(8):
                            next(gd, None)
                    for _ in gd:
                        pass
                    if k + 1 < NG:
                        issue_d(k + 1)
                for _ in down(NG - 1):
                    pass
                ctr["psn"] = 6
                S.barrier()

        MAGIC = 12582912.0

        def sincos(dst_sin, dst_cos, ang, tmp, R):
            S.op("dve", lambda e: e.tensor_scalar(out=tmp, in0=ang, scalar1=1.0 / TWO_PI, scalar2=MAGIC, op0=ALU.mult, op1=ALU.add), reads=R, writes=R)
            S.op("dve", lambda e: e.tensor_scalar(out=tmp, in0=tmp, scalar1=MAGIC, scalar2=-TWO_PI, op0=ALU.subtract, op1=ALU.mult), reads=R, writes=R)
            S.op("dve", lambda e: e.tensor_tensor(out=tmp, in0=tmp, in1=ang, op=ALU.add), reads=R, writes=R)
            S.op("dve", lambda e: e.tensor_scalar(out=tmp, in0=tmp, scalar1=PI_SAFE, scalar2=-PI_SAFE, op0=ALU.min, op1=ALU.max), reads=R, writes=R)
            S.op("act", lambda e: e.activation(out=dst_sin, in_=tmp, func=AF.Sin), reads=R, writes=R)
            S.op("dve", lambda e: e.scalar_tensor_tensor(out=tmp, in0=tmp, scalar=-1.0, in1=tmp, op0=ALU.mult, op1=ALU.max), reads=R, writes=R)
            S.op("act", lambda e: e.activation(out=dst_cos, in_=tmp, func=AF.Sin, scale=-1.0, bias=halfpi[:, 0:1]), reads=R, writes=R)

        def ssm_prep():
            ls, ar, ai = ssmp[:, 0, :], ssmp[:, 1, :], ssmp[:, 2, :]
            tm = lambda i: stmp[:, i, :]
            R = [B_const]

            def dv(fn):
                S.op("dve", fn, reads=R, writes=R)

            def ac(fn):
                S.op("act", fn, reads=R, writes=R)
            ac(lambda e: e.activation(out=tm(0), in_=ls, func=AF.Exp))
            dv(lambda e: e.tensor_tensor(out=tm(1), in0=ar, in1=tm(0), op=ALU.mult))
            dv(lambda e: e.tensor_tensor(out=sth[:], in0=ai, in1=tm(0), op=ALU.mult))
            ac(lambda e: e.activation(out=srd[:], in_=tm(1), func=AF.Exp))
            dv(lambda e: e.tensor_scalar(out=sthn[:], in0=sth[:], scalar1=1.0 / TWO_PI, scalar2=None, op0=ALU.mult))
            sincos(tm(2), tm(3), sth[:], tm(7), R)
            dv(lambda e: e.tensor_scalar(out=tm(6), in0=sth[:], scalar1=float(T), scalar2=None, op0=ALU.mult))
            sincos(ssT[:], scT[:], tm(6), tm(7), R)
            dv(lambda e: e.tensor_tensor(out=tm(4), in0=srd[:], in1=tm(3), op=ALU.mult))
            dv(lambda e: e.tensor_tensor(out=tm(5), in0=srd[:], in1=tm(2), op=ALU.mult))
            dv(lambda e: e.tensor_scalar(out=tm(4), in0=tm(4), scalar1=-1.0, scalar2=None, op0=ALU.add))
            dv(lambda e: e.tensor_tensor(out=tm(6), in0=ar, in1=ar, op=ALU.mult))
            dv(lambda e: e.tensor_tensor(out=tm(7), in0=ai, in1=ai, op=ALU.mult))
            dv(lambda e: e.tensor_tensor(out=tm(6), in0=tm(6), in1=tm(7), op=ALU.add))
            dv(lambda e: e.reciprocal(out=tm(6), in_=tm(6)))
            dv(lambda e: e.tensor_tensor(out=tm(0), in0=tm(4), in1=ar, op=ALU.mult))
            dv(lambda e: e.tensor_tensor(out=tm(1), in0=tm(5), in1=ai, op=ALU.mult))
            dv(lambda e: e.tensor_tensor(out=tm(0), in0=tm(0), in1=tm(1), op=ALU.add))
            dv(lambda e: e.tensor_tensor(out=sfr[:], in0=tm(0), in1=tm(6), op=ALU.mult))
            dv(lambda e: e.tensor_tensor(out=tm(0), in0=tm(5), in1=ar, op=ALU.mult))
            dv(lambda e: e.tensor_tensor(out=tm(1), in0=tm(4), in1=ai, op=ALU.mult))
            dv(lambda e: e.tensor_tensor(out=tm(0), in0=tm(0), in1=tm(1), op=ALU.subtract))
            dv(lambda e: e.tensor_tensor(out=sfi[:], in0=tm(0), in1=tm(6), op=ALU.mult))

        def rotate_state():
            R = [B_const, B_state]
            re, im = state[:, :, 0], state[:, :, 1]
            tm = lambda i: stmp[:, i, :]

            def dv(fn):
                S.op("dve", fn, reads=R, writes=R)
            dv(lambda e: e.tensor_tensor(out=tm(0), in0=scT[:], in1=re, op=ALU.mult))
            dv(lambda e: e.tensor_tensor(out=tm(1), in0=ssT[:], in1=im, op=ALU.mult))
            dv(lambda e: e.tensor_tensor(out=tm(2), in0=ssT[:], in1=re, op=ALU.mult))
            dv(lambda e: e.tensor_tensor(out=tm(3), in0=scT[:], in1=im, op=ALU.mult))
            dv(lambda e: e.tensor_tensor(out=re, in0=tm(0), in1=tm(1), op=ALU.subtract))
            dv(lambda e: e.tensor_tensor(out=im, in0=tm(2), in1=tm(3), op=ALU.add))

        def build_Bl(sc, Bl, B_Bl, scr_a, scr_b):
            bb = [scr_a[:, i * 256:(i + 1) * 256].rearrange("p (r n) -> p r n", r=2) for i in range(2)]
            B_bb = [Buf("bb0"), Buf("bb1")]
            bo = [scr_b[:, i * 256:(i + 1) * 256].rearrange("p (r n) -> p r n", r=2) for i in range(2)]
            B_bo = [Buf("bo0"), Buf("bo1")]
            for ti in range(32):
                j = ti % 2
                S.dma("sp", bb[j], bblk_d[:, ti, :, :], writes=[B_bb[j]])
                fr, fi = sfr[:, ti:ti + 1], sfi[:, ti:ti + 1]
                S.op("dve", lambda e, j=j, fi=fi: e.tensor_scalar(out=bo[j][:, 0, :], in0=bb[j][:, 1, :], scalar1=fi, scalar2=-1.0, op0=ALU.mult, op1=ALU.mult),
                     reads=[B_bb[j], B_const], writes=[B_bo[j]])
                S.op("dve", lambda e, j=j, fr=fr: e.scalar_tensor_tensor(out=bo[j][:, 0, :], in0=bb[j][:, 0, :], scalar=fr, in1=bo[j][:, 0, :], op0=ALU.mult, op1=ALU.add),
                     reads=[B_bb[j], B_const, B_bo[j]], writes=[B_bo[j]])
                S.op("dve", lambda e, j=j, fi=fi: e.tensor_scalar(out=bo[j][:, 1, :], in0=bb[j][:, 0, :], scalar1=fi, scalar2=None, op0=ALU.mult),
                     reads=[B_bb[j], B_const, B_bo[j]], writes=[B_bo[j]])
                S.op("dve", lambda e, j=j, fr=fr: e.scalar_tensor_tensor(out=bo[j][:, 1, :], in0=bb[j][:, 1, :], scalar=fr, in1=bo[j][:, 1, :], op0=ALU.mult, op1=ALU.add),
                     reads=[B_bb[j], B_const, B_bo[j]], writes=[B_bo[j]])
                pst, B_pst = next_ps()
                for ri in range(2):
                    S.op("pe", lambda e, pst=pst, ri=ri, j=j: e.transpose(pst[:, ri * 128:(ri + 1) * 128], bo[j][:, ri, :], ident[:]),
                         reads=[B_bo[j], B_const], writes=[B_pst], signal=(ri == 1))
                S.op("act", lambda e, pst=pst, ti=ti: e.activation(out=Bl[:, ti, :, :], in_=pst[:, 0:256].rearrange("p (r n) -> p r n", r=2), func=AF.Copy),
                     reads=[B_pst], writes=[B_Bl])

        def mixer(own, pos_d):
            t0 = T if own else 0
            sfx = "o" if own else "p"
            with contextlib.ExitStack() as sc:
                def sbl(name, shape, dt=F32):
                    return sc.enter_context(sbt(name + sfx, list(shape), dt))
                ycat = sbl("ycat", [128, 16, T], BF16); B_ycat = Buf("ycat")
                qnT = sbl("qnT", [128, 4, T], BF16) if own else None
                B_qn = Buf("qn")
                s_u = sc.enter_context(contextlib.ExitStack())
                uT = s_u.enter_context(sbt("uT" + sfx, [128, 8, T], BF16)); B_u = Buf("u")

                def rope_tables(s2):
                    ropc = s2.enter_context(sbt("ropc%d" % ctr["ring"], [64, T], F32))
                    rops = s2.enter_context(sbt("rops%d" % ctr["ring"], [64, T], F32))
                    B_rope = Buf("rope")
                    ctr["ring"] += 1
                    posi = s2.enter_context(sbt("posi%d" % ctr["ring"], [64, T], I32))
                    S.dma("sp", posi[:], pos_d, writes=[B_rope])
                    S.op("dve", lambda e: e.tensor_copy(out=rops[:], in_=posi[:]), reads=[B_rope], writes=[B_rope])
                    S.op("dve", lambda e: e.tensor_scalar(out=ropc[:], in0=rops[:], scalar1=col(C_INVF, 64), scalar2=None, op0=ALU.mult),
                         reads=[B_rope, B_const], writes=[B_rope])
                    R = [B_rope, B_const]
                    S.op("dve", lambda e: e.tensor_scalar(out=rops[:], in0=ropc[:], scalar1=1.0 / TWO_PI, scalar2=MAGIC, op0=ALU.mult, op1=ALU.add), reads=R, writes=R)
                    S.op("dve", lambda e: e.tensor_scalar(out=rops[:], in0=rops[:], scalar1=MAGIC, scalar2=-TWO_PI, op0=ALU.subtract, op1=ALU.mult), reads=R, writes=R)
                    S.op("dve", lambda e: e.tensor_tensor(out=rops[:], in0=rops[:], in1=ropc[:], op=ALU.add), reads=R, writes=R)
                    S.op("dve", lambda e: e.tensor_scalar(out=rops[:], in0=rops[:], scalar1=PI_SAFE, scalar2=-PI_SAFE, op0=ALU.min, op1=ALU.max), reads=R, writes=R)
                    S.op("dve", lambda e: e.scalar_tensor_tensor(out=ropc[:], in0=rops[:], scalar=-1.0, in1=rops[:], op0=ALU.mult, op1=ALU.max), reads=R, writes=R)
                    S.op("act", lambda e: e.activation(out=rops[:], in_=rops[:], func=AF.Sin), reads=R, writes=R)
                    S.op("act", lambda e: e.activation(out=ropc[:], in_=ropc[:], func=AF.Sin, scale=-1.0, bias=halfpi[0:64, 0:1]), reads=R, writes=R)
                    S.op("dve", lambda e: e.tensor_scalar(out=rops[:], in0=rops[:], scalar1=col(C_SGN, 64), scalar2=None, op0=ALU.mult), reads=R, writes=R)
                    return ropc, rops, B_rope

                with contextlib.ExitStack() as s2:
                    xnT, B_xn = ycat, B_ycat
                    ropc, rops, B_rope = rope_tables(s2)
                    if SUB == 21:
                        S.barrier()
                        return
                    zt = s2.enter_context(sbt("zt" + sfx, [128, 4, T], F32)); B_zt = Buf("zt")
                    ta = s2.enter_context(sbt("kta" + sfx, [64, 512], F32))
                    tb = s2.enter_context(sbt("ktb" + sfx, [64, 512], F32)); B_t = Buf("kt")
                    open_ring(s2, 2)
                    rmsnorm([(blk(xT, c), 128, B_xT) for c in range(NDC)], D,
                            [col(C_GM + c) for c in range(NDC)], [(blk(xnT, c), B_xn) for c in range(NDC)])
                    pieces = [0, 1, 2, 3, 6] + ([4, 5] if own else [])
                    for pc in pieces:
                        a, v = wpiece_cols(win_d, pc * 256, 256, 16)
                        wsl, B_w = load_w(a, v)
                        for sub in range(2):
                            for hb in range(2):
                                pst, B_pst = next_ps()
                                for dc in range(NDC):
                                    mm(pst[:, :], wsl[:, dc, sub * 128:(sub + 1) * 128], xnT[:, dc, hb * 512:(hb + 1) * 512],
                                       dc == 0, dc == NDC - 1, [B_w, B_xn], B_pst, signal=(dc == NDC - 1))
                                sl = slice(hb * 512, (hb + 1) * 512)
                                if pc < 4:
                                    dst, bd = uT[:, pc * 2 + sub, sl], B_u
                                elif pc == 6:
                                    dst, bd = zt[:, sub, sl], B_zt
                                else:
                                    dst, bd = zt[:, (pc - 4) * 2 + sub, sl], B_zt
                                S.op("act", lambda e, dst=dst, pst=pst: e.activation(out=dst, in_=pst[:, :], func=AF.Copy),
                                     reads=[B_pst], writes=[bd])
                        if pc == 6:
                            rmsnorm([(blk(zt, c), 128, B_zt) for c in range(2)], 256,
                                    [col(C_GKA + c) for c in range(2)],
                                    [((lambda b, c=c: kvnT[:, c, t0 + b * 512:t0 + (b + 1) * 512]), B_kvn) for c in range(2)])
                    if SUB == 22:
                        S.barrier()
                        return
                    if own:
                        rmsnorm([(blk(zt, c), 128, B_zt) for c in range(4)], 512,
                                [col(C_GQA + c) for c in range(4)], [(blk(qnT, c), B_qn) for c in range(4)])
                    if SUB == 23:
                        S.barrier()
                        return
                    a, v = wpiece_cols(win_d, 1792, 128, 16)
                    wsl, B_w = load_w(a, v)
                    for hb in range(2):
                        sl = slice(hb * 512, (hb + 1) * 512)
                        gsl = slice(t0 + hb * 512, t0 + (hb + 1) * 512)
                        pk, B_pk = next_ps()
                        pw, B_pw = next_ps()
                        for dc in range(NDC):
                            mm(pk[0:64, :], wsl[:, dc, 0:64], xnT[:, dc, sl], dc == 0, dc == NDC - 1, [B_w, B_xn], B_pk)
                        for dc in range(NDC):
                            mm(pw[0:64, :], wsl[:, dc, 64:128], xnT[:, dc, sl], dc == 0, dc == NDC - 1, [B_w, B_xn], B_pw)
                        if SUB == 24:
                            continue
                        S.op("act", lambda e, pk=pk, gsl=gsl: e.activation(out=sqkpe[:, gsl], in_=pk[0:64, :], func=AF.Square),
                             reads=[B_pk], writes=[B_sqk])
                        if SUB == 25:
                            continue
                        S.op("dve", lambda e, pk=pk, sl=sl: e.scalar_tensor_tensor(out=ta[:], in0=pk[0:64, :], scalar=col(C_GKR, 64), in1=ropc[:, sl], op0=ALU.mult, op1=ALU.mult),
                             reads=[B_pk, B_rope, B_const], writes=[B_t])
                        if SUB == 26:
                            continue
                        S.op("dve", lambda e, pw=pw, sl=sl: e.scalar_tensor_tensor(out=tb[:], in0=pw[0:64, :], scalar=col(C_GKS, 64), in1=rops[:, sl], op0=ALU.mult, op1=ALU.mult),
                             reads=[B_pw, B_rope, B_const, B_t], writes=[B_t])
                        if SUB == 27:
                            continue
                        S.op("dve", lambda e, gsl=gsl: e.tensor_tensor(out=krT[:, gsl], in0=ta[:], in1=tb[:], op=ALU.add),
                             reads=[B_t], writes=[B_kr])
                    S.barrier()
                if SUB == 2:
                    return
                with contextlib.ExitStack() as s2:
                    def s2b(name, shape, dt=F32):
                        return s2.enter_context(sbt(name + sfx, list(shape), dt))
                    tq = s2b("tq", [128, T]); B_tq = Buf("tq")
                    t1, t2 = tq[:, 0:512], tq[:, 512:1024]
                    s_loop = s2.enter_context(contextlib.ExitStack())
                    tq2 = s_loop.enter_context(sbt("tq2" + sfx, [128, T], F32))

                    def s2b(name, shape, dt=F32):
                        return s_loop.enter_context(sbt(name + sfx, list(shape), dt))
                    Bl = s2b("Bl", [128, 32, 2, 128], BF16); B_Bl = Buf("Bl")
                    ncs2 = [s2b("ncs%d" % i, [128, T]) for i in range(2)]
                    nsn2 = [s2b("nsn%d" % i, [128, T]) for i in range(2)]
                    B_tab2 = [Buf("tab0"), Buf("tab1")]
                    rdec = s2b("rdec", [128, T]); B_rdec = Buf("rdec")
                    cre = s2b("cre", [128, T]); cim = s2b("cim", [128, T]); B_c = Buf("c")
                    sre2 = [s2b("sre%d" % i, [128, 512], BF16) for i in range(2)]
                    sim2 = [s2b("sim%d" % i, [128, 512], BF16) for i in range(2)]
                    B_s2 = [Buf("s0"), Buf("s1")]
                    cl = [s2b("cl%d" % i, [128, 2, 128], BF16) for i in range(2)]
                    B_cl = [Buf("cl0"), Buf("cl1")]
                    Ddiag1 = s2b("Ddiag1", [128, 128], BF16); B_dd = Buf("dd")
                    with contextlib.ExitStack() as s3:
                        build_Bl(s3, Bl, B_Bl, tq, tq2)
                        S.barrier()
                    def gen_tables(ti):
                        ncs, nsn, B_tab = ncs2[ti % 2], nsn2[ti % 2], B_tab2[ti % 2]
                        th = sth[:, ti:ti + 1]
                        thn = sthn[:, ti:ti + 1]
                        S.op("dve", lambda e: e.tensor_scalar(out=nsn[:], in0=tidx[:], scalar1=thn, scalar2=MAGIC, op0=ALU.mult, op1=ALU.add),
                             reads=[B_const], writes=[B_tab])
                        S.op("dve", lambda e: e.tensor_scalar(out=nsn[:], in0=nsn[:], scalar1=MAGIC, scalar2=-TWO_PI, op0=ALU.subtract, op1=ALU.mult),
                             reads=[B_tab], writes=[B_tab])
                        S.op("dve", lambda e: e.scalar_tensor_tensor(out=nsn[:], in0=tidx[:], scalar=th, in1=nsn[:], op0=ALU.mult, op1=ALU.add),
                             reads=[B_tab, B_const], writes=[B_tab])
                        S.op("dve", lambda e: e.tensor_scalar(out=nsn[:], in0=nsn[:], scalar1=PI_SAFE, scalar2=-PI_SAFE, op0=ALU.min, op1=ALU.max),
                             reads=[B_tab], writes=[B_tab])
                        S.op("dve", lambda e: e.scalar_tensor_tensor(out=ncs[:], in0=nsn[:], scalar=-1.0, in1=nsn[:], op0=ALU.mult, op1=ALU.max), reads=[B_tab], writes=[B_tab])
                        S.op("act", lambda e: e.activation(out=nsn[:], in_=nsn[:], func=AF.Sin), reads=[B_tab], writes=[B_tab])
                        S.op("act", lambda e: e.activation(out=ncs[:], in_=ncs[:], func=AF.Sin, scale=-1.0, bias=halfpi[:, 0:1]), reads=[B_tab, B_const], writes=[B_tab])

                    for uc in range(8):
                        if own:
                            py = [long_ps(0), long_ps(1)]
                        for tl in range(4):
                            ti = uc * 4 + tl
                            if ti == 0:
                                gen_tables(0)
                            if ti + 1 < 32:
                                gen_tables(ti + 1)
                            ncs, nsn, B_tab = ncs2[ti % 2], nsn2[ti % 2], B_tab2[ti % 2]
                            S.op("act", lambda e, ti=ti: e.activation(out=rdec[:], in_=tidx[:], func=AF.Identity, scale=0.0, bias=srd[:, ti:ti + 1]),
                                 reads=[B_const], writes=[B_rdec])
                            if own:
                                j = ti % 2
                                S.dma("pool", cl[j][:], cblk_d[:, ti, :, :], writes=[B_cl[j]])
                            pa_i = (ctr["pair"] % 3); pb_i = ((ctr["pair"] + 1) % 3); ctr["pair"] += 2
                            Bpa = [B_ps[2 * pa_i], B_ps[2 * pa_i + 1]]
                            Bpb = [B_ps[2 * pb_i], B_ps[2 * pb_i + 1]]
                            for hb in range(2):
                                sl = slice(hb * 512, (hb + 1) * 512)
                                mm(pp[pa_i][:, sl], Bl[:, ti, 0, :], uT[:, uc, sl], True, True, [B_Bl, B_u], Bpa[hb])
                                mm(pp[pb_i][:, sl], Bl[:, ti, 1, :], uT[:, uc, sl], True, True, [B_Bl, B_u], Bpb[hb])
                            pr, pi_ = pp[pa_i], pp[pb_i]
                            S.op("dve", lambda e: e.tensor_tensor(out=tq[:], in0=ncs[:], in1=pr[:, :], op=ALU.mult), reads=[B_tab] + Bpa, writes=[B_tq])
                            S.op("dve", lambda e: e.tensor_tensor(out=tq2[:], in0=nsn[:], in1=pi_[:, :], op=ALU.mult), reads=[B_tab] + Bpb, writes=[B_tq])
                            S.op("dve", lambda e: e.tensor_tensor(out=cre[:], in0=tq[:], in1=tq2[:], op=ALU.add), reads=[B_tq], writes=[B_c])
                            S.op("dve", lambda e: e.tensor_tensor(out=tq[:], in0=ncs[:], in1=pi_[:, :], op=ALU.mult), reads=[B_tab, B_c] + Bpb, writes=[B_tq])
                            S.op("dve", lambda e: e.tensor_tensor(out=tq2[:], in0=nsn[:], in1=pr[:, :], op=ALU.mult), reads=[B_tab, B_c] + Bpa, writes=[B_tq])
                            S.op("dve", lambda e: e.tensor_tensor(out=cim[:], in0=tq[:], in1=tq2[:], op=ALU.subtract), reads=[B_tq], writes=[B_c])
                            S.op("dve", lambda e, ti=ti: e.tensor_tensor_scan(out=cre[:], data0=rdec[:], data1=cre[:], initial=state[:, ti, 0:1], op0=ALU.mult, op1=ALU.add),
                                 reads=[B_c, B_rdec, B_state], writes=[B_c])
                            S.op("dve", lambda e, ti=ti: e.tensor_tensor_scan(out=cim[:], data0=rdec[:], data1=cim[:], initial=state[:, ti, 1:2], op0=ALU.mult, op1=ALU.add),
                                 reads=[B_c, B_rdec, B_state], writes=[B_c])
                            if not own:
                                S.op("dve", lambda e, ti=ti: e.tensor_copy(out=state[:, ti, 0:1], in_=cre[:, T - 1:T]), reads=[B_c], writes=[B_state])
                                S.op("dve", lambda e, ti=ti: e.tensor_copy(out=state[:, ti, 1:2], in_=cim[:, T - 1:T]), reads=[B_c], writes=[B_state])
                                continue
                            S.op("dve", lambda e: e.tensor_tensor(out=tq[:], in0=ncs[:], in1=cre[:], op=ALU.mult), reads=[B_tab, B_c], writes=[B_tq])
                            S.op("dve", lambda e: e.tensor_tensor(out=tq2[:], in0=nsn[:], in1=cim[:], op=ALU.mult), reads=[B_tab, B_c], writes=[B_tq])
                            for hb in range(2):
                                sl = slice(hb * 512, (hb + 1) * 512)
                                S.op("dve", lambda e: e.tensor_tensor(out=sre2[hb][:], in0=tq[:, sl], in1=tq2[:, sl], op=ALU.subtract), reads=[B_tq], writes=[B_s2[hb]])
                            S.op("dve", lambda e: e.tensor_tensor(out=tq[:], in0=nsn[:], in1=cre[:], op=ALU.mult), reads=[B_tab, B_c] + B_s2, writes=[B_tq])
                            S.op("dve", lambda e: e.tensor_tensor(out=tq2[:], in0=ncs[:], in1=cim[:], op=ALU.mult), reads=[B_tab, B_c] + B_s2, writes=[B_tq])
                            for hb in range(2):
                                sl = slice(hb * 512, (hb + 1) * 512)
                                S.op("dve", lambda e: e.scalar_tensor_tensor(out=sim2[hb][:], in0=tq[:, sl], scalar=-1.0, in1=tq2[:, sl], op0=ALU.mult, op1=ALU.subtract), reads=[B_tq], writes=[B_s2[hb]])
                                p_, B_p = py[hb]
                                mm(p_[:, :], cl[j][:, 0, :], sre2[hb][:], tl == 0, False, [B_cl[j], B_s2[hb]], B_p, signal=False)
                                mm(p_[:, :], cl[j][:, 1, :], sim2[hb][:], False, False, [B_cl[j], B_s2[hb]], B_p, signal=True)
                        if own:
                            for hb in range(2):
                                sl = slice(hb * 512, (hb + 1) * 512)
                                p_, B_p = py[hb]
                                if hb == 0:
                                    S.op("dve", lambda e: e.tensor_scalar(out=Ddiag1[:], in0=ident[:], scalar1=col(C_DSK + uc), scalar2=None, op0=ALU.mult),
                                         reads=[B_const], writes=[B_dd])
                                mm(p_[:, :], Ddiag1[:], uT[:, uc, sl], False, True, [B_dd, B_u], B_p)
                                S.op("act", lambda e, p_=p_: e.activation(out=t1, in_=p_[:, :], func=AF.Copy), reads=[B_p], writes=[B_tq])
                                S.op("dve", lambda e: e.tensor_tensor(out=t2, in0=t1, in1=t1, op=ALU.mult), reads=[B_tq], writes=[B_tq])
                                S.op("dve", lambda e: e.tensor_scalar(out=t2, in0=t2, scalar1=0.044715, scalar2=1.0, op0=ALU.mult, op1=ALU.add), reads=[B_tq], writes=[B_tq])
                                S.op("dve", lambda e: e.tensor_tensor(out=t2, in0=t2, in1=t1, op=ALU.mult), reads=[B_tq], writes=[B_tq])
                                S.op("act", lambda e: e.activation(out=t2, in_=t2, func=AF.Sigmoid, scale=2.0 * math.sqrt(2.0 / math.pi)), reads=[B_tq], writes=[B_tq])
                                S.op("dve", lambda e, uc=uc, sl=sl: e.tensor_tensor(out=ycat[:, 8 + uc, sl], in0=t2, in1=t1, op=ALU.mult), reads=[B_tq], writes=[B_ycat])
                    S.barrier()
                    s_loop.close()
                    if own:
                        open_ring(s2, 2)
                        pz = [long_ps(0), long_ps(1)]
                        for pc in range(4):
                            a, v = wpiece_cols(wglu_d, pc * 256, 256, 8)
                            wsl, B_w = load_w(a, v)
                            for sub in range(2):
                                oc = pc * 2 + sub
                                for hb in range(2):
                                    sl = slice(hb * 512, (hb + 1) * 512)
                                    pst, B_pst = next_ps()
                                    for c in range(8):
                                        mm(pst[:, :], wsl[:, c, sub * 128:(sub + 1) * 128], ycat[:, 8 + c, sl], c == 0, c == 7, [B_w, B_ycat], B_pst)
                                    S.op("act", lambda e, pst=pst, oc=oc: e.activation(out=t1, in_=pst[:, :], func=AF.Sigmoid, bias=col(C_BGLU + oc)),
                                         reads=[B_pst, B_const], writes=[B_tq])
                                    S.op("dve", lambda e, sl=sl, oc=oc: e.tensor_tensor(out=t1, in0=t1, in1=ycat[:, 8 + oc, sl], op=ALU.mult),
                                         reads=[B_tq, B_ycat], writes=[B_tq])
                                    S.op("act", lambda e, sl=sl, oc=oc: e.activation(out=ycat[:, oc, sl], in_=t1, func=AF.Copy), reads=[B_tq], writes=[B_ycat])
                                    jq = ctr["sq"] % 2; ctr["sq"] += 1
                                    S.op("act", lambda e, jq=jq: e.activation(out=sqt[jq][:], in_=t1, func=AF.Square), reads=[B_tq], writes=[B_sqt[jq]])
                                    p_, B_p = pz[hb]
                                    mm(p_[:, :], onesb[:, :], sqt[jq][:], oc == 0, oc == 7, [B_sqt[jq], B_const], B_p, signal=True)
                        for hb in range(2):
                            sl = slice(hb * 512, (hb + 1) * 512)
                            p_, B_p = pz[hb]
                            S.op("act", lambda e, p_=p_: e.activation(out=t1, in_=p_[:, :], func=AF.Ln, scale=1.0 / 1024, bias=epsc[:, 0:1]), reads=[B_p, B_const], writes=[B_tq])
                            S.op("act", lambda e: e.activation(out=t1, in_=t1, func=AF.Exp, scale=-0.5), reads=[B_tq], writes=[B_tq])
                            for c in range(8):
                                S.op("dve", lambda e, c=c, sl=sl: e.scalar_tensor_tensor(out=ycat[:, c, sl], in0=ycat[:, c, sl], scalar=col(C_GSO + c), in1=t1, op0=ALU.mult, op1=ALU.mult),
                                     reads=[B_tq, B_ycat, B_const], writes=[B_ycat])
                    S.barrier()
                s_u.close()
                if not own:
                    rotate_state()
                    S.barrier()
                    return
                if SUB == 3:
                    return
                with contextlib.ExitStack() as s2:
                    def s2b(name, shape, dt=F32):
                        return s2.enter_context(sbt(name, list(shape), dt))
                    ropc, rops, B_rope = rope_tables(s2)
                    ssqA = s2b("ssqA", [128, T]); B_ssqA = Buf("ssqA")
                    knT2 = [s2b("knT%d" % i, [128, 2 * T], BF16) for i in range(2)]
                    kpT2 = [s2b("kpT%d" % i, [64, 2 * T], BF16) for i in range(2)]
                    Vt2 = [s2b("Vt%d" % i, [128, 16, 128], BF16) for i in range(2)]
                    qnh2 = [s2b("qnh%d" % i, [128, T], BF16) for i in range(2)]
                    qrh2 = [s2b("qrh%d" % i, [64, T], BF16) for i in range(2)]
                    B_k2 = [Buf("k0"), Buf("k1")]; B_V2 = [Buf("V0"), Buf("V1")]; B_q2 = [Buf("q0"), Buf("q1")]
                    PT = [s2b("PT%d" % i, [128, 512], BF16) for i in range(2)]
                    B_PT = [Buf("PT0"), Buf("PT1")]
                    ra = s2b("ra", [64, 512]); rb = s2b("rb", [64, 512]); B_r = Buf("r")
                    rsum = s2b("rsum", [128, 512]); otmp = s2b("otmp", [128, 512]); B_rsum = Buf("rsum")
                    S.op("dve", lambda e: e.memset(ssqA[:], 0.0), writes=[B_ssqA])
                    wset = []
                    for i in range(2):
                        wset.append(dict(
                            qn=s2b("hwqn%d" % i, [128, 4, 128], BF16), qr=s2b("hwqr%d" % i, [128, 4, 64], BF16),
                            qs=s2b("hwqs%d" % i, [128, 4, 64], BF16), kk=s2b("hwkk%d" % i, [128, 2, 128], BF16),
                            kv=s2b("hwkv%d" % i, [128, 2, 128], BF16), B=Buf("hw%d" % i)))
                    pti = [0]

                    def prep_gen(h):
                        sl_ = h % 2
                        W = wset[sl_]
                        B_W = W["B"]
                        knT, kpT, Vt, qnh, qrh = knT2[sl_], kpT2[sl_], Vt2[sl_], qnh2[sl_], qrh2[sl_]
                        B_k, B_V, B_q = B_k2[sl_], B_V2[sl_], B_q2[sl_]
                        for nm, wd_, wcols in (("qn", wqn_d, 128), ("qr", wqr_d, 64), ("qs", wqs_d, 64), ("kk", wkk_d, 128), ("kv", wkv_d, 128)):
                            S.dma("pool", W[nm][:], wd_[:, h * wcols:(h + 1) * wcols].rearrange("(c p) n -> p c n", p=128), writes=[B_W])
                        yield
                        for kb in range(4):
                            sl = slice(kb * 512, (kb + 1) * 512)
                            pk, B_pk = next_ps()
                            for c in range(2):
                                mm(pk[:, :], W["kk"][:, c, :], kvnT[:, c, sl], c == 0, c == 1, [B_W, B_kvn], B_pk)
                            pm, B_pm = next_ps()
                            j = ctr["sq"] % 2; ctr["sq"] += 1
                            S.op("act", lambda e: e.activation(out=sqt[j][:], in_=pk[:, :], func=AF.Square), reads=[B_pk], writes=[B_sqt[j]])
                            mm(pm[:, :], onesb[:, :], sqt[j][:], True, False, [B_sqt[j], B_const], B_pm, signal=True)
                            mm(pm[:, :], onesb[0:64, :], sqkpe[:, sl], False, True, [B_sqk, B_const], B_pm)
                            k_ = ctr["rs"] % 2; ctr["rs"] += 1
                            S.op("act", lambda e: e.activation(out=rstd[k_][:], in_=pm[:, :], func=AF.Ln, scale=1.0 / 192, bias=epsc[:, 0:1]), reads=[B_pm, B_const], writes=[B_rstd[k_]])
                            S.op("act", lambda e: e.activation(out=rstd[k_][:], in_=rstd[k_][:], func=AF.Exp, scale=-0.5), reads=[B_rstd[k_]], writes=[B_rstd[k_]])
                            S.op("dve", lambda e: e.scalar_tensor_tensor(out=knT[:, sl], in0=pk[:, :], scalar=col(C_GKN), in1=rstd[k_][:], op0=ALU.mult, op1=ALU.mult),
                                 reads=[B_pk, B_rstd[k_], B_const], writes=[B_k])
                            S.op("dve", lambda e: e.tensor_tensor(out=kpT[:, sl], in0=krT[:, sl], in1=rstd[k_][0:64, :], op=ALU.mult),
                                 reads=[B_kr, B_rstd[k_]], writes=[B_k])
                            yield
                        for kc in range(16):
                            if kc % 4 == 0:
                                pv, B_pv = next_ps()
                            q4 = kc % 4
                            for c in range(2):
                                mm(pv[:, q4 * 128:(q4 + 1) * 128], kvnT[:, c, kc * 128:(kc + 1) * 128], W["kv"][:, c, :], c == 0, c == 1,
                                   [B_W, B_kvn], B_pv, signal=(c == 1 and q4 == 3))
                            if q4 == 3:
                                S.op("act", lambda e: e.activation(out=Vt[:, kc - 3:kc + 1, :], in_=pv[:, :].rearrange("p (a n) -> p a n", a=4), func=AF.Copy),
                                     reads=[B_pv], writes=[B_V])
                                yield
                        for hb in range(2):
                            sl = slice(hb * 512, (hb + 1) * 512)
                            pq, B_pq = next_ps(); pr_, B_pr = next_ps(); pw, B_pw = next_ps(); pm, B_pm = next_ps()
                            for c in range(4):
                                mm(pq[:, :], W["qn"][:, c, :], qnT[:, c, sl], c == 0, c == 3, [B_W, B_qn], B_pq)
                            for c in range(4):
                                mm(pr_[0:64, :], W["qr"][:, c, :], qnT[:, c, sl], c == 0, c == 3, [B_W, B_qn], B_pr)
                            for c in range(4):
                                mm(pw[0:64, :], W["qs"][:, c, :], qnT[:, c, sl], c == 0, c == 3, [B_W, B_qn], B_pw)
                            j = ctr["sq"] % 2; ctr["sq"] += 1
                            S.op("act", lambda e: e.activation(out=sqt[j][:], in_=pq[:, :], func=AF.Square), reads=[B_pq], writes=[B_sqt[j]])
                            mm(pm[:, :], onesb[:, :], sqt[j][:], True, False, [B_sqt[j], B_const], B_pm, signal=True)
                            j2 = ctr["sq"] % 2; ctr["sq"] += 1
                            S.op("act", lambda e: e.activation(out=sqt[j2][0:64, :], in_=pr_[0:64, :], func=AF.Square), reads=[B_pr], writes=[B_sqt[j2]])
                            mm(pm[:, :], onesb[0:64, :], sqt[j2][0:64, :], False, True, [B_sqt[j2], B_const], B_pm)
                            k_ = ctr["rs"] % 2; ctr["rs"] += 1
                            S.op("act", lambda e: e.activation(out=rstd[k_][:], in_=pm[:, :], func=AF.Ln, scale=1.0 / 192, bias=epsc[:, 0:1]), reads=[B_pm, B_const], writes=[B_rstd[k_]])
                            S.op("act", lambda e: e.activation(out=rstd[k_][:], in_=rstd[k_][:], func=AF.Exp, scale=-0.5), reads=[B_rstd[k_]], writes=[B_rstd[k_]])
                            S.op("dve", lambda e: e.scalar_tensor_tensor(out=qnh[:, sl], in0=pq[:, :], scalar=col(C_GQN), in1=rstd[k_][:], op0=ALU.mult, op1=ALU.mult),
                                 reads=[B_pq, B_rstd[k_], B_const], writes=[B_q])
                            S.op("dve", lambda e: e.scalar_tensor_tensor(out=ra[:], in0=pr_[0:64, :], scalar=col(C_GQR, 64), in1=ropc[:, sl], op0=ALU.mult, op1=ALU.mult),
                                 reads=[B_pr, B_rope, B_const], writes=[B_r])
                            S.op("dve", lambda e: e.scalar_tensor_tensor(out=rb[:], in0=pw[0:64, :], scalar=col(C_GQS, 64), in1=rops[:, sl], op0=ALU.mult, op1=ALU.mult),
                                 reads=[B_pw, B_rope, B_const, B_r], writes=[B_r])
                            S.op("dve", lambda e: e.tensor_tensor(out=ra[:], in0=ra[:], in1=rb[:], op=ALU.add), reads=[B_r], writes=[B_r])
                            S.op("dve", lambda e: e.tensor_tensor(out=qrh[:, sl], in0=ra[:], in1=rstd[k_][0:64, :], op=ALU.mult),
                                 reads=[B_r, B_rstd[k_]], writes=[B_q])
                            yield

                    def score_gen(h):
                        sl_ = h % 2
                        knT, kpT, Vt, qnh, qrh = knT2[sl_], kpT2[sl_], Vt2[sl_], qnh2[sl_], qrh2[sl_]
                        B_k, B_V, B_q = B_k2[sl_], B_V2[sl_], B_q2[sl_]
                        for qb in range(2):
                            po, B_po = long_ps(0)
                            pz_, B_pz = long_ps(1)
                            nkc = 8 + 4 * qb + 4

                            def s_mm(kc):
                                jl = kc - 8 - 4 * qb
                                c0 = max(jl, 0) * 128
                                qs = slice(qb * 512 + c0, (qb + 1) * 512)
                                cs = slice(c0, 512)
                                pS, B_pS = next_ps()
                                mm(pS[:, cs], knT[:, kc * 128:(kc + 1) * 128], qnh[:, qs], True, False, [B_k, B_q], B_pS, signal=False)
                                mm(pS[:, cs], kpT[:, kc * 128:(kc + 1) * 128], qrh[:, qs], False, True, [B_k, B_q], B_pS)
                                return pS, B_pS, jl, c0, cs
                            nxt = s_mm(0)
                            for kc in range(nkc):
                                pS, B_pS, jl, c0, cs = nxt
                                if kc + 1 < nkc:
                                    nxt = s_mm(kc + 1)
                                pj = pti[0] % 2; pti[0] += 1
                                bias = maskb[:, 0:1] if kc < 8 else zeroc[:, 0:1]
                                S.op("act", lambda e: e.activation(out=PT[pj][:, cs], in_=pS[:, cs], func=AF.Exp, scale=SCALE, bias=bias),
                                     reads=[B_pS, B_const], writes=[B_PT[pj]])
                                if jl >= 0:
                                    S.op("dve", lambda e: e.memset(PT[pj][64:128, c0:c0 + 64], 0.0), reads=[B_PT[pj]], writes=[B_PT[pj]])
                                last = kc == nkc - 1
                                mm(po[:, cs], Vt[:, kc, :], PT[pj][:, cs], kc == 0, last, [B_V, B_PT[pj]], B_po, signal=last)
                                mm(pz_[:, cs], onesb[:, :], PT[pj][:, cs], kc == 0, last, [B_const, B_PT[pj]], B_pz, signal=True)
                                yield
                            sl = slice(qb * 512, (qb + 1) * 512)
                            S.op("act", lambda e: e.activation(out=rsum[:], in_=pz_[:, :], func=AF.Ln), reads=[B_pz], writes=[B_rsum])
                            S.op("act", lambda e: e.activation(out=rsum[:], in_=rsum[:], func=AF.Exp, scale=-1.0), reads=[B_rsum], writes=[B_rsum])
                            S.op("dve", lambda e: e.tensor_tensor(out=otmp[:], in0=po[:, :], in1=rsum[:], op=ALU.mult), reads=[B_po, B_rsum], writes=[B_rsum])
                            S.op("act", lambda e: e.activation(out=ycat[:, 8 + h, sl], in_=otmp[:], func=AF.Copy), reads=[B_rsum], writes=[B_ycat])
                            jq = ctr["sq"] % 2; ctr["sq"] += 1
                            S.op("act", lambda e: e.activation(out=sqt[jq][:], in_=otmp[:], func=AF.Square), reads=[B_rsum], writes=[B_sqt[jq]])
                            pa, B_pa = next_ps()
                            mm(pa[:, :], onesb[:, :], sqt[jq][:], True, True, [B_sqt[jq], B_const], B_pa)
                            S.op("dve", lambda e: e.tensor_tensor(out=ssqA[:, sl], in0=pa[:, :], in1=ssqA[:, sl], op=ALU.add), reads=[B_pa, B_ssqA], writes=[B_ssqA])
                            yield

                    for _ in prep_gen(0):
                        pass
                    for h in range(8):
                        g1 = score_gen(h)
                        g2 = prep_gen(h + 1) if h < 7 else iter(())
                        done1 = done2 = False
                        step = 0
                        while not (done1 and done2):
                            if not done1:
                                try:
                                    next(g1)
                                except StopIteration:
                                    done1 = True
                            if not done2 and (step % 2 == 1 or done1):
                                try:
                                    next(g2)
                                except StopIteration:
                                    done2 = True
                            step += 1
                    for hb in range(2):
                        sl = slice(hb * 512, (hb + 1) * 512)
                        S.op("act", lambda e: e.activation(out=rsum[:], in_=ssqA[:, sl], func=AF.Ln, scale=1.0 / 1024, bias=epsc[:, 0:1]), reads=[B_ssqA, B_const], writes=[B_rsum])
                        S.op("act", lambda e: e.activation(out=rsum[:], in_=rsum[:], func=AF.Exp, scale=-0.5), reads=[B_rsum], writes=[B_rsum])
                        for c in range(8):
                            S.op("dve", lambda e: e.scalar_tensor_tensor(out=ycat[:, 8 + c, sl], in0=ycat[:, 8 + c, sl], scalar=col(C_GAO + c), in1=rsum[:], op0=ALU.mult, op1=ALU.mult),
                                 reads=[B_rsum, B_ycat, B_const], writes=[B_ycat])
                    S.barrier()
                if SUB == 4:
                    return
                with contextlib.ExitStack() as s2:
                    open_ring(s2, 2)
                    for pc in range(8):
                        a, v = wpiece_cols(wout_d, pc * 256, 256, 16)
                        wsl, B_w = load_w(a, v)
                        for sub in range(2):
                            dc = pc * 2 + sub
                            for hb in range(2):
                                sl = slice(hb * 512, (hb + 1) * 512)
                                pst, B_pst = next_ps()
                                for c in range(16):
                                    mm(pst[:, :], wsl[:, c, sub * 128:(sub + 1) * 128], ycat[:, c, sl], c == 0, c == 15, [B_w, B_ycat], B_pst, signal=(c == 15))
                                S.op("dve", lambda e, pst=pst, dc=dc, sl=sl: e.tensor_tensor(out=xT[:, dc, sl], in0=pst[:, :], in1=xT[:, dc, sl], op=ALU.add),
                                     reads=[B_pst, B_xT], writes=[B_xT])
                    S.barrier()

        def store_out(do_norm):
            with contextlib.ExitStack() as sc:
                xo = [sc.enter_context(sbt("xo%d" % i, [128, D], F32)) for i in range(2)]
                B_xo = [Buf("xo0"), Buf("xo1")]
                gf = sc.enter_context(sbt("gf", [128, D], F32)); B_gf = Buf("gf")
                ssq = sc.enter_context(sbt("ssq", [128, 8]))
                junk = sc.enter_context(sbt("junk", [128, D], F32)); B_j = Buf("junk")
                B_ssq = Buf("ssq")
                S.dma("sp", gf[:], gfin_d, writes=[B_gf])
                S.op("dve", lambda e: e.memset(ssq[:], 0.0), writes=[B_ssq])
                for tc in range(T // 128):
                    j = tc % 2
                    for q in range(4):
                        pst, B_pst = next_ps()
                        for r in range(4):
                            dc = q * 4 + r
                            S.op("pe", lambda e, pst=pst, r=r, dc=dc, tc=tc: e.transpose(
                                pst[:, r * 128:(r + 1) * 128], xT[:, dc, tc * 128:(tc + 1) * 128], ident[:]),
                                reads=[B_xT, B_const], writes=[B_pst], signal=(r == 3))
                        S.op("act", lambda e, pst=pst, q=q, j=j: e.activation(out=xo[j][:, q * 512:(q + 1) * 512], in_=pst[:, :], func=AF.Copy),
                             reads=[B_pst], writes=[B_xo[j]])
                    if do_norm:
                        S.op("act", lambda e, j=j, tc=tc: e.activation(out=junk[:], in_=xo[j][:], func=AF.Square, accum_out=ssq[:, tc:tc + 1]),
                             reads=[B_xo[j], B_ssq], writes=[B_j, B_ssq])
                        S.op("act", lambda e, tc=tc: e.activation(out=ssq[:, tc:tc + 1], in_=ssq[:, tc:tc + 1], func=AF.Sqrt, scale=1.0 / D, bias=epsc[:, 0:1]),
                             reads=[B_ssq, B_const], writes=[B_ssq])
                        S.op("dve", lambda e, tc=tc: e.reciprocal(out=ssq[:, tc:tc + 1], in_=ssq[:, tc:tc + 1]), reads=[B_ssq], writes=[B_ssq])
                        S.op("dve", lambda e, j=j, tc=tc: e.scalar_tensor_tensor(out=xo[j][:], in0=xo[j][:], scalar=ssq[:, tc:tc + 1], in1=gf[:], op0=ALU.mult, op1=ALU.mult),
                             reads=[B_xo[j], B_ssq, B_gf], writes=[B_xo[j]])
                    S.dma("sp", out_d[tc * 128:(tc + 1) * 128, :], xo[j][:], reads=[B_xo[j]])
                S.barrier()

        if st is None or st in (2, 3, 5):
            ssm_prep()
            load_xT(x_pre)
            if USE_FFN1:
                ffn(C_G1, w1g, w1u, w1d)
            if SUB != 20:
                mixer(False, pos_pre)
        load_xT(x_own)
        if USE_FFN1:
            ffn(C_G1, w1g, w1u, w1d)
        if (st is None or st in (2, 3, 5)) and SUB != 20:
            mixer(True, pos_own)
        if USE_FFN2:
            ffn(C_G2, w2g, w2u, w2d)
        store_out(st is None)
        S.barrier()
        with nc.Block() as block:
            S.emit(block)
    nc._in_names = in_names
    return nc


def _cols16(g):
    return np.ascontiguousarray(np.asarray(g, np.float32).reshape(-1, 128).T)


def _prep_shared(inp):
    f = lambda k: np.asarray(inp[k], np.float32)[0]
    sh = {}
    cols = np.zeros((128, 128), np.float32)
    cols[:, 0:16] = _cols16(f("ffn1_norm")); cols[:, 16:32] = _cols16(f("mix_norm"))
    cols[:, 32:48] = _cols16(f("ffn2_norm")); cols[:, 48:64] = _cols16(f("final_norm"))
    cols[:, 64:68] = _cols16(f("mla_q_a_norm")); cols[:, 68:70] = _cols16(f("mla_kv_a_norm"))
    cols[:, 70:78] = _cols16(f("ssm_out_norm")); cols[:, 78:86] = _cols16(f("attn_out_norm"))
    cols[:, 86:94] = _cols16(f("ssm_b_glu")); cols[:, 94:102] = _cols16(f("ssm_d").reshape(-1))
    perm = np.concatenate([np.arange(32, 64), np.arange(0, 32)])
    qn, kn = f("mla_q_norm"), f("mla_k_norm")
    cols[:, 102] = qn[:128]; cols[:64, 103] = qn[128:]; cols[:64, 104] = qn[128:][perm]
    cols[:, 105] = kn[:128]; cols[:64, 106] = kn[128:]; cols[:64, 107] = kn[128:][perm]
    invf = (10000.0 ** (-np.arange(0, 64, 2, dtype=np.float32) / 64)).astype(np.float32)
    cols[:64, 108] = np.concatenate([invf, invf])
    cols[:32, 109] = -1.0; cols[32:64, 109] = 1.0
    sh["cols"] = cols
    sh["ident"] = np.eye(128, dtype=np.float32)
    sh["tidx"] = np.ascontiguousarray(np.broadcast_to(np.arange(T, dtype=np.float32), (128, T)))
    ls, ar, ai = f("ssm_log_step"), f("ssm_a_re"), f("ssm_a_im")
    ssmp = np.zeros((128, 3, 32), np.float32)
    bblk = np.zeros((128, 32, 2, 128), np.float32)
    cblk = np.zeros((128, 32, 2, 128), np.float32)
    bre, bim, cre, cim = f("ssm_b_re"), f("ssm_b_im"), f("ssm_c_re"), f("ssm_c_im")
    for ti in range(32):
        for gl in range(2):
            g = 2 * ti + gl
            rows = slice(gl * 64, gl * 64 + 64)
            ssmp[rows, 0, ti] = ls[g]; ssmp[rows, 1, ti] = ar[g]; ssmp[rows, 2, ti] = ai[g]
            c0 = (ti % 4) * 32 + gl * 16
            bblk[rows, ti, 0, c0:c0 + 16] = bre[g]; bblk[rows, ti, 1, c0:c0 + 16] = bim[g]
            cblk[rows, ti, 0, c0:c0 + 16] = cre[g].T; cblk[rows, ti, 1, c0:c0 + 16] = cim[g].T
    sh["ssmp"], sh["bblk"], sh["cblk"] = ssmp, bblk, cblk
    sh["gfin"] = np.ascontiguousarray(np.broadcast_to(f("final_norm"), (128, D)))
    sh["w1g"], sh["w1u"], sh["w1d"] = f("ffn1_w_gate"), f("ffn1_w_up"), f("ffn1_w_down")
    sh["w2g"], sh["w2u"], sh["w2d"] = f("ffn2_w_gate"), f("ffn2_w_up"), f("ffn2_w_down")
    win = f("w_in")
    sh["win"] = np.ascontiguousarray(np.concatenate([win, win[:, 1792:][:, perm]], axis=1))
    sh["wglu"] = f("ssm_w_glu")
    wq = f("mla_w_q_up").reshape(512, 8, 192)
    sh["wqn"] = np.ascontiguousarray(wq[:, :, :128].reshape(512, 1024))
    sh["wqr"] = np.ascontiguousarray(wq[:, :, 128:].reshape(512, 512))
    sh["wqs"] = np.ascontiguousarray(wq[:, :, 128:][:, :, perm].reshape(512, 512))
    wkv = f("mla_w_kv_up").reshape(256, 8, 256)
    sh["wkk"] = np.ascontiguousarray(wkv[:, :, :128].reshape(256, 1024))
    sh["wkv"] = np.ascontiguousarray(wkv[:, :, 128:].reshape(256, 1024))
    sh["wout"] = f("w_out")
    return sh


def kernel(**inputs):
    x = np.asarray(inputs["x"], np.float32)
    pos = np.asarray(inputs["positions"], np.int32)
    sh = _prep_shared(inputs)
    in_maps = []
    for c in range(8):
        b, p = c // 2, c % 2
        m = dict(sh)
        m["x_own"] = np.ascontiguousarray(x[b, p * T:(p + 1) * T])
        m["pos_own"] = np.ascontiguousarray(np.broadcast_to(pos[b, p * T:(p + 1) * T], (64, T)))
        if p == 1:
            m["x_pre"] = np.ascontiguousarray(x[b, 0:T])
            m["pos_pre"] = np.ascontiguousarray(np.broadcast_to(pos[b, 0:T], (64, T)))
            m["maskbias"] = np.zeros((128, 1), np.float32)
        else:
            m["x_pre"] = np.zeros((T, D), np.float32)
            m["pos_pre"] = np.zeros((64, T), np.int32)
            m["maskbias"] = np.full((128, 1), -30000.0, np.float32)
        in_maps.append(m)
    nc = build_program()
    in_maps = [{k: m[k] for k in nc._in_names} for m in in_maps]
    res = run_bass_kernel_spmd(nc, in_maps, core_ids=list(range(8)))
    out = np.zeros((4, 2 * T, D), np.float32)
    for c in range(8):
        out[c // 2, (c % 2) * T:(c % 2 + 1) * T] = res.results[c]["out"]
    return out
```
